# Optimizing a Trainium2 kernel written in Bass

```python
import math
import jax, jax.numpy as jnp
from jax import lax
import numpy as np

D_MODEL = 1024
BATCH = 4
SEQ = 4096
DEPTH = 1

MIX_WIDTH = D_MODEL
ATTN_WIDTH = D_MODEL // 2
CONV_WIDTH = MIX_WIDTH - ATTN_WIDTH
HEAD_DIM = 64
N_HEADS = ATTN_WIDTH // HEAD_DIM
ROT_DIM = HEAD_DIM // 4
ROPE_THETA = 500000.0
MOBA_BLOCK = 256
MOBA_TOP_K = 3
Q_CHUNK = 64
CONV_WIDTH_K = 31
D_FF = 4 * D_MODEL
EPS = 1e-6
IN_COLS = 3 * ATTN_WIDTH + 2 * CONV_WIDTH

kernel_name = "hybrid_moba_conformer_parallel_heads"


def rms_norm(x, g):
    xf = x.astype(jnp.float32)
    y = xf * lax.rsqrt(jnp.mean(xf * xf, axis=-1, keepdims=True) + EPS)
    return (y * g.astype(jnp.float32)).astype(x.dtype)


def layer_norm(x, g, b):
    xf = x.astype(jnp.float32)
    mu = jnp.mean(xf, axis=-1, keepdims=True)
    var = jnp.mean(jnp.square(xf - mu), axis=-1, keepdims=True)
    y = (xf - mu) * lax.rsqrt(var + EPS)
    return (y * g.astype(jnp.float32) + b.astype(jnp.float32)).astype(x.dtype)


def rope_tables(seq, dtype):
    half = ROT_DIM // 2
    inv_freq = ROPE_THETA ** (-jnp.arange(half, dtype=jnp.float32) * 2.0 / ROT_DIM)
    ang = jnp.arange(seq, dtype=jnp.float32)[:, None] * inv_freq[None, :]
    return jnp.cos(ang).astype(dtype), jnp.sin(ang).astype(dtype)


def partial_rope(x, cos, sin):
    half = ROT_DIM // 2
    x1 = x[..., :half]
    x2 = x[..., half:ROT_DIM]
    rot = jnp.concatenate([x1 * cos - x2 * sin, x2 * cos + x1 * sin], axis=-1)
    return jnp.concatenate([rot, x[..., ROT_DIM:]], axis=-1)


def moba_attention(q, k, v):
    B, H, S, Dh = q.shape
    s_pad = -(-S // MOBA_BLOCK) * MOBA_BLOCK
    padw = ((0, 0), (0, 0), (0, s_pad - S), (0, 0))
    q = jnp.pad(q, padw)
    k = jnp.pad(k, padw)
    v = jnp.pad(v, padw)
    nb = s_pad // MOBA_BLOCK
    k_blk = k.reshape(B, H, nb, MOBA_BLOCK, Dh)
    v_blk = v.reshape(B, H, nb, MOBA_BLOCK, Dh)
    k_mean = jnp.mean(k_blk.astype(jnp.float32), axis=3)
    gate = jnp.einsum('bhsd,bhnd->bhsn', q.astype(jnp.float32), k_mean)
    q_blk = jnp.arange(s_pad) // MOBA_BLOCK
    past = jnp.arange(nb)[None, :] < q_blk[:, None]
    gate = jnp.where(past[None, None], gate, -jnp.inf)
    n_sel = min(MOBA_TOP_K, max(nb - 1, 1))
    _, sel_idx = lax.top_k(gate, n_sel)
    sel_ok = sel_idx < q_blk[None, None, :, None]

    n_chunks = s_pad // Q_CHUNK
    q_c = jnp.moveaxis(q.reshape(B, H, n_chunks, Q_CHUNK, Dh), 2, 0)
    idx_c = jnp.moveaxis(sel_idx.reshape(B, H, n_chunks, Q_CHUNK, n_sel), 2, 0)
    ok_c = jnp.moveaxis(sel_ok.reshape(B, H, n_chunks, Q_CHUNK, n_sel), 2, 0)
    gather = jax.vmap(jax.vmap(lambda blocks, ix: blocks[ix]))
    scale = HEAD_DIM ** -0.5

    def step(args):
        c, qb, ix, ok = args
        start = c * Q_CHUNK
        own = start // MOBA_BLOCK
        k_own = lax.dynamic_index_in_dim(k_blk, own, axis=2, keepdims=False)
        v_own = lax.dynamic_index_in_dim(v_blk, own, axis=2, keepdims=False)
        qpos = start + jnp.arange(Q_CHUNK)
        kpos = own * MOBA_BLOCK + jnp.arange(MOBA_BLOCK)
        own_logits = jnp.einsum('bhqd,bhkd->bhqk', qb, k_own).astype(jnp.float32) * scale
        own_logits = jnp.where((kpos[None, :] <= qpos[:, None])[None, None], own_logits, -jnp.inf)
        k_sel = gather(k_blk, ix)
        v_sel = gather(v_blk, ix)
        sel_logits = jnp.einsum('bhqd,bhqjtd->bhqjt', qb, k_sel).astype(jnp.float32) * scale
        sel_logits = jnp.where(ok[..., None], sel_logits, -jnp.inf)
        sel_logits = sel_logits.reshape(B, H, Q_CHUNK, n_sel * MOBA_BLOCK)
        probs = jax.nn.softmax(jnp.concatenate([sel_logits, own_logits], axis=-1), axis=-1)
        probs = probs.astype(v.dtype)
        p_sel = probs[..., :n_sel * MOBA_BLOCK].reshape(B, H, Q_CHUNK, n_sel, MOBA_BLOCK)
        p_own = probs[..., n_sel * MOBA_BLOCK:]
        return (jnp.einsum('bhqjt,bhqjtd->bhqd', p_sel, v_sel)
                + jnp.einsum('bhqk,bhkd->bhqd', p_own, v_own))

    out = lax.map(step, (jnp.arange(n_chunks), q_c, idx_c, ok_c))
    out = jnp.moveaxis(out, 0, 2).reshape(B, H, s_pad, Dh)
    return out[:, :, :S]


def conformer_conv(u, b_glu, w_dw, b_dw, g_ln, b_ln):
    u = u + b_glu
    a, gte = jnp.split(u, 2, axis=-1)
    h = a * jax.nn.sigmoid(gte)
    h = lax.conv_general_dilated(h, w_dw, window_strides=(1,),
                                 padding=[(CONV_WIDTH_K - 1, 0)],
                                 dimension_numbers=('NWC', 'WIO', 'NWC'),
                                 feature_group_count=CONV_WIDTH) + b_dw
    h = layer_norm(h, g_ln, b_ln)
    return jax.nn.swish(h)


def setup_inputs(seed: int = 0) -> dict:
    key = jax.random.key(seed)
    ks = jax.random.split(key, 16)
    f32 = jnp.float32
    L = DEPTH
    nrm = lambda k, shape, fan: jax.random.normal(k, shape, f32) * fan ** -0.5
    return {
        "x": jax.random.normal(ks[0], (BATCH, SEQ, D_MODEL), f32),
        "g_mix_norm": 1.0 + 0.02 * jax.random.normal(ks[1], (L, D_MODEL), f32),
        "w_in": nrm(ks[2], (L, D_MODEL, IN_COLS), D_MODEL),
        "b_glu": 0.02 * jax.random.normal(ks[3], (L, 2 * CONV_WIDTH), f32),
        "w_dw": nrm(ks[4], (L, CONV_WIDTH_K, 1, CONV_WIDTH), CONV_WIDTH_K),
        "b_dw": 0.02 * jax.random.normal(ks[5], (L, CONV_WIDTH), f32),
        "g_conv_ln": 1.0 + 0.02 * jax.random.normal(ks[6], (L, CONV_WIDTH), f32),
        "b_conv_ln": 0.02 * jax.random.normal(ks[7], (L, CONV_WIDTH), f32),
        "w_out": nrm(ks[8], (L, MIX_WIDTH, D_MODEL), MIX_WIDTH),
        "g_mlp_norm": 1.0 + 0.02 * jax.random.normal(ks[9], (L, D_MODEL), f32),
        "w_mlp_in": nrm(ks[10], (L, D_MODEL, D_FF), D_MODEL),
        "w_mlp_out": nrm(ks[11], (L, D_FF, D_MODEL), D_FF),
        "g_final": 1.0 + 0.02 * jax.random.normal(ks[12], (D_MODEL,), f32),
    }


def reference(x, g_mix_norm, w_in, b_glu, w_dw, b_dw, g_conv_ln, b_conv_ln,
              w_out, g_mlp_norm, w_mlp_in, w_mlp_out, g_final):
    B, S, _ = x.shape
    cos, sin = rope_tables(S, x.dtype)
    h = x
    for l in range(DEPTH):
        xn = rms_norm(h, g_mix_norm[l])
        proj = xn @ w_in[l]
        q, k, v, u = jnp.split(proj, [ATTN_WIDTH, 2 * ATTN_WIDTH, 3 * ATTN_WIDTH], axis=-1)
        to_heads = lambda t: t.reshape(B, S, N_HEADS, HEAD_DIM).transpose(0, 2, 1, 3)
        q = partial_rope(to_heads(q), cos, sin)
        k = partial_rope(to_heads(k), cos, sin)
        attn = moba_attention(q, k, to_heads(v))
        attn = attn.transpose(0, 2, 1, 3).reshape(B, S, ATTN_WIDTH)
        conv = conformer_conv(u, b_glu[l], w_dw[l], b_dw[l], g_conv_ln[l], b_conv_ln[l])
        mixed = jnp.concatenate([attn, conv], axis=-1)
        h = h + mixed @ w_out[l]
        hn = rms_norm(h, g_mlp_norm[l])
        ff = jnp.square(jax.nn.relu(hn @ w_mlp_in[l]))
        h = h + ff @ w_mlp_out[l]
    return rms_norm(h, g_final)
```

```python
from contextlib import ExitStack

import numpy as np
import ml_dtypes

import concourse.bass as bass
import concourse.mybir as mybir
from concourse.bass_utils import run_bass_kernel_spmd

F32 = mybir.dt.float32
BF16 = mybir.dt.bfloat16
AF = mybir.ActivationFunctionType
ALU = mybir.AluOpType
AX = mybir.AxisListType

D = 1024
SEQ = 4096
NBLK = 16
BL = 256
NH = 8
EPS = 1e-6
OWN_BLOCKS = {0: [0, 1, 6, 7, 8, 9, 14, 15], 1: [2, 3, 4, 5, 10, 11, 12, 13]}
N_OTHER_BLOCKS = [2, 4, 6, 8]
NEG = -30000.0
VT = 520
NVEC = 164

ENGS = ("pe", "act", "dve", "pool", "sp")
N_DMA_SEMS = 40


class Region:
    __slots__ = ("writer", "readers")

    def __init__(self):
        self.writer = None
        self.readers = []


class Op:
    __slots__ = ("eng", "fn", "deps", "sig", "needs_sig", "is_dma", "sem", "val", "idx")

    def __init__(self, eng, fn, is_dma):
        self.eng = eng
        self.fn = fn
        self.deps = set()
        self.sig = None
        self.needs_sig = False
        self.is_dma = is_dma
        self.sem = None
        self.val = None


class Sched:
    def __init__(self):
        self.ops = {e: [] for e in ENGS}
        self.all = []
        self.dma_ops = []
        self.dma_since_barrier = []

    def add(self, eng, fn, r=(), w=(), deps=(), dma=False):
        op = Op(eng, fn, dma)
        op.idx = len(self.ops[eng])
        for d in deps:
            if d is not None:
                op.deps.add(d)
        for reg in r:
            if reg.writer is not None:
                op.deps.add(reg.writer)
            reg.readers.append(op)
        for reg in w:
            best = {}
            for rd in reg.readers:
                if rd.is_dma:
                    op.deps.add(rd)
                elif rd.eng not in best or rd.idx > best[rd.eng].idx:
                    best[rd.eng] = rd
            op.deps.update(best.values())
            reg.readers = []
            if reg.writer is not None:
                op.deps.add(reg.writer)
            reg.writer = op
        op.deps.discard(op)
        if dma:
            i = len(self.dma_ops)
            if i >= N_DMA_SEMS:
                op.deps.add(self.dma_ops[i - N_DMA_SEMS])
            op.sem = i % N_DMA_SEMS
            op.val = 16 * (i // N_DMA_SEMS + 1)
            self.dma_ops.append(op)
            self.dma_since_barrier.append(op)
        op.idx = len(self.ops[eng])
        self.ops[eng].append(op)
        self.all.append(op)
        return op

    def barrier(self):
        last = []
        for e in ENGS:
            real = [o for o in self.ops[e] if o.fn is not None]
            if real:
                last.append(real[-1])
        deps = last + self.dma_since_barrier[-4:]
        self.dma_since_barrier = []
        for e in ENGS:
            self.add(e, None, deps=deps)

    def finalize(self):
        for op in self.all:
            for d in op.deps:
                if d.is_dma:
                    continue
                if d.eng == "pe" and op.eng == "pe" and not op.is_dma:
                    continue
                d.needs_sig = True
        for e in ENGS:
            c = 0
            for op in self.ops[e]:
                if op.needs_sig and not op.is_dma:
                    c += 1
                    op.sig = c

    def emit(self, nc):
        self.finalize()
        with ExitStack() as st:
            esem = {e: st.enter_context(nc.semaphore("s_" + e)) for e in ENGS}
            dsem = [st.enter_context(nc.semaphore("d%d" % i)) for i in range(N_DMA_SEMS)]
            block = st.enter_context(nc.Block())

            def run(ename, eng):
                waited = {}
                for op in self.ops[ename]:
                    need = {}
                    for d in op.deps:
                        if d.is_dma:
                            key, sem, val = ("d", d.sem), dsem[d.sem], d.val
                        else:
                            if d.eng == "pe" and ename == "pe" and not op.is_dma:
                                continue
                            key, sem, val = ("e", d.eng), esem[d.eng], d.sig
                        if key not in need or need[key][1] < val:
                            need[key] = (sem, val)
                    for key in sorted(need, key=lambda k: (k[0], str(k[1]))):
                        sem, val = need[key]
                        if waited.get(key, 0) >= val:
                            continue
                        eng.wait_ge(sem, val)
                        waited[key] = val
                    if op.fn is None:
                        continue
                    name, a, kw = op.fn
                    ins = getattr(eng, name)(*a, **kw)
                    if op.is_dma:
                        ins.then_inc(dsem[op.sem], 16)
                    elif op.needs_sig:
                        ins.then_inc(esem[ename], 1)

            @block.tensor
            def _(eng):
                run("pe", eng)

            @block.scalar
            def _(eng):
                run("act", eng)

            @block.vector
            def _(eng):
                run("dve", eng)

            @block.gpsimd
            def _(eng):
                run("pool", eng)

            @block.sync
            def _(eng):
                run("sp", eng)


def build_program(debug=False):
    nc = bass.Bass("TRN2", target_bir_lowering=False)
    S = Sched()

    def OP(eng, name, *a, r=(), w=(), deps=(), **kw):
        return S.add(eng, (name, a, kw), r=r, w=w, deps=deps)

    def dma(out, in_, r=(), w=(), eng="sp", deps=()):
        return S.add(eng, ("dma_start", (), dict(out=out, in_=in_)), r=r, w=w, dma=True, deps=deps)

    def din(name, shape, dt=F32):
        return nc.dram_tensor(name, shape, dt, kind="ExternalInput").ap()

    x_loc = din("x_loc", [SEQ, D])
    x_halo = din("x_halo", [4, 32, D])
    w_in = din("w_in", [D, 2560])
    w_out = din("w_out", [D, D])
    w_m1 = din("w_m1", [D, 4096])
    w_m2 = din("w_m2", [4096, D])
    cosT_d = din("cosT", [128, SEQ])
    sinT_d = din("sinT", [128, SEQ])
    vecs_d = din("vecs", [128, NVEC])
    gfin_d = din("gfin", [128, D])
    gb_d = din("gbias", [128, 256])
    own_d = din("ownind", [128, 256])
    valid_d = din("validf", [128, 256])
    ident_d = din("ident", [128, 128], BF16)
    perm_d = din("perm", [128, 128], BF16)
    causal_d = din("causal", [128, 4 * 512], BF16)
    sel64_d = din("sel64", [128, 128])
    onehot_d = din("onehot", [16, SEQ], BF16)
    y_out = nc.dram_tensor("y", [2048, D], F32, kind="ExternalOutput").ap()
    dbg = {}
    if debug:
        dbg["kt"] = nc.dram_tensor("dbg_kt", [128, 4 * SEQ], BF16, kind="ExternalOutput").ap()
        dbg["q"] = nc.dram_tensor("dbg_q", [128, 4 * 2048], BF16, kind="ExternalOutput").ap()
        dbg["v"] = nc.dram_tensor("dbg_v", [128, 32 * VT + 64], BF16, kind="ExternalOutput").ap()
        dbg["mx"] = nc.dram_tensor("dbg_mx", [128, 8 * 2048], BF16, kind="ExternalOutput").ap()
        dbg["km"] = nc.dram_tensor("dbg_km", [128, 64], F32, kind="ExternalOutput").ap()
        dbg["h1"] = nc.dram_tensor("dbg_h1", [128, 16 * 1024], F32, kind="ExternalOutput").ap()

    with ExitStack() as st:
        def sb(name, shape, dt=F32):
            return st.enter_context(nc.sbuf_tensor("sb_" + name, shape, dt))

        def ps(name, shape, dt=F32):
            return st.enter_context(nc.psum_tensor("ps_" + name, shape, dt))

        R1 = sb("R1", [128, 16384], BF16)
        MX = R1[:].rearrange("p (k t) -> p k t", k=8)
        WQKV = R1[:, 0:12288].rearrange("p (k c) -> p k c", k=8)
        WB = R1[:, 0:8192].rearrange("p (k c) -> p k c", k=8)
        R2 = sb("R2", [128, 16384 + 32 * VT + 64], BF16)
        KT = R2[:, 0:16384].rearrange("p (a t) -> p a t", a=4)
        VF = R2[:, 16384:16384 + 32 * VT + 64]
        V4 = R2[:, 16384:16384 + 32 * VT].rearrange("p (t h e) -> p t h e", h=8, e=65)
        H1 = R2[:, 0:32768].bitcast(F32).rearrange("p (i d) -> p i d", i=16)
        Q = sb("Q", [128, 4, 2048], BF16)
        QPc = R1[:, 8192:16384].rearrange("p (a h t) -> p a h t", a=2, h=8)
        QP_R = [Region(), Region()]
        CVT = sb("CVT", [128, 4, 2048], BF16)
        ident = sb("ident", [128, 128], BF16)
        perm = sb("perm", [128, 128], BF16)
        sel64 = sb("sel64", [128, 128])
        onesf = sb("onesf", [128, 128])
        vecs = sb("vecs", [128, NVEC])
        kms = sb("kms", [128, 4, 16])
        kmhi = sb("kmhi", [128, 4, 16], BF16)
        kmlo = sb("kmlo", [128, 4, 16], BF16)
        kmf = sb("kmf", [128, 4, 16])
        st4 = sb("st4", [128, 3, 4])
        mhalf1 = sb("mhalf1", [128, 1])
        ARENA_BYTES = 77824
        ARENA = sb("ARENA", [128, ARENA_BYTES // 2], BF16)

        g_mix = vecs[:, 0:8]
        g_mlp = vecs[:, 8:16]
        b_a = vecs[:, 16:20]
        b_g = vecs[:, 20:24]
        b_dw = vecs[:, 24:28]
        g_ln = vecs[:, 28:32]
        b_ln = vecs[:, 32:36]
        halo_mask = vecs[:, 36:40]
        w_dw = vecs[:, 40:164].rearrange("p (c k) -> p c k", c=4)

        class Carver:
            def __init__(self):
                self.off = 0

            def get(self, shape, dt):
                n = 1
                for s_ in shape[1:]:
                    n *= s_
                nbytes = n * (2 if dt == BF16 else 4)
                off = self.off
                self.off += (nbytes + 63) // 64 * 64
                assert self.off <= ARENA_BYTES, (self.off, ARENA_BYTES)
                if dt == BF16:
                    ap = ARENA[:, off // 2: off // 2 + n]
                else:
                    ap = ARENA[:, off // 2: off // 2 + 2 * n].bitcast(F32)
                if len(shape) == 2:
                    return ap
                names = " ".join("d%d" % i for i in range(len(shape) - 1))
                kw = {"d%d" % i: shape[i + 1] for i in range(len(shape) - 2)}
                return ap.rearrange("p (%s) -> p %s" % (names, names), **kw)

        PA = [ps("PA%d" % i, [128, 2, 512]) for i in range(2)]
        PR = [ps("PR%d" % i, [128, 512]) for i in range(2)]
        PH = ps("PH", [128, 512])
        PTF = ps("PT", [128, 512])
        PT = PTF[:].bitcast(BF16)
        PA_R = [[Region(), Region()] for _ in range(2)]
        PR_R = [Region(), Region()]
        PH_R = Region()
        PT_R = Region()
        pa_banks = [(PA[i][:, j, :], PA_R[i][j]) for i in range(2) for j in range(2)]
        PT3 = PT.rearrange("p (k t) -> p k t", k=8)

        c_ident, c_perm, c_sel64, c_onesf, c_vecs, c_mhalf1 = [Region() for _ in range(6)]
        dma(ident[:], ident_d, w=[c_ident])
        dma(vecs[:], vecs_d, w=[c_vecs])
        dma(perm[:], perm_d, w=[c_perm])
        dma(sel64[:], sel64_d, w=[c_sel64])
        OP("pool", "memset", onesf[:], 1.0, w=[c_onesf])
        OP("pool", "memset", QPc[:, 1].rearrange("p h t -> p (h t)"), 0.0, w=[QP_R[1]])
        OP("pool", "memset", mhalf1[:], -0.5, w=[c_mhalf1])
        kms_R = Region()
        OP("pool", "memset", kms[:], 0.0, w=[kms_R])

        cv = Carver()
        xnT = cv.get([128, 2, 8, 544], BF16)
        xst = cv.get([128, 3, 1024], F32)
        xh = cv.get([128, 1024], F32)
        xn = cv.get([128, 2, 1024], BF16)
        off_cs2 = cv.off
        cs2 = cv.get([128, 2, 2, 512], F32)
        t12 = cv.get([128, 1024], F32)
        t1 = t12[:, 0:512]
        t2 = t12[:, 512:1024]
        off_qraw = cv.off
        qraw = cv.get([128, 2, 512], BF16)
        hT = cv.get([128, 4, 544], BF16)
        dg = cv.get([128, 2, 16, 128], BF16)
        junk = dg[:, 0, 0:8, :].rearrange("p a t -> p (a t)")
        w_dw_bf = cv.get([128, 4, 32], BF16)
        yb = cv.get([128, 4, 512], F32)
        lnmv = cv.get([128, 1024], F32)
        lnm = lnmv[:, 0:512]
        lnv = lnmv[:, 512:1024]
        sigh = cv.get([128, 32], F32)
        print("pass A/B arena bytes", cv.off)

        xnT_R = [[Region() for _ in range(5)] for _ in range(2)]
        xst_R = [Region() for _ in range(3)]
        xh_R = Region()
        junk_R = Region()
        xn_R = [Region(), Region()]
        st_R = [Region() for _ in range(4)]
        cs_R2 = [Region(), Region()]
        t1_R = Region()
        t2_R = Region()
        qraw_R = [Region(), Region()]
        hT_R = [Region() for _ in range(4)]
        dg_R = [Region(), Region()]
        wdb_R = Region()
        y_R = [Region() for _ in range(4)]
        lnm_R = Region()
        lnv_R = Region()
        sigh_R = Region()
        W_R = [Region() for _ in range(8)]
        WQ_R = [Region() for _ in range(8)]
        WK_R = [Region() for _ in range(8)]
        WV_R = [Region() for _ in range(8)]
        KT_R = [[Region() for _ in range(8)] for _ in range(4)]
        Q_R = [[Region() for _ in range(4)] for _ in range(4)]
        V_R = [Region() for _ in range(32)]
        MX_R = [Region() for _ in range(16)]
        CV_R = [Region() for _ in range(16)]
        vones_R = Region()
        OP("pool", "memset", VF[:, 0:32 * VT].rearrange("p (n e) -> p n e", e=65)[:, :, 64:65], 1.0, w=[vones_R])
        OP("pool", "memset", VF[:, 32 * VT:32 * VT + 64], 0.0, w=[vones_R])

        state = {"tile": 0, "pa": 0, "pr": 0, "qraw": 0, "evac": 0, "ld": 0, "cast": 0, "junk": (junk, [dg_R[0]])}

        def next_pa():
            i = state["pa"] % 4
            state["pa"] += 1
            return pa_banks[i]

        def next_pr():
            i = state["pr"] % 2
            state["pr"] += 1
            return PR[i][:], PR_R[i]

        w_in_v = w_in.rearrange("(k p) c -> p k c", p=128)

        def load_w_in(dst, c0, ncols, wregs_of, stg, n_now=8):
            ns = len(stg)

            def issue(kt):
                st_ap, st_regs = stg[kt % ns]
                dma(st_ap, w_in_v[:, kt, c0:c0 + ncols], w=st_regs)
            for kt in range(min(ns, 8, n_now)):
                issue(kt)

            def finish(deps=()):
                for kt in range(8):
                    st_ap, st_regs = stg[kt % ns]
                    eng = ("dve", "act")[kt % 2]
                    if eng == "act":
                        OP("act", "activation", out=dst[:, kt, :], in_=st_ap, func=AF.Copy, scale=g_mix[:, kt:kt + 1],
                           r=st_regs + [c_vecs], w=wregs_of(kt), deps=deps)
                    else:
                        OP(eng, "tensor_scalar", out=dst[:, kt, :], in0=st_ap, scalar1=g_mix[:, kt:kt + 1], scalar2=None,
                           op0=ALU.mult, r=st_regs + [c_vecs], w=wregs_of(kt), deps=deps)
                    if kt + ns < 8:
                        issue(kt + ns)
            finish.more = lambda: [issue(kt) for kt in range(min(ns, 8, n_now), min(ns, 8))]
            return finish

        def norm_tile(src, src_R, npart, dst_cols, dst_R):
            j = state["tile"]
            state["tile"] += 1
            sl = j % 4
            xb = j % 2
            ssq = st4[0:npart, 0, sl:sl + 1]
            vv = st4[0:npart, 1, sl:sl + 1]
            rs = st4[0:npart, 2, sl:sl + 1]
            jk, jk_regs = state["junk"]
            OP("act", "activation", out=jk[0:npart, :], in_=src, func=AF.Square, accum_out=ssq, r=[src_R], w=[st_R[sl]] + jk_regs)
            OP("pool", "tensor_scalar", out=vv, in0=ssq, scalar1=1.0 / D, scalar2=EPS, op0=ALU.mult, op1=ALU.add,
               r=[st_R[sl]], w=[st_R[sl]])
            OP("pool", "tensor_tensor", out=rs, in0=vv, in1=mhalf1[0:npart, :], op=ALU.pow, r=[st_R[sl], c_mhalf1], w=[st_R[sl]])

            def part_a2():
                if state.get("scale_on_pool"):
                    OP("pool", "tensor_scalar", out=xn[0:npart, xb, :], in0=src, scalar1=rs, scalar2=1.0, op0=ALU.mult, op1=ALU.mult,
                       r=[src_R, st_R[sl]], w=[xn_R[xb]])
                else:
                    OP("dve", "tensor_scalar", out=xn[0:npart, xb, :], in0=src, scalar1=rs, scalar2=None, op0=ALU.mult,
                       r=[src_R, st_R[sl]], w=[xn_R[xb]])
                return part_b

            def part_b():
                for k in range(8):
                    OP("pe", "transpose", out=PT[:, k * 128:k * 128 + npart], in_=xn[0:npart, xb, k * 128:(k + 1) * 128],
                       identity=ident[0:npart, 0:npart], r=[xn_R[xb], c_ident], w=[PT_R])
                src_ps = PT3[:, :, 0:npart]
                if state["evac"] % 2 == 0:
                    OP("act", "activation", out=dst_cols, in_=src_ps, func=AF.Copy, r=[PT_R], w=[dst_R])
                else:
                    OP("dve", "tensor_copy", out=dst_cols, in_=src_ps, r=[PT_R], w=[dst_R])
                state["evac"] += 1
            return part_a2

        def rope_part1(ps_ap, ps_R):
            qb = state["qraw"] % 2
            state["qraw"] += 1
            OP("act", "activation", out=qraw[:, qb, :], in_=ps_ap, func=AF.Copy, r=[ps_R], w=[qraw_R[qb]])
            return qb

        def rope_part2(ps_ap, ps_R, qb, is_k, p, c, col0, csb):
            cs = cs2[:, csb]
            cs_R = cs_R2[csb]
            pr_ap, pr_R = next_pr()
            OP("pe", "matmul", pr_ap, lhsT=perm[:], rhs=qraw[:, qb, :], start=True, stop=True, r=[qraw_R[qb], c_perm], w=[pr_R])
            OP("dve", "tensor_tensor", out=t1, in0=pr_ap, in1=cs[:, 1, :], op=ALU.mult, r=[pr_R, cs_R], w=[t1_R])
            OP("dve", "tensor_tensor", out=t2, in0=ps_ap, in1=cs[:, 0, :], op=ALU.mult, r=[ps_R, cs_R], w=[t2_R])
            if is_k:
                for bb in range(2):
                    n = 2 * c + bb
                    OP("dve", "scalar_tensor_tensor", out=KT[:, p, col0 + bb * 256: col0 + (bb + 1) * 256],
                       in0=t1[:, bb * 256:(bb + 1) * 256], scalar=1.0, in1=t2[:, bb * 256:(bb + 1) * 256],
                       op0=ALU.mult, op1=ALU.add, accum_out=kms[:, p, n:n + 1], r=[t1_R, t2_R], w=[KT_R[p][c], kms_R])
            else:
                OP("dve", "tensor_tensor", out=Q[:, p, col0:col0 + 512], in0=t1, in1=t2, op=ALU.add, r=[t1_R, t2_R], w=[Q_R[p][c]])

        x_tiles = x_loc.rearrange("(n p) d -> n p d", p=128)

        def norm_pieces(c, cb, with_halo):
            pieces = []

            def mk(i):
                def f():
                    if state.get("first_chunk") and i == 3:
                        dma(xh[:], x_tiles[4 * c + i], w=[xh_R])
                        return norm_tile(xh[:], xh_R, 128, xnT[:, cb, :, 32 + 128 * i: 32 + 128 * (i + 1)], xnT_R[cb][i])
                    b = state["ld"] % 3
                    state["ld"] += 1
                    dma(xst[:, b, :], x_tiles[4 * c + i], w=[xst_R[b]])
                    return norm_tile(xst[:, b, :], xst_R[b], 128, xnT[:, cb, :, 32 + 128 * i: 32 + 128 * (i + 1)], xnT_R[cb][i])
                return f

            def halo():
                dma(xh[0:32, :], x_halo[c], w=[xh_R])
                return norm_tile(xh[0:32, :], xh_R, 32, xnT[:, cb, :, 0:32], xnT_R[cb][4])
            if with_halo:
                pieces.append(halo)
            for i in range(4):
                pieces.append(mk(i))
            return pieces

        class Pipe3:
            def __init__(self):
                self.s2 = []
                self.s3 = []

            def tick(self, pending):
                if self.s3:
                    self.s3.pop(0)()
                if self.s2:
                    self.s3.append(self.s2.pop(0)())
                if pending:
                    self.s2.append(pending.pop(0)())

            def drain(self, pending):
                while pending or self.s2 or self.s3:
                    self.tick(pending)

        state["scale_on_pool"] = False
        state["first_chunk"] = True
        stgA = [(R2[:, i * 1024:(i + 1) * 1024].bitcast(F32), [Region()]) for i in range(16)]
        pieces0 = norm_pieces(0, 0, False)
        pipe0 = Pipe3()
        pipe0.tick(pieces0)
        pipe0.tick(pieces0)
        fin_k = load_w_in(WQKV[:, :, 512:1024], 512, 512, lambda kt: [WK_R[kt]], stgA[0:8])
        pipe0.tick(pieces0)
        pipe0.tick(pieces0)
        fin_q = load_w_in(WQKV[:, :, 0:512], 0, 512, lambda kt: [WQ_R[kt]], stgA[8:16])
        fin_k()
        fin_v = load_w_in(WQKV[:, :, 1024:1536], 1024, 512, lambda kt: [WV_R[kt]], stgA[0:8])
        pipe0.drain(pieces0)
        state["scale_on_pool"] = True
        state["first_chunk"] = False
        fin_q()
        fin_v()
        for c in range(8):
            own = c < 4
            cb = c % 2
            col0 = 512 * c
            csb = c % 2
            dma(cs2[:, csb, 0, :], cosT_d[:, col0:col0 + 512], w=[cs_R2[csb]])
            dma(cs2[:, csb, 1, :], sinT_d[:, col0:col0 + 512], w=[cs_R2[csb]])
            if c + 1 < 8:
                pending = norm_pieces(c + 1, (c + 1) % 2, False)
            else:
                pending = norm_pieces(0, 0, True)
            pipeA = Pipe3()
            if c == 6:
                stgB = [(yb[:, 0:2, :].rearrange("p a t -> p (a t)"), [y_R[0], y_R[1]]),
                        (yb[:, 2:4, :].rearrange("p a t -> p (a t)"), [y_R[2], y_R[3]]),
                        (dg[:, 1].rearrange("p a t -> p (a t)").bitcast(F32), [dg_R[1]]),
                        (lnmv, [lnm_R, lnv_R])]
                stgB += [(CVT[:, ct_, :].bitcast(F32), [Region()]) for ct_ in range(4)]
                finish_wb = load_w_in(WB, 1536, 1024, lambda kt: [W_R[kt]], stgB, n_now=4)
            if c == 7:
                finish_wb.more()
            xr_main = xnT_R[cb][0:4]
            jobs = []
            if c == 0:
                jobs = [("k", p) for p in range(4)] + [("q", p) for p in range(4)] + [("v", p) for p in range(4)]
            else:
                for p in range(4):
                    jobs.append(("k", p))
                    if own:
                        jobs.append(("q", p))
                    jobs.append(("v", p))
            prev = None
            for ji, (kind, idx) in enumerate(jobs):
                pa_ap, pa_R = next_pa()
                if kind == "v":
                    i = idx
                    ktile = 4 * c + i
                    for k in range(8):
                        OP("pe", "matmul", pa_ap, lhsT=xnT[:, cb, k, 32 + 128 * i: 32 + 128 * (i + 1)], rhs=WQKV[:, k, 1024:1536],
                           start=(k == 0), stop=(k == 7), r=[WV_R[k], xnT_R[cb][i]], w=[pa_R])
                    src = pa_ap.rearrange("p (h d) -> p h d", h=8)
                    dst = V4[:, ktile, :, 0:64]
                    OP("act", "activation", out=dst, in_=src, func=AF.Copy, r=[pa_R], w=[V_R[ktile]])
                    cur = None
                else:
                    p = idx
                    cb0 = 512 if kind == "k" else 0
                    wr_ = WK_R if kind == "k" else WQ_R
                    for k in range(8):
                        OP("pe", "matmul", pa_ap, lhsT=WQKV[:, k, cb0 + 128 * p: cb0 + 128 * (p + 1)], rhs=xnT[:, cb, k, 32:544],
                           start=(k == 0), stop=(k == 7), r=[wr_[k]] + xr_main, w=[pa_R])
                    qb = rope_part1(pa_ap, pa_R)
                    cur = (pa_ap, pa_R, qb, kind == "k", p, c, col0, csb)
                if prev is not None:
                    rope_part2(*prev)
                prev = cur
                period = 3 if own else 2
                if ji >= 1:
                    pipeA.tick(pending)
            if prev is not None:
                rope_part2(*prev)
            pipeA.drain(pending)

        km_R = Region()
        OP("dve", "tensor_scalar", out=kmf[:], in0=kms[:], scalar1=1.0 / BL, scalar2=None, op0=ALU.mult, r=[kms_R], w=[km_R])
        OP("dve", "tensor_copy", out=kmhi[:], in_=kmf[:], r=[km_R], w=[km_R])
        OP("dve", "tensor_tensor", out=kmlo[:], in0=kmf[:], in1=kmhi[:], op=ALU.subtract, r=[km_R], w=[km_R])
        if debug:
            dma(dbg["kt"], R2[:, 0:16384], r=[KT_R[p][c] for p in range(4) for c in range(8)])
            dma(dbg["q"], Q[:].rearrange("p a t -> p (a t)"), r=[Q_R[p][c] for p in range(4) for c in range(4)])
            dma(dbg["v"], VF, r=V_R + [vones_R])
            dma(dbg["km"], kmf[:].rearrange("p a n -> p (a n)"), r=[km_R])


        class _CarveAt(Carver):
            def __init__(self, off):
                self.off = off
        cvA = _CarveAt(off_cs2)
        scA = dict(biasp=cvA.get([128, 4, 8, 80], BF16), gbias=cvA.get([128, 4, 16], F32), ownind=cvA.get([128, 4, 16], F32),
                   validf=cvA.get([128, 4, 16], F32), km128=cvA.get([128, 2, 8, 16], BF16), gm=cvA.get([128, 8, 16], F32),
                   top=cvA.get([128, 8, 8], F32), selm=cvA.get([128, 8, 16], F32))
        assert cvA.off <= off_cs2 + 8192, cvA.off
        scA["gate4"] = _CarveAt(off_qraw).get([128, 512], F32)
        for nm_ in ("gate4_R", "gm_R", "top_R", "selm_R", "biasp_R", "km_R", "tab_R"):
            scA[nm_] = Region()

        def p2_init_A():
            allA = [scA[n_] for n_ in ("gm_R", "top_R", "selm_R", "biasp_R", "km_R", "tab_R")]
            OP("pool", "memset", cs2.rearrange("p a b t -> p (a b t)"), 0.0, w=[cs_R2[0], cs_R2[1]] + allA)
            OP("pool", "memset", scA["gate4"], 0.0, w=[scA["gate4_R"], qraw_R[0], qraw_R[1]])
            dma(scA["gbias"].rearrange("p a t -> p (a t)"), gb_d[:, 0:64], w=[scA["tab_R"]])
            dma(scA["ownind"].rearrange("p a t -> p (a t)"), own_d[:, 0:64], w=[scA["tab_R"]])
            dma(scA["validf"].rearrange("p a t -> p (a t)"), valid_d[:, 0:64], w=[scA["tab_R"]])
            km4a = scA["km128"][0:64].rearrange("p a (b two) n -> p a b two n", two=2)
            for a_, kmx in enumerate((kmhi, kmlo)):
                OP("dve", "tensor_copy", out=km4a[:, a_, :, 0, :], in_=kmx[0:64, :, :], r=[km_R], w=[scA["km_R"]])
                OP("dve", "tensor_copy", out=km4a[:, a_, :, 1, :], in_=kmx[64:128, :, :], r=[km_R], w=[scA["km_R"]])

        def prep_pieces(c, sc):
            qb = (c + 1) % 2
            col0 = 512 * c
            gate4, gm, top, selm, biasp, km128 = sc["gate4"], sc["gm"], sc["top"], sc["selm"], sc["biasp"], sc["km128"]
            gbias, validf, ownind = sc["gbias"], sc["validf"], sc["ownind"]
            gate4_R, gm_R, top_R, selm_R, biasp_R, km64_R, tab_R = (sc["gate4_R"], sc["gm_R"], sc["top_R"], sc["selm_R"],
                                                                     sc["biasp_R"], sc["km_R"], sc["tab_R"])
            qp4 = QPc[:, qb].rearrange("p (a two) t -> p a two t", two=2)

            def p_q1():
                OP("dve", "tensor_copy", out=qp4[0:64, :, 0, :], in_=Q[0:64, :, col0:col0 + 512], r=[Q_R[p][c] for p in range(4)], w=[QP_R[qb]])
                OP("dve", "tensor_copy", out=qp4[0:64, :, 1, :], in_=Q[64:128, :, col0:col0 + 512], r=[Q_R[p][c] for p in range(4)], w=[QP_R[qb]])

            def p_q():
                for i in range(4):
                    for h in range(NH):
                        for a_ in range(2):
                            OP("pe", "matmul", PTF[:, i * 128 + h * 16: i * 128 + (h + 1) * 16], lhsT=QPc[0:80, qb, h, i * 128:(i + 1) * 128],
                               rhs=km128[0:80, a_, h, :], start=(a_ == 0), stop=(a_ == 1), r=[QP_R[qb], km64_R], w=[PT_R])
                OP("dve", "tensor_copy", out=gate4, in_=PTF[:], r=[PT_R], w=[gate4_R])

            def p_sel(i):
                def f():
                    qt = 4 * c + i
                    OP("dve", "tensor_tensor", out=gm, in0=gate4[:, i * 128:(i + 1) * 128].rearrange("p (h n) -> p h n", h=8),
                       in1=gbias[:, qt, :].unsqueeze(1).to_broadcast([128, 8, 16]), op=ALU.add, r=[gate4_R, tab_R], w=[gm_R])
                    for h in range(NH):
                        OP("dve", "max", out=top[:, h, :], in_=gm[:, h, :], r=[gm_R], w=[top_R])
                    OP("dve", "tensor_tensor", out=selm, in0=gm, in1=top[:, :, 2:3].to_broadcast([128, 8, 16]), op=ALU.is_ge,
                       r=[gm_R, top_R], w=[selm_R])
                    OP("dve", "tensor_tensor", out=selm, in0=selm, in1=validf[:, qt, :].unsqueeze(1).to_broadcast([128, 8, 16]), op=ALU.mult,
                       r=[selm_R, tab_R], w=[selm_R])
                    OP("dve", "tensor_tensor", out=selm, in0=selm, in1=ownind[:, qt, :].unsqueeze(1).to_broadcast([128, 8, 16]), op=ALU.add,
                       r=[selm_R, tab_R], w=[selm_R])
                    OP("dve", "tensor_scalar", out=biasp[:, i, :, 64:80], in0=selm, scalar1=-NEG, scalar2=NEG,
                       op0=ALU.mult, op1=ALU.add, r=[selm_R], w=[biasp_R])
                return f

            def p_bias(h):
                def f():
                    for i in range(4):
                        OP("pe", "matmul", PTF[0:80, i * 128:(i + 1) * 128], lhsT=biasp[:, i, h, :], rhs=ident[:], start=True, stop=True,
                           r=[biasp_R, c_ident], w=[PT_R])
                    OP("dve", "tensor_copy", out=QPc[64:80, qb, h, :], in_=PTF[64:80, :], r=[PT_R], w=[QP_R[qb]])
                return f
            return [p_q1, p_q] + [p_sel(i) for i in range(4)] + [p_bias(h) for h in range(NH)]

        cvK = Carver()
        KA = cvK.get([128, 3, 4096], BF16)
        causal = cvK.get([128, 4, 512], BF16)
        assert cvK.off <= 33792
        KA_R = [Region() for _ in range(3)]
        KAoh_R = [Region() for _ in range(3)]
        causal_R = Region()

        def build_ka(c, h, deps=()):
            buf = (NH * c + h) % 3
            p, eo = h // 2, h % 2
            nk = 512 * (c + 1)
            for k0 in (0, 2048):
                OP("dve", "tensor_copy", out=KA[0:64, buf, k0:k0 + nk], in_=KT[eo * 64:(eo + 1) * 64, p, k0:k0 + nk],
                   r=[KT_R[p][cc] for cc in range(8)], w=[KA_R[buf]], deps=deps)

        fence = OP("dve", "tensor_copy", out=w_dw_bf[:, :, 0:31], in_=w_dw, r=[c_vecs], w=[wdb_R] + WQ_R + WK_R + WV_R)
        state["junk"] = (qraw.rearrange("p a t -> p (a t)"), [qraw_R[0], qraw_R[1]])
        state["scale_on_pool"] = False
        finish_wb(deps=[fence])
        mh512 = mhalf1[:].to_broadcast([128, 512])
        def ln_stats_groups(c, banks=None):
            if banks is None:
                s_ap, s_R = PR[0][:], PR_R[0]
                q_ap, q_R = PR[1][:], PR_R[1]
            else:
                (s_ap, s_R), (q_ap, q_R) = banks

            def mm_s(ct):
                return lambda: OP("pe", "matmul", s_ap, lhsT=onesf[:], rhs=yb[:, ct, :], start=(ct == 0), stop=(ct == 3),
                                  r=[y_R[ct], c_onesf], w=[s_R])

            def sq(ct):
                return lambda: OP("act", "activation", out=t2, in_=yb[:, ct, :], func=AF.Square, r=[y_R[ct]], w=[t2_R])

            def mm_q(ct):
                return lambda: OP("pe", "matmul", q_ap, lhsT=onesf[:], rhs=t2, start=(ct == 0), stop=(ct == 3), r=[t2_R, c_onesf], w=[q_R])
            if banks is not None:
                per_ct = [[mm_s(ct), sq(ct), mm_q(ct)] for ct in range(4)]
                fin = [lambda: OP("dve", "tensor_scalar", out=lnm, in0=s_ap, scalar1=1.0 / 512, scalar2=None, op0=ALU.mult, r=[s_R], w=[lnm_R]),
                       lambda: OP("dve", "tensor_tensor", out=lnv, in0=lnm, in1=lnm, op=ALU.mult, r=[lnm_R], w=[lnv_R]),
                       lambda: OP("dve", "scalar_tensor_tensor", out=lnv, in0=q_ap, scalar=1.0 / 512, in1=lnv, op0=ALU.mult,
                                  op1=ALU.subtract, r=[q_R, lnv_R], w=[lnv_R]),
                       lambda: OP("dve", "tensor_scalar", out=lnv, in0=lnv, scalar1=1.0, scalar2=EPS, op0=ALU.mult, op1=ALU.add,
                                  r=[lnv_R], w=[lnv_R]),
                       lambda: OP("act", "activation", out=lnv, in_=lnv, func=AF.Sqrt, r=[lnv_R], w=[lnv_R]),
                       lambda: OP("dve", "reciprocal", out=lnv, in_=lnv, r=[lnv_R], w=[lnv_R])]
                return per_ct, fin
            g = []
            g.append([mm_s(0), mm_s(1), mm_s(2), mm_s(3), sq(0)])
            g.append([mm_q(0), sq(1)])
            g.append([mm_q(1), sq(2)])
            g.append([mm_q(2), sq(3)])
            g.append([mm_q(3),
                      lambda: OP("dve", "tensor_scalar", out=lnm, in0=s_ap, scalar1=1.0 / 512, scalar2=None, op0=ALU.mult, r=[s_R], w=[lnm_R]),
                      lambda: OP("dve", "tensor_tensor", out=lnv, in0=lnm, in1=lnm, op=ALU.mult, r=[lnm_R], w=[lnv_R])])
            g.append([lambda: OP("dve", "scalar_tensor_tensor", out=lnv, in0=q_ap, scalar=1.0 / 512, in1=lnv, op0=ALU.mult, op1=ALU.subtract,
                                 r=[q_R, lnv_R], w=[lnv_R]),
                      lambda: OP("dve", "tensor_scalar", out=lnv, in0=lnv, scalar1=1.0, scalar2=EPS, op0=ALU.mult, op1=ALU.add,
                                 r=[lnv_R], w=[lnv_R])])
            g.append([lambda: OP("act", "activation", out=lnv, in_=lnv, func=AF.Sqrt, r=[lnv_R], w=[lnv_R])])
            for _ in range(2):
                g.append([])
            g.append([lambda: OP("dve", "reciprocal", out=lnv, in_=lnv, r=[lnv_R], w=[lnv_R])])
            return g

        def ln_norm_ops(c, col0):
            ops = []
            A = ops.append
            for ct in range(4):
                zb, zb_R = ((t2, t2_R), (xh[:, 0:512], xh_R))[ct % 2]
                A(lambda ct=ct, zb=zb, zb_R=zb_R: OP("dve", "tensor_tensor", out=zb, in0=yb[:, ct, :], in1=lnm, op=ALU.subtract,
                                                     r=[y_R[ct], lnm_R], w=[zb_R]))
                A(lambda zb=zb, zb_R=zb_R: OP("dve", "tensor_tensor", out=zb, in0=zb, in1=lnv, op=ALU.mult, r=[zb_R, lnv_R], w=[zb_R]))
                A(lambda ct=ct, zb=zb, zb_R=zb_R: OP("act", "activation", out=CVT[:, ct, col0:col0 + 512], in_=zb, func=AF.Silu,
                                                     bias=b_ln[:, ct:ct + 1], scale=g_ln[:, ct:ct + 1], r=[zb_R, c_vecs],
                                                     w=CV_R[4 * c:4 * c + 4]))
            return ops

        pending_stats = []

        def pop_stats():
            if pending_stats:
                for f in pending_stats.pop(0):
                    f()

        pending_ln = []
        for c in range(4):
            cb = c % 2
            col0 = 512 * c
            pendingB = norm_pieces(c + 1, (c + 1) % 2, True) if c + 1 < 4 else []
            pipeB = Pipe3()
            halves = [(ct_, hf_, k0_, nk_) for ct_ in range(4) for hf_, (k0_, nk_) in enumerate(((0, 16), (16, 15)))]

            def build_diag(ix):
                ct_, hf_, k0_, nk_ = halves[ix]
                OP("dve", "tensor_tensor", out=dg[:, hf_, 0:nk_, :], in0=ident[:].unsqueeze(1).to_broadcast([128, nk_, 128]),
                   in1=w_dw_bf[:, ct_, k0_:k0_ + nk_].unsqueeze(2).to_broadcast([128, nk_, 128]), op=ALU.mult,
                   r=[c_ident, wdb_R], w=[dg_R[hf_]])
            build_diag(0)
            build_diag(1)
            xr_main = xnT_R[cb][0:4]
            for ct in range(4):
                pipeB.tick(pendingB)
                a_ap, a_R = next_pa()
                g_ap, g_R = next_pa()
                for (dst_ap, dst_R, cbase) in ((a_ap, a_R, 0), (g_ap, g_R, 512)):
                    for k in range(8):
                        OP("pe", "matmul", dst_ap, lhsT=WB[:, k, cbase + 128 * ct: cbase + 128 * (ct + 1)], rhs=xnT[:, cb, k, 32:544],
                           start=(k == 0), stop=(k == 7), r=[W_R[k]] + xr_main, w=[dst_R])
                    pop_stats()
                for (off, cbase) in ((0, 0), (32, 512)):
                    for k in range(8):
                        OP("pe", "matmul", PH[:, off:off + 32], lhsT=WB[:, k, cbase + 128 * ct: cbase + 128 * (ct + 1)],
                           rhs=xnT[:, cb, k, 0:32], start=(k == 0), stop=(k == 7), r=[W_R[k], xnT_R[cb][4]], w=[PH_R])
                pop_stats()
                pipeB.tick(pendingB)
                OP("act", "activation", out=t1, in_=g_ap, func=AF.Sigmoid, bias=b_g[:, ct:ct + 1], r=[g_R, c_vecs], w=[t1_R])
                OP("dve", "scalar_tensor_tensor", out=hT[:, ct, 32:544], in0=a_ap, scalar=b_a[:, ct:ct + 1], in1=t1,
                   op0=ALU.add, op1=ALU.mult, r=[a_R, t1_R, c_vecs], w=[hT_R[ct]])
                OP("act", "activation", out=sigh, in_=PH[:, 32:64], func=AF.Sigmoid, bias=b_g[:, ct:ct + 1], r=[PH_R, c_vecs], w=[sigh_R])
                OP("dve", "scalar_tensor_tensor", out=hT[:, ct, 0:32], in0=PH[:, 0:32], scalar=b_a[:, ct:ct + 1], in1=sigh,
                   op0=ALU.add, op1=ALU.mult, r=[PH_R, sigh_R, c_vecs], w=[hT_R[ct]])
                OP("dve", "tensor_scalar", out=hT[:, ct, 0:32], in0=hT[:, ct, 0:32], scalar1=halo_mask[:, c:c + 1], scalar2=None,
                   op0=ALU.mult, r=[hT_R[ct], c_vecs], w=[hT_R[ct]])
            pipeB.drain(pendingB)
            while pending_stats:
                pop_stats()
            early = ([p2_init_A] + prep_pieces(0, scA)) if c == 3 else []
            own_q = []
            if c == 3:
                last_per_ct, last_fin = ln_stats_groups(c, banks=(next_pa(), next_pa()))
            tapn = 0
            for ix, (ct, hf, k0, nk) in enumerate(halves):
                acc_ap, acc_R = PR[ct % 2][:], PR_R[ct % 2]
                for kk in range(nk):
                    k = k0 + kk
                    OP("pe", "matmul", acc_ap, lhsT=dg[:, hf, kk, :], rhs=hT[:, ct, 2 + k:514 + k], start=(k == 0), stop=(k == 30),
                       r=[dg_R[hf], hT_R[ct]], w=[acc_R])
                    if pending_ln and k % 2 == 1:
                        pending_ln.pop(0)()
                    tapn += 1
                    if early and tapn % 8 == 0:
                        early.pop(0)()
                    if own_q and tapn % 4 == 2:
                        own_q.pop(0)()
                if ix + 2 < len(halves):
                    build_diag(ix + 2)
                if hf == 1:
                    if ct == 0:
                        for f in pending_ln:
                            f()
                        pending_ln = []
                    OP("act", "activation", out=yb[:, ct, :], in_=acc_ap, func=AF.Identity, bias=b_dw[:, ct:ct + 1],
                       r=[acc_R, c_vecs], w=[y_R[ct]])
                    if c == 3:
                        own_q.extend(last_per_ct[ct])
            if c == 3:
                for f in own_q:
                    f()
                pending_stats.append(last_fin)
            else:
                pending_stats.extend(ln_stats_groups(c))
            pending_ln = ln_norm_ops(c, col0)
        soft = []
        for e_ in ENGS:
            real = [o for o in S.ops[e_] if o.fn is not None]
            if real:
                soft.append(real[-1])
        dma(causal.rearrange("p a t -> p (a t)"), causal_d, w=[causal_R], deps=soft)
        for b3 in range(3):
            dma(KA[64:80, b3, :], onehot_d, w=[KAoh_R[b3]], deps=soft)
        build_ka(0, 0, deps=soft)
        tail = [f for grp in pending_stats for f in grp] + pending_ln
        del pending_stats[:]
        pending_ln = tail
        while pending_ln or early:
            if pending_ln:
                pending_ln.pop(0)()
            if early:
                early.pop(0)()

        S.barrier()
        cv = cvK
        biasp = cv.get([128, 4, 8, 80], BF16)
        km128 = cv.get([128, 2, 8, 16], BF16)
        km64 = km128[0:64]
        gate4 = cv.get([128, 512], F32)
        gate4_R = Region()
        PTb = cv.get([128, 3, 2, 512], BF16)
        un = cv.get([128, 2, 512], F32)
        gm = cv.get([128, 8, 16], F32)
        top = cv.get([128, 8, 8], F32)
        selm = cv.get([128, 8, 16], F32)
        gbias = cv.get([128, 16, 16], F32)
        ownind = cv.get([128, 16, 16], F32)
        validf = cv.get([128, 16, 16], F32)
        print("phase 2 arena bytes", cv.off)
        PTb_R = [Region() for _ in range(3)]
        un_R = [Region(), Region()]
        gm_R = Region()
        top_R = Region()
        selm_R = Region()
        biasp_R = Region()
        km64_R = Region()
        tab_R = Region()
        dma(gbias.rearrange("p a t -> p (a t)"), gb_d, w=[tab_R])
        dma(ownind.rearrange("p a t -> p (a t)"), own_d, w=[tab_R])
        dma(validf.rearrange("p a t -> p (a t)"), valid_d, w=[tab_R])
        OP("pool", "memset", QPc[:, 0].rearrange("p h t -> p (h t)"), 0.0, w=[QP_R[0]])
        OP("pool", "memset", biasp.rearrange("p a h t -> p (a h t)"), 0.0, w=[biasp_R])
        OP("dve", "memset", km128.rearrange("p a h n -> p (a h n)"), 0.0, w=[km64_R])
        km4 = km64.rearrange("p a (b two) n -> p a b two n", two=2)
        for a_, kmx in enumerate((kmhi, kmlo)):
            OP("dve", "tensor_copy", out=km4[:, a_, :, 0, :], in_=kmx[0:64, :, :], r=[km_R], w=[km64_R])
            OP("dve", "tensor_copy", out=km4[:, a_, :, 1, :], in_=kmx[64:128, :, :], r=[km_R], w=[km64_R])

        scB = dict(gate4=gate4, gm=gm, top=top, selm=selm, biasp=biasp, km128=km128, gbias=gbias, validf=validf, ownind=ownind,
                   gate4_R=gate4_R, gm_R=gm_R, top_R=top_R, selm_R=selm_R, biasp_R=biasp_R, km_R=km64_R, tab_R=tab_R)
        denr = cv.get([128, 2, 512], F32)
        rc4 = cv.get([128, 2, 4], F32)
        rchl = cv.get([128, 2, 2, 4], BF16)
        sel64b = cv.get([128, 128], BF16)
        wo_stg = cv.get([128, 1024], F32)
        wo_stg_R = Region()
        WO = Q[:].rearrange("p a t -> p (a t)").rearrange("p (k c) -> p k c", k=8)
        WO_R = [Region() for _ in range(8)]
        all_Q_R = [Q_R[p_][c_] for p_ in range(4) for c_ in range(4)]
        w_out_v = w_out.rearrange("(k p) c -> p k c", p=128)

        def wout_prefetch_pieces():
            res = []
            for k in range(8):
                def f(k=k):
                    dma(wo_stg, w_out_v[:, k, :], w=[wo_stg_R])
                    OP("dve", "tensor_copy", out=WO[:, k, :], in_=wo_stg, r=[wo_stg_R], w=[WO_R[k]] + all_Q_R)
                res.append(f)
            return res

        denr_R = [Region(), Region()]
        rc_R = [Region(), Region()]
        sel64b_R = Region()
        OP("dve", "tensor_copy", out=sel64b, in_=sel64[:], r=[c_sel64], w=[sel64b_R])

        work = []
        for c in range(4):
            tiles = list(range(4 * (c + 1))) + [16 + t for t in range(2 * N_OTHER_BLOCKS[c])]
            groups = [tiles[g:g + 2] for g in range(0, len(tiles), 2)]
            for h in range(NH):
                for gi, grp in enumerate(groups):
                    work.append((c, h, gi, grp, len(groups)))

        def emit_qk(wi):
            c, h, gi, grp, ngroups = work[wi]
            qb = (c + 1) % 2
            buf = (NH * c + h) % 3
            sb_i = wi % 2
            for s_, kt in enumerate(grp):
                diag = 4 * c <= kt < 4 * c + 4
                sc_ap, sc_R = PA[sb_i][:, s_, :], PA_R[sb_i][s_]
                OP("pe", "matmul", sc_ap, lhsT=KA[0:80, buf, kt * 128:(kt + 1) * 128], rhs=QPc[0:80, qb, h, :], start=True, stop=not diag,
                   r=[KA_R[buf], KAoh_R[buf], QP_R[qb]], w=[sc_R])
                if diag:
                    OP("pe", "matmul", sc_ap, lhsT=ident[:], rhs=causal[:, kt - 4 * c, :], start=False, stop=True,
                       r=[causal_R, c_ident], w=[sc_R])

        def emit_exp(wi):
            c, h, gi, grp, ngroups = work[wi]
            sb_i = wi % 2
            pb = wi % 3
            ng = len(grp)
            OP("act", "activation", out=PTb[:, pb, 0:ng, :], in_=PA[sb_i][:, 0:ng, :], func=AF.Exp, scale=0.125,
               r=PA_R[sb_i][0:ng], w=[PTb_R[pb]])

        def emit_pv(wi):
            c, h, gi, grp, ngroups = work[wi]
            pb = wi % 3
            ng = len(grp)
            ob = (c * NH + h) % 2
            o_ap, o_R = PR[ob][:], PR_R[ob]
            for s_, kt in enumerate(grp):
                first = (gi == 0 and s_ == 0)
                last = (gi == ngroups - 1 and s_ == ng - 1)
                OP("pe", "matmul", o_ap, lhsT=VF[:, kt * VT + h * 65: kt * VT + h * 65 + 128], rhs=PTb[:, pb, s_, :],
                   start=first, stop=last, r=[V_R[kt], vones_R, PTb_R[pb]], w=[o_R])

        def emit_norm1(c, h):
            ob = (c * NH + h) % 2
            o_ap, o_R = PR[ob][:], PR_R[ob]
            ub = h % 2
            OP("dve", "tensor_copy", out=denr[64:65, ub, :], in_=o_ap[64:65, :], r=[o_R], w=[denr_R[ub]])
            OP("dve", "tensor_copy", out=un[0:64, ub, :], in_=o_ap[0:64, :], r=[o_R], w=[un_R[ub]])

        def norm_tail_pieces(c, h):
            p, eo = h // 2, h % 2
            ub = h % 2
            col0 = 512 * c

            def den_mm(j):
                def f():
                    OP("pe", "matmul", PTF[:, j:j + 1], lhsT=denr[64:65, ub, 128 * j:128 * (j + 1)], rhs=onesf[64:65, 0:1],
                       start=True, stop=True, r=[denr_R[ub], c_onesf], w=[PT_R])
                return f

            def recip():
                OP("dve", "reciprocal", out=rc4[:, ub, :], in_=PTF[:, 0:4], r=[PT_R], w=[rc_R[ub]])
                OP("dve", "tensor_copy", out=rchl[:, ub, 0, :], in_=rc4[:, ub, :], r=[rc_R[ub]], w=[rc_R[ub]])
                OP("dve", "tensor_tensor", out=rchl[:, ub, 1, :], in0=rc4[:, ub, :], in1=rchl[:, ub, 0, :], op=ALU.subtract,
                   r=[rc_R[ub]], w=[rc_R[ub]])

            def bc_mm(j):
                def f():
                    for a_ in range(2):
                        OP("pe", "matmul", PH[:, 128 * j:128 * (j + 1)], lhsT=rchl[:, ub, a_, j:j + 1].to_broadcast([128, 128]), rhs=ident[:],
                           start=(a_ == 0), stop=(a_ == 1), r=[rc_R[ub], c_ident], w=[PH_R])
                return f

            def final():
                OP("dve", "tensor_tensor", out=MX[eo * 64:(eo + 1) * 64, p, col0:col0 + 512], in0=un[0:64, ub, :], in1=PH[0:64, :],
                   op=ALU.mult, r=[un_R[ub], PH_R], w=MX_R[4 * c:4 * c + 4])
            def den_all():
                for j in range(4):
                    den_mm(j)()
                recip()
            return [den_all, bc_mm(0), bc_mm(1), bc_mm(2), bc_mm(3), final]

        emit_qk(0)
        emit_qk(1)
        deferred = []
        for wi in range(len(work)):
            c, h, gi, grp, ngroups = work[wi]
            if gi == 0:
                nxt = NH * c + h + 1
                if nxt < 4 * NH:
                    build_ka(nxt // NH, nxt % NH)
            emit_exp(wi)
            if wi + 2 < len(work):
                emit_qk(wi + 2)
            emit_pv(wi)
            nd = []
            for (cnt, fn) in deferred:
                if cnt <= 1:
                    fn()
                else:
                    nd.append((cnt - 1, fn))
            deferred = nd
            if gi == ngroups - 1:
                emit_norm1(c, h)
                tp = norm_tail_pieces(c, h)
                if ngroups >= 8:
                    sched_ = [4, 8, 9, 10, 11, 11]
                else:
                    sched_ = [3, 5, 5, 6, 6, 6]
                for cnt_, fn_ in zip(sched_, tp):
                    deferred.append((cnt_, fn_))
                if h == 0 and c + 1 < 4:
                    offs = [1, 3, 5, 7, 9, 11] + [15 + (3 if c >= 1 else 1) * hh for hh in range(NH)]
                    for off_, piece in zip(offs, prep_pieces(c + 1, scB)):
                        deferred.append((off_, piece))
                if c == 3 and h == 0:
                    for pi, piece in enumerate(wout_prefetch_pieces()):
                        deferred.append((2 + 12 * pi, piece))
        for (cnt, fn) in deferred:
            fn()

        if debug:
            dma(dbg["mx"][:, 0:8192], R1[:, 0:8192], r=MX_R)
            dma(dbg["mx"][:, 8192:16384], CVT[:].rearrange("p a t -> p (a t)"), r=CV_R)
        S.barrier()
        cv = Carver()
        W1B = cv.get([128, 2, 8, 512], BF16)
        W2B = cv.get([128, 2, 4, 1024], BF16)
        wst3 = cv.get([128, 2, 2048], F32)
        FF = cv.get([128, 2, 4, 512], BF16)
        rl = cv.get([128, 2, 512], F32)
        hn = cv.get([128, 3, 1024], BF16)
        junk3 = cv.get([128, 1024], BF16)
        ost = cv.get([128, 1024], F32)
        gfin = cv.get([128, 1024], F32)
        print("phase 3 arena bytes", cv.off)
        W1B_R = [[Region() for _ in range(8)] for _ in range(2)]
        W2B_R = [[Region() for _ in range(4)] for _ in range(2)]
        wst3_R = [Region(), Region()]
        FF_R = [[Region() for _ in range(4)] for _ in range(2)]
        rl_R = [Region(), Region()]
        hn_R = [Region(), Region(), Region()]
        ost_R = Region()
        gfin_R = Region()
        H1_R = [Region() for _ in range(16)]
        HN = MX
        HN_R = [Region() for _ in range(16)]
        dma(gfin, gfin_d, w=[gfin_R])

        cast_i = {"n": 0}

        def cast(out, in_, r, w, scale=None):
            eng = ("dve", "act")[cast_i["n"] % 2]
            cast_i["n"] += 1
            if eng == "dve":
                if scale is None:
                    OP("dve", "tensor_copy", out=out, in_=in_, r=r, w=w)
                else:
                    OP("dve", "tensor_scalar", out=out, in0=in_, scalar1=scale, scalar2=None, op0=ALU.mult, r=r, w=w)
            else:
                if scale is None:
                    OP("act", "activation", out=out, in_=in_, func=AF.Copy, r=r, w=w)
                else:
                    OP("act", "activation", out=out, in_=in_, func=AF.Copy, scale=scale, r=r, w=w)

        stage_i = {"n": 0}

        def stage_dma(src_ap):
            b = stage_i["n"] % 2
            stage_i["n"] += 1
            dma(wst3[:, b, :], src_ap, w=[wst3_R[b]])
            return b

        w_out_v = w_out.rearrange("(k p) c -> p k c", p=128)
        w1_v = w_m1.rearrange("(k p) c -> p k c", p=128)
        w2_v = w_m2.rearrange("(f p) c -> p f c", p=128)

        def wout_pieces():
            res = []
            for k2 in range(4):
                def d(k2=k2):
                    return stage_dma(w_out_v[:, 2 * k2:2 * k2 + 2, :])

                def cfn(b, k2=k2):
                    for kk in range(2):
                        cast(WO[:, 2 * k2 + kk, :], wst3[:, b, kk * 1024:(kk + 1) * 1024], [wst3_R[b]], [WO_R[2 * k2 + kk]])
                res.append((d, cfn))
            return res

        def ffblock_pieces(fb):
            wb = fb % 2
            res = []
            for half in range(2):
                def d(half=half):
                    return stage_dma(w1_v[:, 4 * half:4 * half + 4, fb * 512:(fb + 1) * 512])

                def cfn(b, half=half):
                    for kk in range(4):
                        k = 4 * half + kk
                        cast(W1B[:, wb, k, :], wst3[:, b, kk * 512:(kk + 1) * 512], [wst3_R[b], c_vecs], [W1B_R[wb][k]], scale=g_mlp[:, k:k + 1])
                res.append((d, cfn))
            for half in range(2):
                def d(half=half):
                    return stage_dma(w2_v[:, 4 * fb + 2 * half: 4 * fb + 2 * half + 2, :])

                def cfn(b, half=half):
                    for kk in range(2):
                        f = 2 * half + kk
                        cast(W2B[:, wb, f, :], wst3[:, b, kk * 1024:(kk + 1) * 1024], [wst3_R[b]], [W2B_R[wb][f]])
                res.append((d, cfn))
            return res

        class Loader:
            def __init__(self):
                self.queue = []
                self.inflight = []

            def add(self, pieces):
                self.queue.extend(pieces)
                self.pump()

            def pump(self):
                while self.queue and len(self.inflight) < 2:
                    d, cfn = self.queue.pop(0)
                    self.inflight.append((d(), cfn))

            def tick(self, n=1):
                for _ in range(n):
                    if not self.inflight:
                        return
                    b, cfn = self.inflight.pop(0)
                    cfn(b)
                    self.pump()

            def drain(self):
                while self.inflight:
                    self.tick()

        def h1_view(i):
            return H1[:, i, :].rearrange("p (a d) -> p a d", a=2)

        def rms_stats(i, sl):
            ssq = st4[:, 0, sl:sl + 1]
            vv = st4[:, 1, sl:sl + 1]
            rs = st4[:, 2, sl:sl + 1]
            OP("act", "activation", out=junk3, in_=H1[:, i, :], func=AF.Square, accum_out=ssq, r=[H1_R[i]], w=[st_R[sl], junk_R])
            OP("pool", "tensor_scalar", out=vv, in0=ssq, scalar1=1.0 / D, scalar2=EPS, op0=ALU.mult, op1=ALU.add, r=[st_R[sl]], w=[st_R[sl]])
            OP("pool", "tensor_tensor", out=rs, in0=vv, in1=mhalf1[:], op=ALU.pow, r=[st_R[sl], c_mhalf1], w=[st_R[sl]])
            return rs

        LD = Loader()
        for i in range(16):
            dma(H1[:, i, :], x_tiles[i], w=[H1_R[i]])
        LD.add(ffblock_pieces(0))

        pa_i = 0

        def outproj_mm(i):
            pa = i % 2
            for half in range(2):
                for k in range(8):
                    src = MX[:, k, 128 * i:128 * (i + 1)] if k < 4 else CVT[:, k - 4, 128 * i:128 * (i + 1)]
                    OP("pe", "matmul", PA[pa][:, half, :], lhsT=src, rhs=WO[:, k, half * 512:(half + 1) * 512], start=(k == 0), stop=(k == 7),
                       r=[MX_R[i], CV_R[i], WO_R[k]], w=[PA_R[pa][half]])
            OP("dve", "tensor_tensor", out=h1_view(i), in0=PA[pa][:], in1=h1_view(i), op=ALU.add, r=PA_R[pa] + [H1_R[i]], w=[H1_R[i]])
            sl = i % 4
            rms_stats(i, sl)

        def outproj_scale(i):
            sl = i % 4
            hb = i % 3
            rs = st4[:, 2, sl:sl + 1]
            OP("dve", "tensor_scalar", out=hn[:, hb, :], in0=H1[:, i, :], scalar1=rs, scalar2=None, op0=ALU.mult,
               r=[H1_R[i], st_R[sl]], w=[hn_R[hb]])

        def outproj_tr(i):
            hb = i % 3
            for k in range(8):
                OP("pe", "transpose", out=PT[:, k * 128:(k + 1) * 128], in_=hn[:, hb, k * 128:(k + 1) * 128], identity=ident[:],
                   r=[hn_R[hb], c_ident], w=[PT_R])
            OP("act", "activation", out=HN[:, :, 128 * i:128 * (i + 1)], in_=PT3, func=AF.Copy, r=[PT_R], w=[HN_R[i], MX_R[i]])

        outproj_mm(0)
        outproj_mm(1)
        outproj_scale(0)
        for i in range(16):
            if i + 2 < 16:
                outproj_mm(i + 2)
            if i + 1 < 16:
                outproj_scale(i + 1)
            outproj_tr(i)
            if i % 4 == 3:
                LD.tick()
        LD.drain()
        if debug:
            dma(dbg["h1"], H1.rearrange("p i d -> p (i d)"), r=H1_R)

        items = [(fb, tc) for fb in range(8) for tc in range(4)]
        out_ops = []

        def mlp_in(ii):
            fb, tc = items[ii]
            wb = fb % 2
            fbuf = ii % 2
            for ft in range(4):
                fpr = ft % 2
                for k in range(8):
                    OP("pe", "matmul", PR[fpr][:], lhsT=W1B[:, wb, k, ft * 128:(ft + 1) * 128], rhs=HN[:, k, 512 * tc:512 * (tc + 1)],
                       start=(k == 0), stop=(k == 7), r=[W1B_R[wb][k]] + HN_R[4 * tc:4 * tc + 4], w=[PR_R[fpr]])
                rb = ft % 2
                OP("act", "activation", out=rl[:, rb, :], in_=PR[fpr][:], func=AF.Relu, r=[PR_R[fpr]], w=[rl_R[rb]])
                OP("dve", "tensor_tensor", out=FF[:, fbuf, ft, :], in0=rl[:, rb, :], in1=rl[:, rb, :], op=ALU.mult,
                   r=[rl_R[rb]], w=[FF_R[fbuf][ft]])

        def mlp_out(ii):
            fb, tc = items[ii]
            wb = fb % 2
            fbuf = ii % 2
            for ti in range(4):
                i = 4 * tc + ti
                pa = ti % 2
                for half in range(2):
                    for ft in range(4):
                        OP("pe", "matmul", PA[pa][:, half, :], lhsT=FF[:, fbuf, ft, 128 * ti:128 * (ti + 1)],
                           rhs=W2B[:, wb, ft, half * 512:(half + 1) * 512], start=(ft == 0), stop=(ft == 3),
                           r=[FF_R[fbuf][ft], W2B_R[wb][ft]], w=[PA_R[pa][half]])
                OP("dve", "tensor_tensor", out=h1_view(i), in0=PA[pa][:], in1=h1_view(i), op=ALU.add, r=PA_R[pa] + [H1_R[i]], w=[H1_R[i]])
                if fb == 7:
                    sl = i % 4
                    rs = rms_stats(i, sl)

                    def fin(i=i, sl=sl, rs=rs):
                        OP("dve", "scalar_tensor_tensor", out=H1[:, i, :], in0=H1[:, i, :], scalar=rs, in1=gfin, op0=ALU.mult, op1=ALU.mult,
                           r=[H1_R[i], st_R[sl], gfin_R], w=[H1_R[i]])
                        out_ops.append(dma(y_out[128 * i:128 * (i + 1), :], H1[:, i, :], r=[H1_R[i]]))
                    finals.append(fin)
                    if len(finals) > 2:
                        finals.pop(0)()

        finals = []
        mlp_in(0)
        for ii in range(len(items)):
            fb, tc = items[ii]
            if tc == 0 and fb + 1 < 8:
                LD.add(ffblock_pieces(fb + 1))
            if ii + 1 < len(items):
                if items[ii + 1][1] == 0:
                    LD.drain()
                mlp_in(ii + 1)
            mlp_out(ii)
            LD.tick()
        for f in finals:
            f()
        S.add("sp", None, deps=out_ops + S.dma_since_barrier)
        S.emit(nc)
    return nc


def _core_tables(role):
    own = OWN_BLOCKS[role]
    oth = OWN_BLOCKS[1 - role]
    nat = own + oth
    pos = np.concatenate([np.arange(b * BL, (b + 1) * BL) for b in nat]).astype(np.float32)
    inv_freq = (np.float32(500000.0) ** (-np.arange(8, dtype=np.float32) * np.float32(2.0) / np.float32(16))).astype(np.float32)
    ang = (pos[:, None] * inv_freq[None, :]).astype(np.float32)
    cos = np.cos(ang).astype(np.float32)
    sin = np.sin(ang).astype(np.float32)
    cosT = np.ones((128, SEQ), np.float32)
    sinT = np.zeros((128, SEQ), np.float32)
    for p in range(128):
        d = p % 64
        if d < 8:
            cosT[p] = cos[:, d]
            sinT[p] = -sin[:, d]
        elif d < 16:
            cosT[p] = cos[:, d - 8]
            sinT[p] = sin[:, d - 8]
    gb = np.zeros((16, 16), np.float32)
    ownind = np.zeros((16, 16), np.float32)
    valid = np.zeros((16, 16), np.float32)
    for qt in range(16):
        j = qt // 2
        for n in range(16):
            if nat[n] < own[j]:
                valid[qt, n] = 1.0
            else:
                gb[qt, n] = -1e9
            if n == j:
                ownind[qt, n] = 1.0
    rep = lambda a: np.ascontiguousarray(np.broadcast_to(a.reshape(1, -1), (128, a.size))).astype(np.float32)
    return dict(nat=nat, own=own, cosT=cosT, sinT=sinT, gbias=rep(gb), ownind=rep(ownind), validf=rep(valid))


def _const_tables():
    ident = np.eye(128, dtype=np.float32).astype(ml_dtypes.bfloat16)
    perm = np.zeros((128, 128), np.float32)
    for m in range(128):
        d = m % 64
        if d < 8:
            perm[m + 8, m] = 1.0
        elif d < 16:
            perm[m - 8, m] = 1.0
    causal = np.zeros((128, 4, 512), np.float32)
    for kk in range(4):
        kp = 128 * kk + np.arange(128)[:, None]
        qi = np.arange(512)[None, :]
        causal[:, kk, :] = np.where(kp <= qi, 0.0, NEG)
    sel64 = np.zeros((128, 128), np.float32)
    sel64[64, :] = 1.0
    onehot = np.zeros((16, SEQ), np.float32)
    for n in range(16):
        onehot[n, n * BL:(n + 1) * BL] = 1.0
    return dict(ident=ident, perm=perm.astype(ml_dtypes.bfloat16),
                causal=causal.reshape(128, 2048).astype(ml_dtypes.bfloat16), sel64=sel64,
                onehot=onehot.astype(ml_dtypes.bfloat16))


_NC_CACHE = {}


def kernel(x, g_mix_norm, w_in, b_glu, w_dw, b_dw, g_conv_ln, b_conv_ln, w_out, g_mlp_norm, w_mlp_in, w_mlp_out, g_final,
           _debug=False):
    f = lambda a: np.ascontiguousarray(np.asarray(a, dtype=np.float32))
    x = f(x)
    w_in0, w_out0, w_m1, w_m2 = f(w_in)[0], f(w_out)[0], f(w_mlp_in)[0], f(w_mlp_out)[0]
    vecs = np.zeros((128, NVEC), np.float32)
    vecs[:, 0:8] = f(g_mix_norm)[0].reshape(8, 128).T
    vecs[:, 8:16] = f(g_mlp_norm)[0].reshape(8, 128).T
    bg = f(b_glu)[0]
    vecs[:, 16:20] = bg[0:512].reshape(4, 128).T
    vecs[:, 20:24] = bg[512:1024].reshape(4, 128).T
    vecs[:, 24:28] = f(b_dw)[0].reshape(4, 128).T
    vecs[:, 28:32] = f(g_conv_ln)[0].reshape(4, 128).T
    vecs[:, 32:36] = f(b_conv_ln)[0].reshape(4, 128).T
    wd = f(w_dw)[0, :, 0, :]
    vecs[:, 40:164] = wd.reshape(31, 4, 128).transpose(2, 1, 0).reshape(128, 124)
    gfin = np.ascontiguousarray(np.broadcast_to(f(g_final).reshape(1, D), (128, D)))
    consts = _const_tables()
    tabs = [_core_tables(0), _core_tables(1)]

    in_maps = []
    for core in range(8):
        b, role = core // 2, core % 2
        T = tabs[role]
        xb = x[b]
        x_loc = np.concatenate([xb[n * BL:(n + 1) * BL] for n in T["nat"]], axis=0)
        x_halo = np.zeros((4, 32, D), np.float32)
        v = vecs.copy()
        for c in range(4):
            start = T["own"][2 * c] * BL
            if start > 0:
                x_halo[c] = xb[start - 32:start]
                v[:, 36 + c] = 1.0
        in_maps.append(dict(x_loc=np.ascontiguousarray(x_loc), x_halo=x_halo, w_in=w_in0, w_out=w_out0, w_m1=w_m1, w_m2=w_m2,
                            cosT=T["cosT"], sinT=T["sinT"], vecs=v, gfin=gfin, gbias=T["gbias"], ownind=T["ownind"],
                            validf=T["validf"], **consts))

    key = bool(_debug)
    if key not in _NC_CACHE:
        _NC_CACHE[key] = build_program(debug=_debug)
    nc = _NC_CACHE[key]
    res = run_bass_kernel_spmd(nc, in_maps, core_ids=list(range(8)))
    out = np.zeros((4, SEQ, D), np.float32)
    for core in range(8):
        b, role = core // 2, core % 2
        y = res.results[core]["y"]
        for j, n in enumerate(tabs[role]["own"]):
            out[b, n * BL:(n + 1) * BL] = y[j * BL:(j + 1) * BL]
    if _debug:
        return out, res.results, tabs
    return out
```

```python
from contextlib import ExitStack

import numpy as np
import ml_dtypes

import concourse.bass as bass
import concourse.mybir as mybir
from concourse.bass_utils import run_bass_kernel_spmd

F32 = mybir.dt.float32
BF16 = mybir.dt.bfloat16
AF = mybir.ActivationFunctionType
ALU = mybir.AluOpType
AX = mybir.AxisListType

D = 1024
SEQ = 4096
NBLK = 16
BL = 256
NH = 8
EPS = 1e-6
OWN_BLOCKS = {0: [0, 1, 6, 7, 8, 9, 14, 15], 1: [2, 3, 4, 5, 10, 11, 12, 13]}
N_OTHER_BLOCKS = [2, 4, 6, 8]
NEG = -30000.0
VT = 520
NVEC = 164

ENGS = ("pe", "act", "dve", "pool", "sp")
N_DMA_SEMS = 40


class Region:
    __slots__ = ("writer", "readers")

    def __init__(self):
        self.writer = None
        self.readers = []


class Op:
    __slots__ = ("eng", "fn", "deps", "sig", "needs_sig", "is_dma", "sem", "val", "idx")

    def __init__(self, eng, fn, is_dma):
        self.eng = eng
        self.fn = fn
        self.deps = set()
        self.sig = None
        self.needs_sig = False
        self.is_dma = is_dma
        self.sem = None
        self.val = None


class Sched:
    def __init__(self):
        self.ops = {e: [] for e in ENGS}
        self.all = []
        self.dma_ops = []
        self.dma_since_barrier = []

    def add(self, eng, fn, r=(), w=(), deps=(), dma=False):
        op = Op(eng, fn, dma)
        op.idx = len(self.ops[eng])
        for d in deps:
            if d is not None:
                op.deps.add(d)
        for reg in r:
            if reg.writer is not None:
                op.deps.add(reg.writer)
            reg.readers.append(op)
        for reg in w:
            best = {}
            for rd in reg.readers:
                if rd.is_dma:
                    op.deps.add(rd)
                elif rd.eng not in best or rd.idx > best[rd.eng].idx:
                    best[rd.eng] = rd
            op.deps.update(best.values())
            reg.readers = []
            if reg.writer is not None:
                op.deps.add(reg.writer)
            reg.writer = op
        op.deps.discard(op)
        if dma:
            i = len(self.dma_ops)
            if i >= N_DMA_SEMS:
                op.deps.add(self.dma_ops[i - N_DMA_SEMS])
            op.sem = i % N_DMA_SEMS
            op.val = 16 * (i // N_DMA_SEMS + 1)
            self.dma_ops.append(op)
            self.dma_since_barrier.append(op)
        op.idx = len(self.ops[eng])
        self.ops[eng].append(op)
        self.all.append(op)
        return op

    def barrier(self):
        last = []
        for e in ENGS:
            real = [o for o in self.ops[e] if o.fn is not None]
            if real:
                last.append(real[-1])
        deps = last + self.dma_since_barrier[-4:]
        self.dma_since_barrier = []
        for e in ENGS:
            self.add(e, None, deps=deps)

    def finalize(self):
        for op in self.all:
            for d in op.deps:
                if d.is_dma:
                    continue
                if d.eng == "pe" and op.eng == "pe" and not op.is_dma:
                    continue
                d.needs_sig = True
        for e in ENGS:
            c = 0
            for op in self.ops[e]:
                if op.needs_sig and not op.is_dma:
                    c += 1
                    op.sig = c

    def emit(self, nc):
        self.finalize()
        with ExitStack() as st:
            esem = {e: st.enter_context(nc.semaphore("s_" + e)) for e in ENGS}
            dsem = [st.enter_context(nc.semaphore("d%d" % i)) for i in range(N_DMA_SEMS)]
            block = st.enter_context(nc.Block())

            def run(ename, eng):
                waited = {}
                for op in self.ops[ename]:
                    need = {}
                    for d in op.deps:
                        if d.is_dma:
                            key, sem, val = ("d", d.sem), dsem[d.sem], d.val
                        else:
                            if d.eng == "pe" and ename == "pe" and not op.is_dma:
                                continue
                            key, sem, val = ("e", d.eng), esem[d.eng], d.sig
                        if key not in need or need[key][1] < val:
                            need[key] = (sem, val)
                    for key in sorted(need, key=lambda k: (k[0], str(k[1]))):
                        sem, val = need[key]
                        if waited.get(key, 0) >= val:
                            continue
                        eng.wait_ge(sem, val)
                        waited[key] = val
                    if op.fn is None:
                        continue
                    name, a, kw = op.fn
                    ins = getattr(eng, name)(*a, **kw)
                    if op.is_dma:
                        ins.then_inc(dsem[op.sem], 16)
                    elif op.needs_sig:
                        ins.then_inc(esem[ename], 1)

            @block.tensor
            def _(eng):
                run("pe", eng)

            @block.scalar
            def _(eng):
                run("act", eng)

            @block.vector
            def _(eng):
                run("dve", eng)

            @block.gpsimd
            def _(eng):
                run("pool", eng)

            @block.sync
            def _(eng):
                run("sp", eng)


def build_program(debug=False):
    nc = bass.Bass("TRN2", target_bir_lowering=False)
    S = Sched()

    def OP(eng, name, *a, r=(), w=(), deps=(), **kw):
        return S.add(eng, (name, a, kw), r=r, w=w, deps=deps)

    def dma(out, in_, r=(), w=(), eng="sp", deps=()):
        return S.add(eng, ("dma_start", (), dict(out=out, in_=in_)), r=r, w=w, dma=True, deps=deps)

    def din(name, shape, dt=F32):
        return nc.dram_tensor(name, shape, dt, kind="ExternalInput").ap()

    x_loc = din("x_loc", [SEQ, D])
    x_halo = din("x_halo", [4, 32, D])
    w_in = din("w_in", [D, 2560])
    w_out = din("w_out", [D, D])
    w_m1 = din("w_m1", [D, 4096])
    w_m2 = din("w_m2", [4096, D])
    cosT_d = din("cosT", [128, SEQ])
    sinT_d = din("sinT", [128, SEQ])
    vecs_d = din("vecs", [128, NVEC])
    gfin_d = din("gfin", [128, D])
    gb_d = din("gbias", [128, 256])
    own_d = din("ownind", [128, 256])
    valid_d = din("validf", [128, 256])
    ident_d = din("ident", [128, 128], BF16)
    perm_d = din("perm", [128, 128], BF16)
    causal_d = din("causal", [128, 4 * 512], BF16)
    sel64_d = din("sel64", [128, 128])
    onehot_d = din("onehot", [16, SEQ], BF16)
    y_out = nc.dram_tensor("y", [2048, D], F32, kind="ExternalOutput").ap()
    dbg = {}
    if debug:
        dbg["kt"] = nc.dram_tensor("dbg_kt", [128, 4 * SEQ], BF16, kind="ExternalOutput").ap()
        dbg["q"] = nc.dram_tensor("dbg_q", [128, 4 * 2048], BF16, kind="ExternalOutput").ap()
        dbg["v"] = nc.dram_tensor("dbg_v", [128, 32 * VT + 64], BF16, kind="ExternalOutput").ap()
        dbg["mx"] = nc.dram_tensor("dbg_mx", [128, 8 * 2048], BF16, kind="ExternalOutput").ap()
        dbg["km"] = nc.dram_tensor("dbg_km", [128, 64], F32, kind="ExternalOutput").ap()
        dbg["h1"] = nc.dram_tensor("dbg_h1", [128, 16 * 1024], F32, kind="ExternalOutput").ap()

    with ExitStack() as st:
        def sb(name, shape, dt=F32):
            return st.enter_context(nc.sbuf_tensor("sb_" + name, shape, dt))

        def ps(name, shape, dt=F32):
            return st.enter_context(nc.psum_tensor("ps_" + name, shape, dt))

        R1 = sb("R1", [128, 16384], BF16)
        MX = R1[:].rearrange("p (k t) -> p k t", k=8)
        WQKV = R1[:, 0:12288].rearrange("p (k c) -> p k c", k=8)
        WB = R1[:, 0:8192].rearrange("p (k c) -> p k c", k=8)
        R2 = sb("R2", [128, 16384 + 32 * VT + 64], BF16)
        KT = R2[:, 0:16384].rearrange("p (a t) -> p a t", a=4)
        VF = R2[:, 16384:16384 + 32 * VT + 64]
        V4 = R2[:, 16384:16384 + 32 * VT].rearrange("p (t h e) -> p t h e", h=8, e=65)
        H1 = R2[:, 0:32768].bitcast(F32).rearrange("p (i d) -> p i d", i=16)
        Q = sb("Q", [128, 4, 2048], BF16)
        QPc = R1[:, 8192:16384].rearrange("p (a h t) -> p a h t", a=2, h=8)
        QP_R = [Region(), Region()]
        CVT = sb("CVT", [128, 4, 2048], BF16)
        ident = sb("ident", [128, 128], BF16)
        perm = sb("perm", [128, 128], BF16)
        sel64 = sb("sel64", [128, 128])
        onesf = sb("onesf", [128, 128])
        vecs = sb("vecs", [128, NVEC])
        kms = sb("kms", [128, 4, 16])
        kmhi = sb("kmhi", [128, 4, 16], BF16)
        kmlo = sb("kmlo", [128, 4, 16], BF16)
        kmf = sb("kmf", [128, 4, 16])
        st4 = sb("st4", [128, 3, 4])
        mhalf1 = sb("mhalf1", [128, 1])
        ARENA_BYTES = 77824
        ARENA = sb("ARENA", [128, ARENA_BYTES // 2], BF16)

        g_mix = vecs[:, 0:8]
        g_mlp = vecs[:, 8:16]
        b_a = vecs[:, 16:20]
        b_g = vecs[:, 20:24]
        b_dw = vecs[:, 24:28]
        g_ln = vecs[:, 28:32]
        b_ln = vecs[:, 32:36]
        halo_mask = vecs[:, 36:40]
        w_dw = vecs[:, 40:164].rearrange("p (c k) -> p c k", c=4)

        class Carver:
            def __init__(self):
                self.off = 0

            def get(self, shape, dt):
                n = 1
                for s_ in shape[1:]:
                    n *= s_
                nbytes = n * (2 if dt == BF16 else 4)
                off = self.off
                self.off += (nbytes + 63) // 64 * 64
                assert self.off <= ARENA_BYTES, (self.off, ARENA_BYTES)
                if dt == BF16:
                    ap = ARENA[:, off // 2: off // 2 + n]
                else:
                    ap = ARENA[:, off // 2: off // 2 + 2 * n].bitcast(F32)
                if len(shape) == 2:
                    return ap
                names = " ".join("d%d" % i for i in range(len(shape) - 1))
                kw = {"d%d" % i: shape[i + 1] for i in range(len(shape) - 2)}
                return ap.rearrange("p (%s) -> p %s" % (names, names), **kw)

        PA = [ps("PA%d" % i, [128, 2, 512]) for i in range(2)]
        PR = [ps("PR%d" % i, [128, 512]) for i in range(2)]
        PH = ps("PH", [128, 512])
        PTF = ps("PT", [128, 512])
        PT = PTF[:].bitcast(BF16)
        PA_R = [[Region(), Region()] for _ in range(2)]
        PR_R = [Region(), Region()]
        PH_R = Region()
        PT_R = Region()
        pa_banks = [(PA[i][:, j, :], PA_R[i][j]) for i in range(2) for j in range(2)]
        PT3 = PT.rearrange("p (k t) -> p k t", k=8)

        c_ident, c_perm, c_sel64, c_onesf, c_vecs, c_mhalf1 = [Region() for _ in range(6)]
        dma(ident[:], ident_d, w=[c_ident])
        dma(vecs[:], vecs_d, w=[c_vecs])
        dma(perm[:], perm_d, w=[c_perm])
        dma(sel64[:], sel64_d, w=[c_sel64])
        OP("pool", "memset", onesf[:], 1.0, w=[c_onesf])
        OP("pool", "memset", QPc[:, 1].rearrange("p h t -> p (h t)"), 0.0, w=[QP_R[1]])
        OP("pool", "memset", mhalf1[:], -0.5, w=[c_mhalf1])
        kms_R = Region()
        OP("pool", "memset", kms[:], 0.0, w=[kms_R])

        cv = Carver()
        xnT = cv.get([128, 2, 8, 544], BF16)
        xst = cv.get([128, 3, 1024], F32)
        xh = cv.get([128, 1024], F32)
        xn = cv.get([128, 2, 1024], BF16)
        off_cs2 = cv.off
        cs2 = cv.get([128, 2, 2, 512], F32)
        t12 = cv.get([128, 1024], F32)
        t1 = t12[:, 0:512]
        t2 = t12[:, 512:1024]
        off_qraw = cv.off
        qraw = cv.get([128, 2, 512], BF16)
        hT = cv.get([128, 4, 544], BF16)
        dg = cv.get([128, 2, 16, 128], BF16)
        junk = dg[:, 0, 0:8, :].rearrange("p a t -> p (a t)")
        w_dw_bf = cv.get([128, 4, 32], BF16)
        yb = cv.get([128, 4, 512], F32)
        lnmv = cv.get([128, 1024], F32)
        lnm = lnmv[:, 0:512]
        lnv = lnmv[:, 512:1024]
        sigh = cv.get([128, 32], F32)
        print("pass A/B arena bytes", cv.off)

        xnT_R = [[Region() for _ in range(5)] for _ in range(2)]
        xst_R = [Region() for _ in range(3)]
        xh_R = Region()
        junk_R = Region()
        xn_R = [Region(), Region()]
        st_R = [Region() for _ in range(4)]
        cs_R2 = [Region(), Region()]
        t1_R = Region()
        t2_R = Region()
        qraw_R = [Region(), Region()]
        hT_R = [Region() for _ in range(4)]
        dg_R = [Region(), Region()]
        wdb_R = Region()
        y_R = [Region() for _ in range(4)]
        lnm_R = Region()
        lnv_R = Region()
        sigh_R = Region()
        W_R = [Region() for _ in range(8)]
        WQ_R = [Region() for _ in range(8)]
        WK_R = [Region() for _ in range(8)]
        WV_R = [Region() for _ in range(8)]
        KT_R = [[Region() for _ in range(8)] for _ in range(4)]
        Q_R = [[Region() for _ in range(4)] for _ in range(4)]
        V_R = [Region() for _ in range(32)]
        MX_R = [Region() for _ in range(16)]
        CV_R = [Region() for _ in range(16)]
        vones_R = Region()
        OP("pool", "memset", VF[:, 0:32 * VT].rearrange("p (n e) -> p n e", e=65)[:, :, 64:65], 1.0, w=[vones_R])
        OP("pool", "memset", VF[:, 32 * VT:32 * VT + 64], 0.0, w=[vones_R])

        state = {"tile": 0, "pa": 0, "pr": 0, "qraw": 0, "evac": 0, "ld": 0, "cast": 0, "junk": (junk, [dg_R[0]])}

        def next_pa():
            i = state["pa"] % 4
            state["pa"] += 1
            return pa_banks[i]

        def next_pr():
            i = state["pr"] % 2
            state["pr"] += 1
            return PR[i][:], PR_R[i]

        w_in_v = w_in.rearrange("(k p) c -> p k c", p=128)

        def load_w_in(dst, c0, ncols, wregs_of, stg, n_now=8):
            ns = len(stg)

            def issue(kt):
                st_ap, st_regs = stg[kt % ns]
                dma(st_ap, w_in_v[:, kt, c0:c0 + ncols], w=st_regs)
            for kt in range(min(ns, 8, n_now)):
                issue(kt)

            def finish(deps=()):
                for kt in range(8):
                    st_ap, st_regs = stg[kt % ns]
                    eng = ("dve", "act")[kt % 2]
                    if eng == "act":
                        OP("act", "activation", out=dst[:, kt, :], in_=st_ap, func=AF.Copy, scale=g_mix[:, kt:kt + 1],
                           r=st_regs + [c_vecs], w=wregs_of(kt), deps=deps)
                    else:
                        OP(eng, "tensor_scalar", out=dst[:, kt, :], in0=st_ap, scalar1=g_mix[:, kt:kt + 1], scalar2=None,
                           op0=ALU.mult, r=st_regs + [c_vecs], w=wregs_of(kt), deps=deps)
                    if kt + ns < 8:
                        issue(kt + ns)
            finish.more = lambda: [issue(kt) for kt in range(min(ns, 8, n_now), min(ns, 8))]
            return finish

        def norm_tile(src, src_R, npart, dst_cols, dst_R):
            j = state["tile"]
            state["tile"] += 1
            sl = j % 4
            xb = j % 2
            ssq = st4[0:npart, 0, sl:sl + 1]
            vv = st4[0:npart, 1, sl:sl + 1]
            rs = st4[0:npart, 2, sl:sl + 1]
            jk, jk_regs = state["junk"]
            OP("act", "activation", out=jk[0:npart, :], in_=src, func=AF.Square, accum_out=ssq, r=[src_R], w=[st_R[sl]] + jk_regs)
            OP("pool", "tensor_scalar", out=vv, in0=ssq, scalar1=1.0 / D, scalar2=EPS, op0=ALU.mult, op1=ALU.add,
               r=[st_R[sl]], w=[st_R[sl]])
            OP("pool", "tensor_tensor", out=rs, in0=vv, in1=mhalf1[0:npart, :], op=ALU.pow, r=[st_R[sl], c_mhalf1], w=[st_R[sl]])

            def part_a2():
                if state.get("scale_on_pool"):
                    OP("pool", "tensor_scalar", out=xn[0:npart, xb, :], in0=src, scalar1=rs, scalar2=1.0, op0=ALU.mult, op1=ALU.mult,
                       r=[src_R, st_R[sl]], w=[xn_R[xb]])
                else:
                    OP("dve", "tensor_scalar", out=xn[0:npart, xb, :], in0=src, scalar1=rs, scalar2=None, op0=ALU.mult,
                       r=[src_R, st_R[sl]], w=[xn_R[xb]])
                return part_b

            def part_b():
                for k in range(8):
                    OP("pe", "transpose", out=PT[:, k * 128:k * 128 + npart], in_=xn[0:npart, xb, k * 128:(k + 1) * 128],
                       identity=ident[0:npart, 0:npart], r=[xn_R[xb], c_ident], w=[PT_R])
                src_ps = PT3[:, :, 0:npart]
                if state["evac"] % 2 == 0:
                    OP("act", "activation", out=dst_cols, in_=src_ps, func=AF.Copy, r=[PT_R], w=[dst_R])
                else:
                    OP("dve", "tensor_copy", out=dst_cols, in_=src_ps, r=[PT_R], w=[dst_R])
                state["evac"] += 1
            return part_a2

        def rope_part1(ps_ap, ps_R):
            qb = state["qraw"] % 2
            state["qraw"] += 1
            OP("act", "activation", out=qraw[:, qb, :], in_=ps_ap, func=AF.Copy, r=[ps_R], w=[qraw_R[qb]])
            return qb

        def rope_part2(ps_ap, ps_R, qb, is_k, p, c, col0, csb):
            cs = cs2[:, csb]
            cs_R = cs_R2[csb]
            pr_ap, pr_R = next_pr()
            OP("pe", "matmul", pr_ap, lhsT=perm[:], rhs=qraw[:, qb, :], start=True, stop=True, r=[qraw_R[qb], c_perm], w=[pr_R])
            OP("dve", "tensor_tensor", out=t1, in0=pr_ap, in1=cs[:, 1, :], op=ALU.mult, r=[pr_R, cs_R], w=[t1_R])
            OP("dve", "tensor_tensor", out=t2, in0=ps_ap, in1=cs[:, 0, :], op=ALU.mult, r=[ps_R, cs_R], w=[t2_R])
            if is_k:
                for bb in range(2):
                    n = 2 * c + bb
                    OP("dve", "scalar_tensor_tensor", out=KT[:, p, col0 + bb * 256: col0 + (bb + 1) * 256],
                       in0=t1[:, bb * 256:(bb + 1) * 256], scalar=1.0, in1=t2[:, bb * 256:(bb + 1) * 256],
                       op0=ALU.mult, op1=ALU.add, accum_out=kms[:, p, n:n + 1], r=[t1_R, t2_R], w=[KT_R[p][c], kms_R])
            else:
                OP("dve", "tensor_tensor", out=Q[:, p, col0:col0 + 512], in0=t1, in1=t2, op=ALU.add, r=[t1_R, t2_R], w=[Q_R[p][c]])

        x_tiles = x_loc.rearrange("(n p) d -> n p d", p=128)

        def norm_pieces(c, cb, with_halo):
            pieces = []

            def mk(i):
                def f():
                    if state.get("first_chunk") and i == 3:
                        dma(xh[:], x_tiles[4 * c + i], w=[xh_R])
                        return norm_tile(xh[:], xh_R, 128, xnT[:, cb, :, 32 + 128 * i: 32 + 128 * (i + 1)], xnT_R[cb][i])
                    b = state["ld"] % 3
                    state["ld"] += 1
                    dma(xst[:, b, :], x_tiles[4 * c + i], w=[xst_R[b]])
                    return norm_tile(xst[:, b, :], xst_R[b], 128, xnT[:, cb, :, 32 + 128 * i: 32 + 128 * (i + 1)], xnT_R[cb][i])
                return f

            def halo():
                dma(xh[0:32, :], x_halo[c], w=[xh_R])
                return norm_tile(xh[0:32, :], xh_R, 32, xnT[:, cb, :, 0:32], xnT_R[cb][4])
            if with_halo:
                pieces.append(halo)
            for i in range(4):
                pieces.append(mk(i))
            return pieces

        class Pipe3:
            def __init__(self):
                self.s2 = []
                self.s3 = []

            def tick(self, pending):
                if self.s3:
                    self.s3.pop(0)()
                if self.s2:
                    self.s3.append(self.s2.pop(0)())
                if pending:
                    self.s2.append(pending.pop(0)())

            def drain(self, pending):
                while pending or self.s2 or self.s3:
                    self.tick(pending)

        state["scale_on_pool"] = False
        state["first_chunk"] = True
        stgA = [(R2[:, i * 1024:(i + 1) * 1024].bitcast(F32), [Region()]) for i in range(16)]
        pieces0 = norm_pieces(0, 0, False)
        pipe0 = Pipe3()
        pipe0.tick(pieces0)
        pipe0.tick(pieces0)
        fin_k = load_w_in(WQKV[:, :, 512:1024], 512, 512, lambda kt: [WK_R[kt]], stgA[0:8])
        pipe0.tick(pieces0)
        pipe0.tick(pieces0)
        fin_q = load_w_in(WQKV[:, :, 0:512], 0, 512, lambda kt: [WQ_R[kt]], stgA[8:16])
        fin_k()
        fin_v = load_w_in(WQKV[:, :, 1024:1536], 1024, 512, lambda kt: [WV_R[kt]], stgA[0:8])
        pipe0.drain(pieces0)
        state["scale_on_pool"] = True
        state["first_chunk"] = False
        fin_q()
        fin_v()
        for c in range(8):
            own = c < 4
            cb = c % 2
            col0 = 512 * c
            csb = c % 2
            dma(cs2[:, csb, 0, :], cosT_d[:, col0:col0 + 512], w=[cs_R2[csb]])
            dma(cs2[:, csb, 1, :], sinT_d[:, col0:col0 + 512], w=[cs_R2[csb]])
            if c + 1 < 8:
                pending = norm_pieces(c + 1, (c + 1) % 2, False)
            else:
                pending = norm_pieces(0, 0, True)
            pipeA = Pipe3()
            if c == 6:
                stgB = [(yb[:, 0:2, :].rearrange("p a t -> p (a t)"), [y_R[0], y_R[1]]),
                        (yb[:, 2:4, :].rearrange("p a t -> p (a t)"), [y_R[2], y_R[3]]),
                        (dg[:, 1].rearrange("p a t -> p (a t)").bitcast(F32), [dg_R[1]]),
                        (lnmv, [lnm_R, lnv_R])]
                stgB += [(CVT[:, ct_, :].bitcast(F32), [Region()]) for ct_ in range(4)]
                finish_wb = load_w_in(WB, 1536, 1024, lambda kt: [W_R[kt]], stgB, n_now=4)
            if c == 7:
                finish_wb.more()
            xr_main = xnT_R[cb][0:4]
            jobs = []
            if c == 0:
                jobs = [("k", p) for p in range(4)] + [("q", p) for p in range(4)] + [("v", p) for p in range(4)]
            else:
                for p in range(4):
                    jobs.append(("k", p))
                    if own:
                        jobs.append(("q", p))
                    jobs.append(("v", p))
            prev = None
            for ji, (kind, idx) in enumerate(jobs):
                pa_ap, pa_R = next_pa()
                if kind == "v":
                    i = idx
                    ktile = 4 * c + i
                    for k in range(8):
                        OP("pe", "matmul", pa_ap, lhsT=xnT[:, cb, k, 32 + 128 * i: 32 + 128 * (i + 1)], rhs=WQKV[:, k, 1024:1536],
                           start=(k == 0), stop=(k == 7), r=[WV_R[k], xnT_R[cb][i]], w=[pa_R])
                    src = pa_ap.rearrange("p (h d) -> p h d", h=8)
                    dst = V4[:, ktile, :, 0:64]
                    OP("act", "activation", out=dst, in_=src, func=AF.Copy, r=[pa_R], w=[V_R[ktile]])
                    cur = None
                else:
                    p = idx
                    cb0 = 512 if kind == "k" else 0
                    wr_ = WK_R if kind == "k" else WQ_R
                    for k in range(8):
                        OP("pe", "matmul", pa_ap, lhsT=WQKV[:, k, cb0 + 128 * p: cb0 + 128 * (p + 1)], rhs=xnT[:, cb, k, 32:544],
                           start=(k == 0), stop=(k == 7), r=[wr_[k]] + xr_main, w=[pa_R])
                    qb = rope_part1(pa_ap, pa_R)
                    cur = (pa_ap, pa_R, qb, kind == "k", p, c, col0, csb)
                if prev is not None:
                    rope_part2(*prev)
                prev = cur
                period = 3 if own else 2
                if ji >= 1:
                    pipeA.tick(pending)
            if prev is not None:
                rope_part2(*prev)
            pipeA.drain(pending)

        km_R = Region()
        OP("dve", "tensor_scalar", out=kmf[:], in0=kms[:], scalar1=1.0 / BL, scalar2=None, op0=ALU.mult, r=[kms_R], w=[km_R])
        OP("dve", "tensor_copy", out=kmhi[:], in_=kmf[:], r=[km_R], w=[km_R])
        OP("dve", "tensor_tensor", out=kmlo[:], in0=kmf[:], in1=kmhi[:], op=ALU.subtract, r=[km_R], w=[km_R])
        if debug:
            dma(dbg["kt"], R2[:, 0:16384], r=[KT_R[p][c] for p in range(4) for c in range(8)])
            dma(dbg["q"], Q[:].rearrange("p a t -> p (a t)"), r=[Q_R[p][c] for p in range(4) for c in range(4)])
            dma(dbg["v"], VF, r=V_R + [vones_R])
            dma(dbg["km"], kmf[:].rearrange("p a n -> p (a n)"), r=[km_R])


        class _CarveAt(Carver):
            def __init__(self, off):
                self.off = off
        cvA = _CarveAt(off_cs2)
        scA = dict(biasp=cvA.get([128, 4, 8, 80], BF16), gbias=cvA.get([128, 4, 16], F32), ownind=cvA.get([128, 4, 16], F32),
                   validf=cvA.get([128, 4, 16], F32), km128=cvA.get([128, 2, 8, 16], BF16), gm=cvA.get([128, 8, 16], F32),
                   top=cvA.get([128, 8, 8], F32), selm=cvA.get([128, 8, 16], F32))
        assert cvA.off <= off_cs2 + 8192, cvA.off
        scA["gate4"] = _CarveAt(off_qraw).get([128, 512], F32)
        for nm_ in ("gate4_R", "gm_R", "top_R", "selm_R", "biasp_R", "km_R", "tab_R"):
            scA[nm_] = Region()

        def p2_init_A():
            allA = [scA[n_] for n_ in ("gm_R", "top_R", "selm_R", "biasp_R", "km_R", "tab_R")]
            OP("pool", "memset", cs2.rearrange("p a b t -> p (a b t)"), 0.0, w=[cs_R2[0], cs_R2[1]] + allA)
            OP("pool", "memset", scA["gate4"], 0.0, w=[scA["gate4_R"], qraw_R[0], qraw_R[1]])
            dma(scA["gbias"].rearrange("p a t -> p (a t)"), gb_d[:, 0:64], w=[scA["tab_R"]])
            dma(scA["ownind"].rearrange("p a t -> p (a t)"), own_d[:, 0:64], w=[scA["tab_R"]])
            dma(scA["validf"].rearrange("p a t -> p (a t)"), valid_d[:, 0:64], w=[scA["tab_R"]])
            km4a = scA["km128"][0:64].rearrange("p a (b two) n -> p a b two n", two=2)
            for a_, kmx in enumerate((kmhi, kmlo)):
                OP("dve", "tensor_copy", out=km4a[:, a_, :, 0, :], in_=kmx[0:64, :, :], r=[km_R], w=[scA["km_R"]])
                OP("dve", "tensor_copy", out=km4a[:, a_, :, 1, :], in_=kmx[64:128, :, :], r=[km_R], w=[scA["km_R"]])

        def prep_pieces(c, sc):
            qb = (c + 1) % 2
            col0 = 512 * c
            gate4, gm, top, selm, biasp, km128 = sc["gate4"], sc["gm"], sc["top"], sc["selm"], sc["biasp"], sc["km128"]
            gbias, validf, ownind = sc["gbias"], sc["validf"], sc["ownind"]
            gate4_R, gm_R, top_R, selm_R, biasp_R, km64_R, tab_R = (sc["gate4_R"], sc["gm_R"], sc["top_R"], sc["selm_R"],
                                                                     sc["biasp_R"], sc["km_R"], sc["tab_R"])
            qp4 = QPc[:, qb].rearrange("p (a two) t -> p a two t", two=2)

            def p_q1():
                OP("dve", "tensor_copy", out=qp4[0:64, :, 0, :], in_=Q[0:64, :, col0:col0 + 512], r=[Q_R[p][c] for p in range(4)], w=[QP_R[qb]])
                OP("dve", "tensor_copy", out=qp4[0:64, :, 1, :], in_=Q[64:128, :, col0:col0 + 512], r=[Q_R[p][c] for p in range(4)], w=[QP_R[qb]])

            def p_q():
                for i in range(4):
                    for h in range(NH):
                        for a_ in range(2):
                            OP("pe", "matmul", PTF[:, i * 128 + h * 16: i * 128 + (h + 1) * 16], lhsT=QPc[0:80, qb, h, i * 128:(i + 1) * 128],
                               rhs=km128[0:80, a_, h, :], start=(a_ == 0), stop=(a_ == 1), r=[QP_R[qb], km64_R], w=[PT_R])
                OP("dve", "tensor_copy", out=gate4, in_=PTF[:], r=[PT_R], w=[gate4_R])

            def p_sel(i):
                def f():
                    qt = 4 * c + i
                    OP("dve", "tensor_tensor", out=gm, in0=gate4[:, i * 128:(i + 1) * 128].rearrange("p (h n) -> p h n", h=8),
                       in1=gbias[:, qt, :].unsqueeze(1).to_broadcast([128, 8, 16]), op=ALU.add, r=[gate4_R, tab_R], w=[gm_R])
                    for h in range(NH):
                        OP("dve", "max", out=top[:, h, :], in_=gm[:, h, :], r=[gm_R], w=[top_R])
                    OP("dve", "tensor_tensor", out=selm, in0=gm, in1=top[:, :, 2:3].to_broadcast([128, 8, 16]), op=ALU.is_ge,
                       r=[gm_R, top_R], w=[selm_R])
                    OP("dve", "tensor_tensor", out=selm, in0=selm, in1=validf[:, qt, :].unsqueeze(1).to_broadcast([128, 8, 16]), op=ALU.mult,
                       r=[selm_R, tab_R], w=[selm_R])
                    OP("dve", "tensor_tensor", out=selm, in0=selm, in1=ownind[:, qt, :].unsqueeze(1).to_broadcast([128, 8, 16]), op=ALU.add,
                       r=[selm_R, tab_R], w=[selm_R])
                    OP("dve", "tensor_scalar", out=biasp[:, i, :, 64:80], in0=selm, scalar1=-NEG, scalar2=NEG,
                       op0=ALU.mult, op1=ALU.add, r=[selm_R], w=[biasp_R])
                return f

            def p_bias(h):
                def f():
                    for i in range(4):
                        OP("pe", "matmul", PTF[0:80, i * 128:(i + 1) * 128], lhsT=biasp[:, i, h, :], rhs=ident[:], start=True, stop=True,
                           r=[biasp_R, c_ident], w=[PT_R])
                    OP("dve", "tensor_copy", out=QPc[64:80, qb, h, :], in_=PTF[64:80, :], r=[PT_R], w=[QP_R[qb]])
                return f
            return [p_q1, p_q] + [p_sel(i) for i in range(4)] + [p_bias(h) for h in range(NH)]

        cvK = Carver()
        KA = cvK.get([128, 3, 4096], BF16)
        causal = cvK.get([128, 4, 512], BF16)
        assert cvK.off <= 33792
        KA_R = [Region() for _ in range(3)]
        KAoh_R = [Region() for _ in range(3)]
        causal_R = Region()

        def build_ka(c, h, deps=()):
            buf = (NH * c + h) % 3
            p, eo = h // 2, h % 2
            nk = 512 * (c + 1)
            for k0 in (0, 2048):
                OP("dve", "tensor_copy", out=KA[0:64, buf, k0:k0 + nk], in_=KT[eo * 64:(eo + 1) * 64, p, k0:k0 + nk],
                   r=[KT_R[p][cc] for cc in range(8)], w=[KA_R[buf]], deps=deps)

        fence = OP("dve", "tensor_copy", out=w_dw_bf[:, :, 0:31], in_=w_dw, r=[c_vecs], w=[wdb_R] + WQ_R + WK_R + WV_R)
        state["junk"] = (qraw.rearrange("p a t -> p (a t)"), [qraw_R[0], qraw_R[1]])
        state["scale_on_pool"] = False
        finish_wb(deps=[fence])
        mh512 = mhalf1[:].to_broadcast([128, 512])
        def ln_stats_groups(c, banks=None):
            if banks is None:
                s_ap, s_R = PR[0][:], PR_R[0]
                q_ap, q_R = PR[1][:], PR_R[1]
            else:
                (s_ap, s_R), (q_ap, q_R) = banks

            def mm_s(ct):
                return lambda: OP("pe", "matmul", s_ap, lhsT=onesf[:], rhs=yb[:, ct, :], start=(ct == 0), stop=(ct == 3),
                                  r=[y_R[ct], c_onesf], w=[s_R])

            def sq(ct):
                return lambda: OP("act", "activation", out=t2, in_=yb[:, ct, :], func=AF.Square, r=[y_R[ct]], w=[t2_R])

            def mm_q(ct):
                return lambda: OP("pe", "matmul", q_ap, lhsT=onesf[:], rhs=t2, start=(ct == 0), stop=(ct == 3), r=[t2_R, c_onesf], w=[q_R])
            if banks is not None:
                per_ct = [[mm_s(ct), sq(ct), mm_q(ct)] for ct in range(4)]
                fin = [lambda: OP("dve", "tensor_scalar", out=lnm, in0=s_ap, scalar1=1.0 / 512, scalar2=None, op0=ALU.mult, r=[s_R], w=[lnm_R]),
                       lambda: OP("dve", "tensor_tensor", out=lnv, in0=lnm, in1=lnm, op=ALU.mult, r=[lnm_R], w=[lnv_R]),
                       lambda: OP("dve", "scalar_tensor_tensor", out=lnv, in0=q_ap, scalar=1.0 / 512, in1=lnv, op0=ALU.mult,
                                  op1=ALU.subtract, r=[q_R, lnv_R], w=[lnv_R]),
                       lambda: OP("dve", "tensor_scalar", out=lnv, in0=lnv, scalar1=1.0, scalar2=EPS, op0=ALU.mult, op1=ALU.add,
                                  r=[lnv_R], w=[lnv_R]),
                       lambda: OP("act", "activation", out=lnv, in_=lnv, func=AF.Sqrt, r=[lnv_R], w=[lnv_R]),
                       lambda: OP("dve", "reciprocal", out=lnv, in_=lnv, r=[lnv_R], w=[lnv_R])]
                return per_ct, fin
            g = []
            g.append([mm_s(0), mm_s(1), mm_s(2), mm_s(3), sq(0)])
            g.append([mm_q(0), sq(1)])
            g.append([mm_q(1), sq(2)])
            g.append([mm_q(2), sq(3)])
            g.append([mm_q(3),
                      lambda: OP("dve", "tensor_scalar", out=lnm, in0=s_ap, scalar1=1.0 / 512, scalar2=None, op0=ALU.mult, r=[s_R], w=[lnm_R]),
                      lambda: OP("dve", "tensor_tensor", out=lnv, in0=lnm, in1=lnm, op=ALU.mult, r=[lnm_R], w=[lnv_R])])
            g.append([lambda: OP("dve", "scalar_tensor_tensor", out=lnv, in0=q_ap, scalar=1.0 / 512, in1=lnv, op0=ALU.mult, op1=ALU.subtract,
                                 r=[q_R, lnv_R], w=[lnv_R]),
                      lambda: OP("dve", "tensor_scalar", out=lnv, in0=lnv, scalar1=1.0, scalar2=EPS, op0=ALU.mult, op1=ALU.add,
                                 r=[lnv_R], w=[lnv_R])])
            g.append([lambda: OP("act", "activation", out=lnv, in_=lnv, func=AF.Sqrt, r=[lnv_R], w=[lnv_R])])
            for _ in range(2):
                g.append([])
            g.append([lambda: OP("dve", "reciprocal", out=lnv, in_=lnv, r=[lnv_R], w=[lnv_R])])
            return g

        def ln_norm_ops(c, col0):
            ops = []
            A = ops.append
            for ct in range(4):
                zb, zb_R = ((t2, t2_R), (xh[:, 0:512], xh_R))[ct % 2]
                A(lambda ct=ct, zb=zb, zb_R=zb_R: OP("dve", "tensor_tensor", out=zb, in0=yb[:, ct, :], in1=lnm, op=ALU.subtract,
                                                     r=[y_R[ct], lnm_R], w=[zb_R]))
                A(lambda zb=zb, zb_R=zb_R: OP("dve", "tensor_tensor", out=zb, in0=zb, in1=lnv, op=ALU.mult, r=[zb_R, lnv_R], w=[zb_R]))
                A(lambda ct=ct, zb=zb, zb_R=zb_R: OP("act", "activation", out=CVT[:, ct, col0:col0 + 512], in_=zb, func=AF.Silu,
                                                     bias=b_ln[:, ct:ct + 1], scale=g_ln[:, ct:ct + 1], r=[zb_R, c_vecs],
                                                     w=CV_R[4 * c:4 * c + 4]))
            return ops

        pending_stats = []

        def pop_stats():
            if pending_stats:
                for f in pending_stats.pop(0):
                    f()

        pending_ln = []
        for c in range(4):
            cb = c % 2
            col0 = 512 * c
            pendingB = norm_pieces(c + 1, (c + 1) % 2, True) if c + 1 < 4 else []
            pipeB = Pipe3()
            halves = [(ct_, hf_, k0_, nk_) for ct_ in range(4) for hf_, (k0_, nk_) in enumerate(((0, 16), (16, 15)))]

            def build_diag(ix):
                ct_, hf_, k0_, nk_ = halves[ix]
                OP("dve", "tensor_tensor", out=dg[:, hf_, 0:nk_, :], in0=ident[:].unsqueeze(1).to_broadcast([128, nk_, 128]),
                   in1=w_dw_bf[:, ct_, k0_:k0_ + nk_].unsqueeze(2).to_broadcast([128, nk_, 128]), op=ALU.mult,
                   r=[c_ident, wdb_R], w=[dg_R[hf_]])
            build_diag(0)
            build_diag(1)
            xr_main = xnT_R[cb][0:4]
            for ct in range(4):
                pipeB.tick(pendingB)
                a_ap, a_R = next_pa()
                g_ap, g_R = next_pa()
                for (dst_ap, dst_R, cbase) in ((a_ap, a_R, 0), (g_ap, g_R, 512)):
                    for k in range(8):
                        OP("pe", "matmul", dst_ap, lhsT=WB[:, k, cbase + 128 * ct: cbase + 128 * (ct + 1)], rhs=xnT[:, cb, k, 32:544],
                           start=(k == 0), stop=(k == 7), r=[W_R[k]] + xr_main, w=[dst_R])
                    pop_stats()
                for (off, cbase) in ((0, 0), (32, 512)):
                    for k in range(8):
                        OP("pe", "matmul", PH[:, off:off + 32], lhsT=WB[:, k, cbase + 128 * ct: cbase + 128 * (ct + 1)],
                           rhs=xnT[:, cb, k, 0:32], start=(k == 0), stop=(k == 7), r=[W_R[k], xnT_R[cb][4]], w=[PH_R])
                pop_stats()
                pipeB.tick(pendingB)
                OP("act", "activation", out=t1, in_=g_ap, func=AF.Sigmoid, bias=b_g[:, ct:ct + 1], r=[g_R, c_vecs], w=[t1_R])
                OP("dve", "scalar_tensor_tensor", out=hT[:, ct, 32:544], in0=a_ap, scalar=b_a[:, ct:ct + 1], in1=t1,
                   op0=ALU.add, op1=ALU.mult, r=[a_R, t1_R, c_vecs], w=[hT_R[ct]])
                OP("act", "activation", out=sigh, in_=PH[:, 32:64], func=AF.Sigmoid, bias=b_g[:, ct:ct + 1], r=[PH_R, c_vecs], w=[sigh_R])
                OP("dve", "scalar_tensor_tensor", out=hT[:, ct, 0:32], in0=PH[:, 0:32], scalar=b_a[:, ct:ct + 1], in1=sigh,
                   op0=ALU.add, op1=ALU.mult, r=[PH_R, sigh_R, c_vecs], w=[hT_R[ct]])
                OP("dve", "tensor_scalar", out=hT[:, ct, 0:32], in0=hT[:, ct, 0:32], scalar1=halo_mask[:, c:c + 1], scalar2=None,
                   op0=ALU.mult, r=[hT_R[ct], c_vecs], w=[hT_R[ct]])
            pipeB.drain(pendingB)
            while pending_stats:
                pop_stats()
            early = ([p2_init_A] + prep_pieces(0, scA)) if c == 3 else []
            own_q = []
            if c == 3:
                last_per_ct, last_fin = ln_stats_groups(c, banks=(next_pa(), next_pa()))
            tapn = 0
            for ix, (ct, hf, k0, nk) in enumerate(halves):
                acc_ap, acc_R = PR[ct % 2][:], PR_R[ct % 2]
                for kk in range(nk):
                    k = k0 + kk
                    OP("pe", "matmul", acc_ap, lhsT=dg[:, hf, kk, :], rhs=hT[:, ct, 2 + k:514 + k], start=(k == 0), stop=(k == 30),
                       r=[dg_R[hf], hT_R[ct]], w=[acc_R])
                    if pending_ln and k % 2 == 1:
                        pending_ln.pop(0)()
                    tapn += 1
                    if early and tapn % 8 == 0:
                        early.pop(0)()
                    if own_q and tapn % 4 == 2:
                        own_q.pop(0)()
                if ix + 2 < len(halves):
                    build_diag(ix + 2)
                if hf == 1:
                    if ct == 0:
                        for f in pending_ln:
                            f()
                        pending_ln = []
                    OP("act", "activation", out=yb[:, ct, :], in_=acc_ap, func=AF.Identity, bias=b_dw[:, ct:ct + 1],
                       r=[acc_R, c_vecs], w=[y_R[ct]])
                    if c == 3:
                        own_q.extend(last_per_ct[ct])
            if c == 3:
                for f in own_q:
                    f()
                pending_stats.append(last_fin)
            else:
                pending_stats.extend(ln_stats_groups(c))
            pending_ln = ln_norm_ops(c, col0)
        soft = []
        for e_ in ENGS:
            real = [o for o in S.ops[e_] if o.fn is not None]
            if real:
                soft.append(real[-1])
        dma(causal.rearrange("p a t -> p (a t)"), causal_d, w=[causal_R], deps=soft)
        for b3 in range(3):
            dma(KA[64:80, b3, :], onehot_d, w=[KAoh_R[b3]], deps=soft)
        build_ka(0, 0, deps=soft)
        tail = [f for grp in pending_stats for f in grp] + pending_ln
        del pending_stats[:]
        pending_ln = tail
        while pending_ln or early:
            if pending_ln:
                pending_ln.pop(0)()
            if early:
                early.pop(0)()

        S.barrier()
        cv = cvK
        biasp = cv.get([128, 4, 8, 80], BF16)
        km128 = cv.get([128, 2, 8, 16], BF16)
        km64 = km128[0:64]
        gate4 = cv.get([128, 512], F32)
        gate4_R = Region()
        PTb = cv.get([128, 3, 2, 512], BF16)
        un = cv.get([128, 2, 512], F32)
        gm = cv.get([128, 8, 16], F32)
        top = cv.get([128, 8, 8], F32)
        selm = cv.get([128, 8, 16], F32)
        gbias = cv.get([128, 16, 16], F32)
        ownind = cv.get([128, 16, 16], F32)
        validf = cv.get([128, 16, 16], F32)
        print("phase 2 arena bytes", cv.off)
        PTb_R = [Region() for _ in range(3)]
        un_R = [Region(), Region()]
        gm_R = Region()
        top_R = Region()
        selm_R = Region()
        biasp_R = Region()
        km64_R = Region()
        tab_R = Region()
        dma(gbias.rearrange("p a t -> p (a t)"), gb_d, w=[tab_R])
        dma(ownind.rearrange("p a t -> p (a t)"), own_d, w=[tab_R])
        dma(validf.rearrange("p a t -> p (a t)"), valid_d, w=[tab_R])
        OP("pool", "memset", QPc[:, 0].rearrange("p h t -> p (h t)"), 0.0, w=[QP_R[0]])
        OP("pool", "memset", biasp.rearrange("p a h t -> p (a h t)"), 0.0, w=[biasp_R])
        OP("dve", "memset", km128.rearrange("p a h n -> p (a h n)"), 0.0, w=[km64_R])
        km4 = km64.rearrange("p a (b two) n -> p a b two n", two=2)
        for a_, kmx in enumerate((kmhi, kmlo)):
            OP("dve", "tensor_copy", out=km4[:, a_, :, 0, :], in_=kmx[0:64, :, :], r=[km_R], w=[km64_R])
            OP("dve", "tensor_copy", out=km4[:, a_, :, 1, :], in_=kmx[64:128, :, :], r=[km_R], w=[km64_R])

        scB = dict(gate4=gate4, gm=gm, top=top, selm=selm, biasp=biasp, km128=km128, gbias=gbias, validf=validf, ownind=ownind,
                   gate4_R=gate4_R, gm_R=gm_R, top_R=top_R, selm_R=selm_R, biasp_R=biasp_R, km_R=km64_R, tab_R=tab_R)
        denr = cv.get([128, 2, 512], F32)
        rc4 = cv.get([128, 2, 4], F32)
        rchl = cv.get([128, 2, 2, 4], BF16)
        sel64b = cv.get([128, 128], BF16)
        wo_stg = cv.get([128, 1024], F32)
        wo_stg_R = Region()
        WO = Q[:].rearrange("p a t -> p (a t)").rearrange("p (k c) -> p k c", k=8)
        WO_R = [Region() for _ in range(8)]
        all_Q_R = [Q_R[p_][c_] for p_ in range(4) for c_ in range(4)]
        w_out_v = w_out.rearrange("(k p) c -> p k c", p=128)

        def wout_prefetch_pieces():
            res = []
            for k in range(8):
                def f(k=k):
                    dma(wo_stg, w_out_v[:, k, :], w=[wo_stg_R])
                    OP("dve", "tensor_copy", out=WO[:, k, :], in_=wo_stg, r=[wo_stg_R], w=[WO_R[k]] + all_Q_R)
                res.append(f)
            return res

        denr_R = [Region(), Region()]
        rc_R = [Region(), Region()]
        sel64b_R = Region()
        OP("dve", "tensor_copy", out=sel64b, in_=sel64[:], r=[c_sel64], w=[sel64b_R])

        work = []
        for c in range(4):
            tiles = list(range(4 * (c + 1))) + [16 + t for t in range(2 * N_OTHER_BLOCKS[c])]
            groups = [tiles[g:g + 2] for g in range(0, len(tiles), 2)]
            for h in range(NH):
                for gi, grp in enumerate(groups):
                    work.append((c, h, gi, grp, len(groups)))

        def emit_qk(wi):
            c, h, gi, grp, ngroups = work[wi]
            qb = (c + 1) % 2
            buf = (NH * c + h) % 3
            sb_i = wi % 2
            for s_, kt in enumerate(grp):
                diag = 4 * c <= kt < 4 * c + 4
                sc_ap, sc_R = PA[sb_i][:, s_, :], PA_R[sb_i][s_]
                OP("pe", "matmul", sc_ap, lhsT=KA[0:80, buf, kt * 128:(kt + 1) * 128], rhs=QPc[0:80, qb, h, :], start=True, stop=not diag,
                   r=[KA_R[buf], KAoh_R[buf], QP_R[qb]], w=[sc_R])
                if diag:
                    OP("pe", "matmul", sc_ap, lhsT=ident[:], rhs=causal[:, kt - 4 * c, :], start=False, stop=True,
                       r=[causal_R, c_ident], w=[sc_R])

        def emit_exp(wi):
            c, h, gi, grp, ngroups = work[wi]
            sb_i = wi % 2
            pb = wi % 3
            ng = len(grp)
            OP("act", "activation", out=PTb[:, pb, 0:ng, :], in_=PA[sb_i][:, 0:ng, :], func=AF.Exp, scale=0.125,
               r=PA_R[sb_i][0:ng], w=[PTb_R[pb]])

        def emit_pv(wi):
            c, h, gi, grp, ngroups = work[wi]
            pb = wi % 3
            ng = len(grp)
            ob = (c * NH + h) % 2
            o_ap, o_R = PR[ob][:], PR_R[ob]
            for s_, kt in enumerate(grp):
                first = (gi == 0 and s_ == 0)
                last = (gi == ngroups - 1 and s_ == ng - 1)
                OP("pe", "matmul", o_ap, lhsT=VF[:, kt * VT + h * 65: kt * VT + h * 65 + 128], rhs=PTb[:, pb, s_, :],
                   start=first, stop=last, r=[V_R[kt], vones_R, PTb_R[pb]], w=[o_R])

        def emit_norm1(c, h):
            ob = (c * NH + h) % 2
            o_ap, o_R = PR[ob][:], PR_R[ob]
            ub = h % 2
            OP("dve", "tensor_copy", out=denr[64:65, ub, :], in_=o_ap[64:65, :], r=[o_R], w=[denr_R[ub]])
            OP("dve", "tensor_copy", out=un[0:64, ub, :], in_=o_ap[0:64, :], r=[o_R], w=[un_R[ub]])

        def norm_tail_pieces(c, h):
            p, eo = h // 2, h % 2
            ub = h % 2
            col0 = 512 * c

            def den_mm(j):
                def f():
                    OP("pe", "matmul", PTF[:, j:j + 1], lhsT=denr[64:65, ub, 128 * j:128 * (j + 1)], rhs=onesf[64:65, 0:1],
                       start=True, stop=True, r=[denr_R[ub], c_onesf], w=[PT_R])
                return f

            def recip():
                OP("dve", "reciprocal", out=rc4[:, ub, :], in_=PTF[:, 0:4], r=[PT_R], w=[rc_R[ub]])
                OP("dve", "tensor_copy", out=rchl[:, ub, 0, :], in_=rc4[:, ub, :], r=[rc_R[ub]], w=[rc_R[ub]])
                OP("dve", "tensor_tensor", out=rchl[:, ub, 1, :], in0=rc4[:, ub, :], in1=rchl[:, ub, 0, :], op=ALU.subtract,
                   r=[rc_R[ub]], w=[rc_R[ub]])

            def bc_mm(j):
                def f():
                    for a_ in range(2):
                        OP("pe", "matmul", PH[:, 128 * j:128 * (j + 1)], lhsT=rchl[:, ub, a_, j:j + 1].to_broadcast([128, 128]), rhs=ident[:],
                           start=(a_ == 0), stop=(a_ == 1), r=[rc_R[ub], c_ident], w=[PH_R])
                return f

            def final():
                OP("dve", "tensor_tensor", out=MX[eo * 64:(eo + 1) * 64, p, col0:col0 + 512], in0=un[0:64, ub, :], in1=PH[0:64, :],
                   op=ALU.mult, r=[un_R[ub], PH_R], w=MX_R[4 * c:4 * c + 4])
            def den_all():
                for j in range(4):
                    den_mm(j)()
                recip()
            return [den_all, bc_mm(0), bc_mm(1), bc_mm(2), bc_mm(3), final]

        emit_qk(0)
        emit_qk(1)
        deferred = []
        for wi in range(len(work)):
            c, h, gi, grp, ngroups = work[wi]
            if gi == 0:
                nxt = NH * c + h + 1
                if nxt < 4 * NH:
                    build_ka(nxt // NH, nxt % NH)
            emit_exp(wi)
            if wi + 2 < len(work):
                emit_qk(wi + 2)
            emit_pv(wi)
            nd = []
            for (cnt, fn) in deferred:
                if cnt <= 1:
                    fn()
                else:
                    nd.append((cnt - 1, fn))
            deferred = nd
            if gi == ngroups - 1:
                emit_norm1(c, h)
                tp = norm_tail_pieces(c, h)
                if ngroups >= 8:
                    sched_ = [4, 8, 9, 10, 11, 11]
                else:
                    sched_ = [3, 5, 5, 6, 6, 6]
                for cnt_, fn_ in zip(sched_, tp):
                    deferred.append((cnt_, fn_))
                if h == 0 and c + 1 < 4:
                    if c >= 1:
                        offs = [1, 3, 5, 9, 13, 17] + [21 + 3 * hh for hh in range(NH)]
                    else:
                        offs = [1, 3, 5, 7, 9, 11] + [15 + hh for hh in range(NH)]
                    for off_, piece in zip(offs, prep_pieces(c + 1, scB)):
                        deferred.append((off_, piece))
                if c == 3 and h == 0:
                    for pi, piece in enumerate(wout_prefetch_pieces()):
                        deferred.append((2 + 12 * pi, piece))
        for (cnt, fn) in deferred:
            fn()

        if debug:
            dma(dbg["mx"][:, 0:8192], R1[:, 0:8192], r=MX_R)
            dma(dbg["mx"][:, 8192:16384], CVT[:].rearrange("p a t -> p (a t)"), r=CV_R)
        S.barrier()
        cv = Carver()
        W1B = cv.get([128, 2, 8, 512], BF16)
        W2B = cv.get([128, 2, 4, 1024], BF16)
        wst3 = cv.get([128, 2, 2048], F32)
        FF = cv.get([128, 2, 4, 512], BF16)
        rl = cv.get([128, 2, 512], F32)
        hn = cv.get([128, 3, 1024], BF16)
        junk3 = cv.get([128, 1024], BF16)
        ost = cv.get([128, 1024], F32)
        gfin = cv.get([128, 1024], F32)
        print("phase 3 arena bytes", cv.off)
        W1B_R = [[Region() for _ in range(8)] for _ in range(2)]
        W2B_R = [[Region() for _ in range(4)] for _ in range(2)]
        wst3_R = [Region(), Region()]
        FF_R = [[Region() for _ in range(4)] for _ in range(2)]
        rl_R = [Region(), Region()]
        hn_R = [Region(), Region(), Region()]
        ost_R = Region()
        gfin_R = Region()
        H1_R = [Region() for _ in range(16)]
        HN = MX
        HN_R = [Region() for _ in range(16)]
        dma(gfin, gfin_d, w=[gfin_R])

        cast_i = {"n": 0}

        def cast(out, in_, r, w, scale=None):
            eng = ("dve", "act")[cast_i["n"] % 2]
            cast_i["n"] += 1
            if eng == "dve":
                if scale is None:
                    OP("dve", "tensor_copy", out=out, in_=in_, r=r, w=w)
                else:
                    OP("dve", "tensor_scalar", out=out, in0=in_, scalar1=scale, scalar2=None, op0=ALU.mult, r=r, w=w)
            else:
                if scale is None:
                    OP("act", "activation", out=out, in_=in_, func=AF.Copy, r=r, w=w)
                else:
                    OP("act", "activation", out=out, in_=in_, func=AF.Copy, scale=scale, r=r, w=w)

        stage_i = {"n": 0}

        def stage_dma(src_ap):
            b = stage_i["n"] % 2
            stage_i["n"] += 1
            dma(wst3[:, b, :], src_ap, w=[wst3_R[b]])
            return b

        w_out_v = w_out.rearrange("(k p) c -> p k c", p=128)
        w1_v = w_m1.rearrange("(k p) c -> p k c", p=128)
        w2_v = w_m2.rearrange("(f p) c -> p f c", p=128)

        def wout_pieces():
            res = []
            for k2 in range(4):
                def d(k2=k2):
                    return stage_dma(w_out_v[:, 2 * k2:2 * k2 + 2, :])

                def cfn(b, k2=k2):
                    for kk in range(2):
                        cast(WO[:, 2 * k2 + kk, :], wst3[:, b, kk * 1024:(kk + 1) * 1024], [wst3_R[b]], [WO_R[2 * k2 + kk]])
                res.append((d, cfn))
            return res

        def ffblock_pieces(fb):
            wb = fb % 2
            res = []
            for half in range(2):
                def d(half=half):
                    return stage_dma(w1_v[:, 4 * half:4 * half + 4, fb * 512:(fb + 1) * 512])

                def cfn(b, half=half):
                    for kk in range(4):
                        k = 4 * half + kk
                        cast(W1B[:, wb, k, :], wst3[:, b, kk * 512:(kk + 1) * 512], [wst3_R[b], c_vecs], [W1B_R[wb][k]], scale=g_mlp[:, k:k + 1])
                res.append((d, cfn))
            for half in range(2):
                def d(half=half):
                    return stage_dma(w2_v[:, 4 * fb + 2 * half: 4 * fb + 2 * half + 2, :])

                def cfn(b, half=half):
                    for kk in range(2):
                        f = 2 * half + kk
                        cast(W2B[:, wb, f, :], wst3[:, b, kk * 1024:(kk + 1) * 1024], [wst3_R[b]], [W2B_R[wb][f]])
                res.append((d, cfn))
            return res

        class Loader:
            def __init__(self):
                self.queue = []
                self.inflight = []

            def add(self, pieces):
                self.queue.extend(pieces)
                self.pump()

            def pump(self):
                while self.queue and len(self.inflight) < 2:
                    d, cfn = self.queue.pop(0)
                    self.inflight.append((d(), cfn))

            def tick(self, n=1):
                for _ in range(n):
                    if not self.inflight:
                        return
                    b, cfn = self.inflight.pop(0)
                    cfn(b)
                    self.pump()

            def drain(self):
                while self.inflight:
                    self.tick()

        def h1_view(i):
            return H1[:, i, :].rearrange("p (a d) -> p a d", a=2)

        def rms_stats(i, sl):
            ssq = st4[:, 0, sl:sl + 1]
            vv = st4[:, 1, sl:sl + 1]
            rs = st4[:, 2, sl:sl + 1]
            OP("act", "activation", out=junk3, in_=H1[:, i, :], func=AF.Square, accum_out=ssq, r=[H1_R[i]], w=[st_R[sl], junk_R])
            OP("pool", "tensor_scalar", out=vv, in0=ssq, scalar1=1.0 / D, scalar2=EPS, op0=ALU.mult, op1=ALU.add, r=[st_R[sl]], w=[st_R[sl]])
            OP("pool", "tensor_tensor", out=rs, in0=vv, in1=mhalf1[:], op=ALU.pow, r=[st_R[sl], c_mhalf1], w=[st_R[sl]])
            return rs

        LD = Loader()
        for i in range(16):
            dma(H1[:, i, :], x_tiles[i], w=[H1_R[i]])
        LD.add(ffblock_pieces(0))

        pa_i = 0

        def outproj_mm(i):
            pa = i % 2
            for half in range(2):
                for k in range(8):
                    src = MX[:, k, 128 * i:128 * (i + 1)] if k < 4 else CVT[:, k - 4, 128 * i:128 * (i + 1)]
                    OP("pe", "matmul", PA[pa][:, half, :], lhsT=src, rhs=WO[:, k, half * 512:(half + 1) * 512], start=(k == 0), stop=(k == 7),
                       r=[MX_R[i], CV_R[i], WO_R[k]], w=[PA_R[pa][half]])
            OP("dve", "tensor_tensor", out=h1_view(i), in0=PA[pa][:], in1=h1_view(i), op=ALU.add, r=PA_R[pa] + [H1_R[i]], w=[H1_R[i]])
            sl = i % 4
            rms_stats(i, sl)

        def outproj_scale(i):
            sl = i % 4
            hb = i % 3
            rs = st4[:, 2, sl:sl + 1]
            OP("dve", "tensor_scalar", out=hn[:, hb, :], in0=H1[:, i, :], scalar1=rs, scalar2=None, op0=ALU.mult,
               r=[H1_R[i], st_R[sl]], w=[hn_R[hb]])

        def outproj_tr(i):
            hb = i % 3
            for k in range(8):
                OP("pe", "transpose", out=PT[:, k * 128:(k + 1) * 128], in_=hn[:, hb, k * 128:(k + 1) * 128], identity=ident[:],
                   r=[hn_R[hb], c_ident], w=[PT_R])
            OP("act", "activation", out=HN[:, :, 128 * i:128 * (i + 1)], in_=PT3, func=AF.Copy, r=[PT_R], w=[HN_R[i], MX_R[i]])

        outproj_mm(0)
        outproj_mm(1)
        outproj_scale(0)
        for i in range(16):
            if i + 2 < 16:
                outproj_mm(i + 2)
            if i + 1 < 16:
                outproj_scale(i + 1)
            outproj_tr(i)
            if i % 4 == 3:
                LD.tick()
        LD.drain()
        if debug:
            dma(dbg["h1"], H1.rearrange("p i d -> p (i d)"), r=H1_R)

        items = [(fb, tc) for fb in range(8) for tc in range(4)]
        out_ops = []

        def mlp_in(ii):
            fb, tc = items[ii]
            wb = fb % 2
            fbuf = ii % 2
            for ft in range(4):
                fpr = ft % 2
                for k in range(8):
                    OP("pe", "matmul", PR[fpr][:], lhsT=W1B[:, wb, k, ft * 128:(ft + 1) * 128], rhs=HN[:, k, 512 * tc:512 * (tc + 1)],
                       start=(k == 0), stop=(k == 7), r=[W1B_R[wb][k]] + HN_R[4 * tc:4 * tc + 4], w=[PR_R[fpr]])
                rb = ft % 2
                OP("act", "activation", out=rl[:, rb, :], in_=PR[fpr][:], func=AF.Relu, r=[PR_R[fpr]], w=[rl_R[rb]])
                OP("dve", "tensor_tensor", out=FF[:, fbuf, ft, :], in0=rl[:, rb, :], in1=rl[:, rb, :], op=ALU.mult,
                   r=[rl_R[rb]], w=[FF_R[fbuf][ft]])

        def mlp_out(ii):
            fb, tc = items[ii]
            wb = fb % 2
            fbuf = ii % 2
            for ti in range(4):
                i = 4 * tc + ti
                pa = ti % 2
                for half in range(2):
                    for ft in range(4):
                        OP("pe", "matmul", PA[pa][:, half, :], lhsT=FF[:, fbuf, ft, 128 * ti:128 * (ti + 1)],
                           rhs=W2B[:, wb, ft, half * 512:(half + 1) * 512], start=(ft == 0), stop=(ft == 3),
                           r=[FF_R[fbuf][ft], W2B_R[wb][ft]], w=[PA_R[pa][half]])
                OP("dve", "tensor_tensor", out=h1_view(i), in0=PA[pa][:], in1=h1_view(i), op=ALU.add, r=PA_R[pa] + [H1_R[i]], w=[H1_R[i]])
                if fb == 7:
                    sl = i % 4
                    rs = rms_stats(i, sl)

                    def fin(i=i, sl=sl, rs=rs):
                        OP("dve", "scalar_tensor_tensor", out=H1[:, i, :], in0=H1[:, i, :], scalar=rs, in1=gfin, op0=ALU.mult, op1=ALU.mult,
                           r=[H1_R[i], st_R[sl], gfin_R], w=[H1_R[i]])
                        out_ops.append(dma(y_out[128 * i:128 * (i + 1), :], H1[:, i, :], r=[H1_R[i]]))
                    finals.append(fin)
                    if len(finals) > 2:
                        finals.pop(0)()

        finals = []
        mlp_in(0)
        for ii in range(len(items)):
            fb, tc = items[ii]
            if tc == 0 and fb + 1 < 8:
                LD.add(ffblock_pieces(fb + 1))
            if ii + 1 < len(items):
                if items[ii + 1][1] == 0:
                    LD.drain()
                mlp_in(ii + 1)
            mlp_out(ii)
            LD.tick()
        for f in finals:
            f()
        S.add("sp", None, deps=out_ops + S.dma_since_barrier)
        S.emit(nc)
    return nc


def _core_tables(role):
    own = OWN_BLOCKS[role]
    oth = OWN_BLOCKS[1 - role]
    nat = own + oth
    pos = np.concatenate([np.arange(b * BL, (b + 1) * BL) for b in nat]).astype(np.float32)
    inv_freq = (np.float32(500000.0) ** (-np.arange(8, dtype=np.float32) * np.float32(2.0) / np.float32(16))).astype(np.float32)
    ang = (pos[:, None] * inv_freq[None, :]).astype(np.float32)
    cos = np.cos(ang).astype(np.float32)
    sin = np.sin(ang).astype(np.float32)
    cosT = np.ones((128, SEQ), np.float32)
    sinT = np.zeros((128, SEQ), np.float32)
    for p in range(128):
        d = p % 64
        if d < 8:
            cosT[p] = cos[:, d]
            sinT[p] = -sin[:, d]
        elif d < 16:
            cosT[p] = cos[:, d - 8]
            sinT[p] = sin[:, d - 8]
    gb = np.zeros((16, 16), np.float32)
    ownind = np.zeros((16, 16), np.float32)
    valid = np.zeros((16, 16), np.float32)
    for qt in range(16):
        j = qt // 2
        for n in range(16):
            if nat[n] < own[j]:
                valid[qt, n] = 1.0
            else:
                gb[qt, n] = -1e9
            if n == j:
                ownind[qt, n] = 1.0
    rep = lambda a: np.ascontiguousarray(np.broadcast_to(a.reshape(1, -1), (128, a.size))).astype(np.float32)
    return dict(nat=nat, own=own, cosT=cosT, sinT=sinT, gbias=rep(gb), ownind=rep(ownind), validf=rep(valid))


def _const_tables():
    ident = np.eye(128, dtype=np.float32).astype(ml_dtypes.bfloat16)
    perm = np.zeros((128, 128), np.float32)
    for m in range(128):
        d = m % 64
        if d < 8:
            perm[m + 8, m] = 1.0
        elif d < 16:
            perm[m - 8, m] = 1.0
    causal = np.zeros((128, 4, 512), np.float32)
    for kk in range(4):
        kp = 128 * kk + np.arange(128)[:, None]
        qi = np.arange(512)[None, :]
        causal[:, kk, :] = np.where(kp <= qi, 0.0, NEG)
    sel64 = np.zeros((128, 128), np.float32)
    sel64[64, :] = 1.0
    onehot = np.zeros((16, SEQ), np.float32)
    for n in range(16):
        onehot[n, n * BL:(n + 1) * BL] = 1.0
    return dict(ident=ident, perm=perm.astype(ml_dtypes.bfloat16),
                causal=causal.reshape(128, 2048).astype(ml_dtypes.bfloat16), sel64=sel64,
                onehot=onehot.astype(ml_dtypes.bfloat16))


_NC_CACHE = {}


def kernel(x, g_mix_norm, w_in, b_glu, w_dw, b_dw, g_conv_ln, b_conv_ln, w_out, g_mlp_norm, w_mlp_in, w_mlp_out, g_final,
           _debug=False):
    f = lambda a: np.ascontiguousarray(np.asarray(a, dtype=np.float32))
    x = f(x)
    w_in0, w_out0, w_m1, w_m2 = f(w_in)[0], f(w_out)[0], f(w_mlp_in)[0], f(w_mlp_out)[0]
    vecs = np.zeros((128, NVEC), np.float32)
    vecs[:, 0:8] = f(g_mix_norm)[0].reshape(8, 128).T
    vecs[:, 8:16] = f(g_mlp_norm)[0].reshape(8, 128).T
    bg = f(b_glu)[0]
    vecs[:, 16:20] = bg[0:512].reshape(4, 128).T
    vecs[:, 20:24] = bg[512:1024].reshape(4, 128).T
    vecs[:, 24:28] = f(b_dw)[0].reshape(4, 128).T
    vecs[:, 28:32] = f(g_conv_ln)[0].reshape(4, 128).T
    vecs[:, 32:36] = f(b_conv_ln)[0].reshape(4, 128).T
    wd = f(w_dw)[0, :, 0, :]
    vecs[:, 40:164] = wd.reshape(31, 4, 128).transpose(2, 1, 0).reshape(128, 124)
    gfin = np.ascontiguousarray(np.broadcast_to(f(g_final).reshape(1, D), (128, D)))
    consts = _const_tables()
    tabs = [_core_tables(0), _core_tables(1)]

    in_maps = []
    for core in range(8):
        b, role = core // 2, core % 2
        T = tabs[role]
        xb = x[b]
        x_loc = np.concatenate([xb[n * BL:(n + 1) * BL] for n in T["nat"]], axis=0)
        x_halo = np.zeros((4, 32, D), np.float32)
        v = vecs.copy()
        for c in range(4):
            start = T["own"][2 * c] * BL
            if start > 0:
                x_halo[c] = xb[start - 32:start]
                v[:, 36 + c] = 1.0
        in_maps.append(dict(x_loc=np.ascontiguousarray(x_loc), x_halo=x_halo, w_in=w_in0, w_out=w_out0, w_m1=w_m1, w_m2=w_m2,
                            cosT=T["cosT"], sinT=T["sinT"], vecs=v, gfin=gfin, gbias=T["gbias"], ownind=T["ownind"],
                            validf=T["validf"], **consts))

    key = bool(_debug)
    if key not in _NC_CACHE:
        _NC_CACHE[key] = build_program(debug=_debug)
    nc = _NC_CACHE[key]
    res = run_bass_kernel_spmd(nc, in_maps, core_ids=list(range(8)))
    out = np.zeros((4, SEQ, D), np.float32)
    for core in range(8):
        b, role = core // 2, core % 2
        y = res.results[core]["y"]
        for j, n in enumerate(tabs[role]["own"]):
            out[b, n * BL:(n + 1) * BL] = y[j * BL:(j + 1) * BL]
    if _debug:
        return out, res.results, tabs
    return out
```

```python
from contextlib import ExitStack

import numpy as np
import ml_dtypes

import concourse.bass as bass
import concourse.mybir as mybir
from concourse.bass_utils import run_bass_kernel_spmd

F32 = mybir.dt.float32
BF16 = mybir.dt.bfloat16
AF = mybir.ActivationFunctionType
ALU = mybir.AluOpType
AX = mybir.AxisListType

D = 1024
SEQ = 4096
NBLK = 16
BL = 256
NH = 8
EPS = 1e-6
OWN_BLOCKS = {0: [0, 1, 6, 7, 8, 9, 14, 15], 1: [2, 3, 4, 5, 10, 11, 12, 13]}
N_OTHER_BLOCKS = [2, 4, 6, 8]
NEG = -30000.0
VT = 520
NVEC = 164

ENGS = ("pe", "act", "dve", "pool", "sp")
N_DMA_SEMS = 40


class Region:
    __slots__ = ("writer", "readers")

    def __init__(self):
        self.writer = None
        self.readers = []


class Op:
    __slots__ = ("eng", "fn", "deps", "sig", "needs_sig", "is_dma", "sem", "val", "idx")

    def __init__(self, eng, fn, is_dma):
        self.eng = eng
        self.fn = fn
        self.deps = set()
        self.sig = None
        self.needs_sig = False
        self.is_dma = is_dma
        self.sem = None
        self.val = None


class Sched:
    def __init__(self):
        self.ops = {e: [] for e in ENGS}
        self.all = []
        self.dma_ops = []
        self.dma_since_barrier = []

    def add(self, eng, fn, r=(), w=(), deps=(), dma=False):
        op = Op(eng, fn, dma)
        op.idx = len(self.ops[eng])
        for d in deps:
            if d is not None:
                op.deps.add(d)
        for reg in r:
            if reg.writer is not None:
                op.deps.add(reg.writer)
            reg.readers.append(op)
        for reg in w:
            best = {}
            for rd in reg.readers:
                if rd.is_dma:
                    op.deps.add(rd)
                elif rd.eng not in best or rd.idx > best[rd.eng].idx:
                    best[rd.eng] = rd
            op.deps.update(best.values())
            reg.readers = []
            if reg.writer is not None:
                op.deps.add(reg.writer)
            reg.writer = op
        op.deps.discard(op)
        if dma:
            i = len(self.dma_ops)
            if i >= N_DMA_SEMS:
                op.deps.add(self.dma_ops[i - N_DMA_SEMS])
            op.sem = i % N_DMA_SEMS
            op.val = 16 * (i // N_DMA_SEMS + 1)
            self.dma_ops.append(op)
            self.dma_since_barrier.append(op)
        op.idx = len(self.ops[eng])
        self.ops[eng].append(op)
        self.all.append(op)
        return op

    def barrier(self):
        last = []
        for e in ENGS:
            real = [o for o in self.ops[e] if o.fn is not None]
            if real:
                last.append(real[-1])
        deps = last + self.dma_since_barrier[-4:]
        self.dma_since_barrier = []
        for e in ENGS:
            self.add(e, None, deps=deps)

    def finalize(self):
        for op in self.all:
            for d in op.deps:
                if d.is_dma:
                    continue
                if d.eng == "pe" and op.eng == "pe" and not op.is_dma:
                    continue
                d.needs_sig = True
        for e in ENGS:
            c = 0
            for op in self.ops[e]:
                if op.needs_sig and not op.is_dma:
                    c += 1
                    op.sig = c

    def emit(self, nc):
        self.finalize()
        with ExitStack() as st:
            esem = {e: st.enter_context(nc.semaphore("s_" + e)) for e in ENGS}
            dsem = [st.enter_context(nc.semaphore("d%d" % i)) for i in range(N_DMA_SEMS)]
            block = st.enter_context(nc.Block())

            def run(ename, eng):
                waited = {}
                for op in self.ops[ename]:
                    need = {}
                    for d in op.deps:
                        if d.is_dma:
                            key, sem, val = ("d", d.sem), dsem[d.sem], d.val
                        else:
                            if d.eng == "pe" and ename == "pe" and not op.is_dma:
                                continue
                            key, sem, val = ("e", d.eng), esem[d.eng], d.sig
                        if key not in need or need[key][1] < val:
                            need[key] = (sem, val)
                    for key in sorted(need, key=lambda k: (k[0], str(k[1]))):
                        sem, val = need[key]
                        if waited.get(key, 0) >= val:
                            continue
                        eng.wait_ge(sem, val)
                        waited[key] = val
                    if op.fn is None:
                        continue
                    name, a, kw = op.fn
                    ins = getattr(eng, name)(*a, **kw)
                    if op.is_dma:
                        ins.then_inc(dsem[op.sem], 16)
                    elif op.needs_sig:
                        ins.then_inc(esem[ename], 1)

            @block.tensor
            def _(eng):
                run("pe", eng)

            @block.scalar
            def _(eng):
                run("act", eng)

            @block.vector
            def _(eng):
                run("dve", eng)

            @block.gpsimd
            def _(eng):
                run("pool", eng)

            @block.sync
            def _(eng):
                run("sp", eng)


def build_program(debug=False):
    nc = bass.Bass("TRN2", target_bir_lowering=False)
    S = Sched()

    def OP(eng, name, *a, r=(), w=(), deps=(), **kw):
        return S.add(eng, (name, a, kw), r=r, w=w, deps=deps)

    def dma(out, in_, r=(), w=(), eng="sp", deps=()):
        return S.add(eng, ("dma_start", (), dict(out=out, in_=in_)), r=r, w=w, dma=True, deps=deps)

    def din(name, shape, dt=F32):
        return nc.dram_tensor(name, shape, dt, kind="ExternalInput").ap()

    x_loc = din("x_loc", [SEQ, D])
    x_halo = din("x_halo", [4, 32, D])
    w_in = din("w_in", [D, 2560])
    w_out = din("w_out", [D, D])
    w_m1 = din("w_m1", [D, 4096])
    w_m2 = din("w_m2", [4096, D])
    cosT_d = din("cosT", [128, SEQ])
    sinT_d = din("sinT", [128, SEQ])
    vecs_d = din("vecs", [128, NVEC])
    gfin_d = din("gfin", [128, D])
    gb_d = din("gbias", [128, 256])
    own_d = din("ownind", [128, 256])
    valid_d = din("validf", [128, 256])
    ident_d = din("ident", [128, 128], BF16)
    perm_d = din("perm", [128, 128], BF16)
    causal_d = din("causal", [128, 4 * 512], BF16)
    sel64_d = din("sel64", [128, 128])
    onehot_d = din("onehot", [16, SEQ], BF16)
    y_out = nc.dram_tensor("y", [2048, D], F32, kind="ExternalOutput").ap()
    dbg = {}
    if debug:
        dbg["kt"] = nc.dram_tensor("dbg_kt", [128, 4 * SEQ], BF16, kind="ExternalOutput").ap()
        dbg["q"] = nc.dram_tensor("dbg_q", [128, 4 * 2048], BF16, kind="ExternalOutput").ap()
        dbg["v"] = nc.dram_tensor("dbg_v", [128, 32 * VT + 64], BF16, kind="ExternalOutput").ap()
        dbg["mx"] = nc.dram_tensor("dbg_mx", [128, 8 * 2048], BF16, kind="ExternalOutput").ap()
        dbg["km"] = nc.dram_tensor("dbg_km", [128, 64], F32, kind="ExternalOutput").ap()
        dbg["h1"] = nc.dram_tensor("dbg_h1", [128, 16 * 1024], F32, kind="ExternalOutput").ap()

    with ExitStack() as st:
        def sb(name, shape, dt=F32):
            return st.enter_context(nc.sbuf_tensor("sb_" + name, shape, dt))

        def ps(name, shape, dt=F32):
            return st.enter_context(nc.psum_tensor("ps_" + name, shape, dt))

        R1 = sb("R1", [128, 16384], BF16)
        MX = R1[:].rearrange("p (k t) -> p k t", k=8)
        WQKV = R1[:, 0:12288].rearrange("p (k c) -> p k c", k=8)
        WB = R1[:, 0:8192].rearrange("p (k c) -> p k c", k=8)
        R2 = sb("R2", [128, 16384 + 32 * VT + 64], BF16)
        KT = R2[:, 0:16384].rearrange("p (a t) -> p a t", a=4)
        VF = R2[:, 16384:16384 + 32 * VT + 64]
        V4 = R2[:, 16384:16384 + 32 * VT].rearrange("p (t h e) -> p t h e", h=8, e=65)
        H1 = R2[:, 0:32768].bitcast(F32).rearrange("p (i d) -> p i d", i=16)
        Q = sb("Q", [128, 4, 2048], BF16)
        QPc = R1[:, 8192:16384].rearrange("p (a h t) -> p a h t", a=2, h=8)
        QP_R = [Region(), Region()]
        CVT = sb("CVT", [128, 4, 2048], BF16)
        ident = sb("ident", [128, 128], BF16)
        perm = sb("perm", [128, 128], BF16)
        sel64 = sb("sel64", [128, 128])
        onesf = sb("onesf", [128, 128])
        vecs = sb("vecs", [128, NVEC])
        kms = sb("kms", [128, 4, 16])
        kmhi = sb("kmhi", [128, 4, 16], BF16)
        kmlo = sb("kmlo", [128, 4, 16], BF16)
        kmf = sb("kmf", [128, 4, 16])
        st4 = sb("st4", [128, 3, 4])
        mhalf1 = sb("mhalf1", [128, 1])
        ARENA_BYTES = 77824
        ARENA = sb("ARENA", [128, ARENA_BYTES // 2], BF16)

        g_mix = vecs[:, 0:8]
        g_mlp = vecs[:, 8:16]
        b_a = vecs[:, 16:20]
        b_g = vecs[:, 20:24]
        b_dw = vecs[:, 24:28]
        g_ln = vecs[:, 28:32]
        b_ln = vecs[:, 32:36]
        halo_mask = vecs[:, 36:40]
        w_dw = vecs[:, 40:164].rearrange("p (c k) -> p c k", c=4)

        class Carver:
            def __init__(self):
                self.off = 0

            def get(self, shape, dt):
                n = 1
                for s_ in shape[1:]:
                    n *= s_
                nbytes = n * (2 if dt == BF16 else 4)
                off = self.off
                self.off += (nbytes + 63) // 64 * 64
                assert self.off <= ARENA_BYTES, (self.off, ARENA_BYTES)
                if dt == BF16:
                    ap = ARENA[:, off // 2: off // 2 + n]
                else:
                    ap = ARENA[:, off // 2: off // 2 + 2 * n].bitcast(F32)
                if len(shape) == 2:
                    return ap
                names = " ".join("d%d" % i for i in range(len(shape) - 1))
                kw = {"d%d" % i: shape[i + 1] for i in range(len(shape) - 2)}
                return ap.rearrange("p (%s) -> p %s" % (names, names), **kw)

        PA = [ps("PA%d" % i, [128, 2, 512]) for i in range(2)]
        PR = [ps("PR%d" % i, [128, 512]) for i in range(2)]
        PH = ps("PH", [128, 512])
        PTF = ps("PT", [128, 512])
        PT = PTF[:].bitcast(BF16)
        PA_R = [[Region(), Region()] for _ in range(2)]
        PR_R = [Region(), Region()]
        PH_R = Region()
        PT_R = Region()
        pa_banks = [(PA[i][:, j, :], PA_R[i][j]) for i in range(2) for j in range(2)]
        PT3 = PT.rearrange("p (k t) -> p k t", k=8)

        c_ident, c_perm, c_sel64, c_onesf, c_vecs, c_mhalf1 = [Region() for _ in range(6)]
        dma(ident[:], ident_d, w=[c_ident])
        dma(vecs[:], vecs_d, w=[c_vecs])
        dma(perm[:], perm_d, w=[c_perm])
        dma(sel64[:], sel64_d, w=[c_sel64])
        OP("pool", "memset", onesf[:], 1.0, w=[c_onesf])
        OP("pool", "memset", QPc[:, 1].rearrange("p h t -> p (h t)"), 0.0, w=[QP_R[1]])
        OP("pool", "memset", mhalf1[:], -0.5, w=[c_mhalf1])
        kms_R = Region()
        OP("pool", "memset", kms[:], 0.0, w=[kms_R])

        cv = Carver()
        xnT = cv.get([128, 2, 8, 544], BF16)
        xst = cv.get([128, 3, 1024], F32)
        xh = cv.get([128, 1024], F32)
        xn = cv.get([128, 2, 1024], BF16)
        off_cs2 = cv.off
        cs2 = cv.get([128, 2, 2, 512], F32)
        t12 = cv.get([128, 1024], F32)
        t1 = t12[:, 0:512]
        t2 = t12[:, 512:1024]
        off_qraw = cv.off
        qraw = cv.get([128, 2, 512], BF16)
        hT = cv.get([128, 4, 544], BF16)
        dg = cv.get([128, 2, 16, 128], BF16)
        junk = dg[:, 0, 0:8, :].rearrange("p a t -> p (a t)")
        w_dw_bf = cv.get([128, 4, 32], BF16)
        yb = cv.get([128, 4, 512], F32)
        lnmv = cv.get([128, 1024], F32)
        lnm = lnmv[:, 0:512]
        lnv = lnmv[:, 512:1024]
        sigh = cv.get([128, 32], F32)
        print("pass A/B arena bytes", cv.off)

        xnT_R = [[Region() for _ in range(5)] for _ in range(2)]
        xst_R = [Region() for _ in range(3)]
        xh_R = Region()
        junk_R = Region()
        xn_R = [Region(), Region()]
        st_R = [Region() for _ in range(4)]
        cs_R2 = [Region(), Region()]
        t1_R = Region()
        t2_R = Region()
        qraw_R = [Region(), Region()]
        hT_R = [Region() for _ in range(4)]
        dg_R = [Region(), Region()]
        wdb_R = Region()
        y_R = [Region() for _ in range(4)]
        lnm_R = Region()
        lnv_R = Region()
        sigh_R = Region()
        W_R = [Region() for _ in range(8)]
        WQ_R = [Region() for _ in range(8)]
        WK_R = [Region() for _ in range(8)]
        WV_R = [Region() for _ in range(8)]
        KT_R = [[Region() for _ in range(8)] for _ in range(4)]
        Q_R = [[Region() for _ in range(4)] for _ in range(4)]
        V_R = [Region() for _ in range(32)]
        MX_R = [Region() for _ in range(16)]
        CV_R = [Region() for _ in range(16)]
        vones_R = Region()
        OP("pool", "memset", VF[:, 0:32 * VT].rearrange("p (n e) -> p n e", e=65)[:, :, 64:65], 1.0, w=[vones_R])
        OP("pool", "memset", VF[:, 32 * VT:32 * VT + 64], 0.0, w=[vones_R])

        state = {"tile": 0, "pa": 0, "pr": 0, "qraw": 0, "evac": 0, "ld": 0, "cast": 0, "junk": (junk, [dg_R[0]])}

        def next_pa():
            i = state["pa"] % 4
            state["pa"] += 1
            return pa_banks[i]

        def next_pr():
            i = state["pr"] % 2
            state["pr"] += 1
            return PR[i][:], PR_R[i]

        w_in_v = w_in.rearrange("(k p) c -> p k c", p=128)

        def load_w_in(dst, c0, ncols, wregs_of, stg, n_now=8):
            ns = len(stg)

            def issue(kt):
                st_ap, st_regs = stg[kt % ns]
                dma(st_ap, w_in_v[:, kt, c0:c0 + ncols], w=st_regs)
            for kt in range(min(ns, 8, n_now)):
                issue(kt)

            def finish(deps=()):
                for kt in range(8):
                    st_ap, st_regs = stg[kt % ns]
                    eng = ("dve", "act")[kt % 2]
                    if eng == "act":
                        OP("act", "activation", out=dst[:, kt, :], in_=st_ap, func=AF.Copy, scale=g_mix[:, kt:kt + 1],
                           r=st_regs + [c_vecs], w=wregs_of(kt), deps=deps)
                    else:
                        OP(eng, "tensor_scalar", out=dst[:, kt, :], in0=st_ap, scalar1=g_mix[:, kt:kt + 1], scalar2=None,
                           op0=ALU.mult, r=st_regs + [c_vecs], w=wregs_of(kt), deps=deps)
                    if kt + ns < 8:
                        issue(kt + ns)
            finish.more = lambda: [issue(kt) for kt in range(min(ns, 8, n_now), min(ns, 8))]
            return finish

        def norm_tile(src, src_R, npart, dst_cols, dst_R):
            j = state["tile"]
            state["tile"] += 1
            sl = j % 4
            xb = j % 2
            ssq = st4[0:npart, 0, sl:sl + 1]
            vv = st4[0:npart, 1, sl:sl + 1]
            rs = st4[0:npart, 2, sl:sl + 1]
            jk, jk_regs = state["junk"]
            OP("act", "activation", out=jk[0:npart, :], in_=src, func=AF.Square, accum_out=ssq, r=[src_R], w=[st_R[sl]] + jk_regs)
            OP("pool", "tensor_scalar", out=vv, in0=ssq, scalar1=1.0 / D, scalar2=EPS, op0=ALU.mult, op1=ALU.add,
               r=[st_R[sl]], w=[st_R[sl]])
            OP("pool", "tensor_tensor", out=rs, in0=vv, in1=mhalf1[0:npart, :], op=ALU.pow, r=[st_R[sl], c_mhalf1], w=[st_R[sl]])

            def part_a2():
                if state.get("scale_on_pool"):
                    OP("pool", "tensor_scalar", out=xn[0:npart, xb, :], in0=src, scalar1=rs, scalar2=1.0, op0=ALU.mult, op1=ALU.mult,
                       r=[src_R, st_R[sl]], w=[xn_R[xb]])
                else:
                    OP("dve", "tensor_scalar", out=xn[0:npart, xb, :], in0=src, scalar1=rs, scalar2=None, op0=ALU.mult,
                       r=[src_R, st_R[sl]], w=[xn_R[xb]])
                return part_b

            def part_b():
                for k in range(8):
                    OP("pe", "transpose", out=PT[:, k * 128:k * 128 + npart], in_=xn[0:npart, xb, k * 128:(k + 1) * 128],
                       identity=ident[0:npart, 0:npart], r=[xn_R[xb], c_ident], w=[PT_R])
                src_ps = PT3[:, :, 0:npart]
                if state["evac"] % 2 == 0:
                    OP("act", "activation", out=dst_cols, in_=src_ps, func=AF.Copy, r=[PT_R], w=[dst_R])
                else:
                    OP("dve", "tensor_copy", out=dst_cols, in_=src_ps, r=[PT_R], w=[dst_R])
                state["evac"] += 1
            return part_a2

        def rope_part1(ps_ap, ps_R):
            qb = state["qraw"] % 2
            state["qraw"] += 1
            OP("act", "activation", out=qraw[:, qb, :], in_=ps_ap, func=AF.Copy, r=[ps_R], w=[qraw_R[qb]])
            return qb

        def rope_part2(ps_ap, ps_R, qb, is_k, p, c, col0, csb):
            cs = cs2[:, csb]
            cs_R = cs_R2[csb]
            pr_ap, pr_R = next_pr()
            OP("pe", "matmul", pr_ap, lhsT=perm[:], rhs=qraw[:, qb, :], start=True, stop=True, r=[qraw_R[qb], c_perm], w=[pr_R])
            OP("dve", "tensor_tensor", out=t1, in0=pr_ap, in1=cs[:, 1, :], op=ALU.mult, r=[pr_R, cs_R], w=[t1_R])
            OP("dve", "tensor_tensor", out=t2, in0=ps_ap, in1=cs[:, 0, :], op=ALU.mult, r=[ps_R, cs_R], w=[t2_R])
            if is_k:
                for bb in range(2):
                    n = 2 * c + bb
                    OP("dve", "scalar_tensor_tensor", out=KT[:, p, col0 + bb * 256: col0 + (bb + 1) * 256],
                       in0=t1[:, bb * 256:(bb + 1) * 256], scalar=1.0, in1=t2[:, bb * 256:(bb + 1) * 256],
                       op0=ALU.mult, op1=ALU.add, accum_out=kms[:, p, n:n + 1], r=[t1_R, t2_R], w=[KT_R[p][c], kms_R])
            else:
                OP("dve", "tensor_tensor", out=Q[:, p, col0:col0 + 512], in0=t1, in1=t2, op=ALU.add, r=[t1_R, t2_R], w=[Q_R[p][c]])

        x_tiles = x_loc.rearrange("(n p) d -> n p d", p=128)

        def norm_pieces(c, cb, with_halo):
            pieces = []

            def mk(i):
                def f():
                    if state.get("first_chunk") and i == 3:
                        dma(xh[:], x_tiles[4 * c + i], w=[xh_R])
                        return norm_tile(xh[:], xh_R, 128, xnT[:, cb, :, 32 + 128 * i: 32 + 128 * (i + 1)], xnT_R[cb][i])
                    b = state["ld"] % 3
                    state["ld"] += 1
                    dma(xst[:, b, :], x_tiles[4 * c + i], w=[xst_R[b]])
                    return norm_tile(xst[:, b, :], xst_R[b], 128, xnT[:, cb, :, 32 + 128 * i: 32 + 128 * (i + 1)], xnT_R[cb][i])
                return f

            def halo():
                dma(xh[0:32, :], x_halo[c], w=[xh_R])
                return norm_tile(xh[0:32, :], xh_R, 32, xnT[:, cb, :, 0:32], xnT_R[cb][4])
            if with_halo:
                pieces.append(halo)
            for i in range(4):
                pieces.append(mk(i))
            return pieces

        class Pipe3:
            def __init__(self):
                self.s2 = []
                self.s3 = []

            def tick(self, pending):
                if self.s3:
                    self.s3.pop(0)()
                if self.s2:
                    self.s3.append(self.s2.pop(0)())
                if pending:
                    self.s2.append(pending.pop(0)())

            def drain(self, pending):
                while pending or self.s2 or self.s3:
                    self.tick(pending)

        state["scale_on_pool"] = False
        state["first_chunk"] = True
        stgA = [(R2[:, i * 1024:(i + 1) * 1024].bitcast(F32), [Region()]) for i in range(16)]
        pieces0 = norm_pieces(0, 0, False)
        pipe0 = Pipe3()
        pipe0.tick(pieces0)
        pipe0.tick(pieces0)
        fin_k = load_w_in(WQKV[:, :, 512:1024], 512, 512, lambda kt: [WK_R[kt]], stgA[0:8])
        pipe0.tick(pieces0)
        pipe0.tick(pieces0)
        fin_q = load_w_in(WQKV[:, :, 0:512], 0, 512, lambda kt: [WQ_R[kt]], stgA[8:16])
        fin_k()
        fin_v = load_w_in(WQKV[:, :, 1024:1536], 1024, 512, lambda kt: [WV_R[kt]], stgA[0:8])
        pipe0.drain(pieces0)
        state["scale_on_pool"] = True
        state["first_chunk"] = False
        fin_q()
        fin_v()
        for c in range(8):
            own = c < 4
            cb = c % 2
            col0 = 512 * c
            csb = c % 2
            dma(cs2[:, csb, 0, :], cosT_d[:, col0:col0 + 512], w=[cs_R2[csb]])
            dma(cs2[:, csb, 1, :], sinT_d[:, col0:col0 + 512], w=[cs_R2[csb]])
            if c + 1 < 8:
                pending = norm_pieces(c + 1, (c + 1) % 2, False)
            else:
                pending = norm_pieces(0, 0, True)
            pipeA = Pipe3()
            if c == 6:
                stgB = [(yb[:, 0:2, :].rearrange("p a t -> p (a t)"), [y_R[0], y_R[1]]),
                        (yb[:, 2:4, :].rearrange("p a t -> p (a t)"), [y_R[2], y_R[3]]),
                        (dg[:, 1].rearrange("p a t -> p (a t)").bitcast(F32), [dg_R[1]]),
                        (lnmv, [lnm_R, lnv_R])]
                stgB += [(CVT[:, ct_, :].bitcast(F32), [Region()]) for ct_ in range(4)]
                finish_wb = load_w_in(WB, 1536, 1024, lambda kt: [W_R[kt]], stgB, n_now=4)
            if c == 7:
                finish_wb.more()
            xr_main = xnT_R[cb][0:4]
            jobs = []
            if c == 0:
                jobs = [("k", p) for p in range(4)] + [("q", p) for p in range(4)] + [("v", p) for p in range(4)]
            else:
                for p in range(4):
                    jobs.append(("k", p))
                    if own:
                        jobs.append(("q", p))
                    jobs.append(("v", p))
            prev = None
            for ji, (kind, idx) in enumerate(jobs):
                pa_ap, pa_R = next_pa()
                if kind == "v":
                    i = idx
                    ktile = 4 * c + i
                    for k in range(8):
                        OP("pe", "matmul", pa_ap, lhsT=xnT[:, cb, k, 32 + 128 * i: 32 + 128 * (i + 1)], rhs=WQKV[:, k, 1024:1536],
                           start=(k == 0), stop=(k == 7), r=[WV_R[k], xnT_R[cb][i]], w=[pa_R])
                    src = pa_ap.rearrange("p (h d) -> p h d", h=8)
                    dst = V4[:, ktile, :, 0:64]
                    OP("act", "activation", out=dst, in_=src, func=AF.Copy, r=[pa_R], w=[V_R[ktile]])
                    cur = None
                else:
                    p = idx
                    cb0 = 512 if kind == "k" else 0
                    wr_ = WK_R if kind == "k" else WQ_R
                    for k in range(8):
                        OP("pe", "matmul", pa_ap, lhsT=WQKV[:, k, cb0 + 128 * p: cb0 + 128 * (p + 1)], rhs=xnT[:, cb, k, 32:544],
                           start=(k == 0), stop=(k == 7), r=[wr_[k]] + xr_main, w=[pa_R])
                    qb = rope_part1(pa_ap, pa_R)
                    cur = (pa_ap, pa_R, qb, kind == "k", p, c, col0, csb)
                if prev is not None:
                    rope_part2(*prev)
                prev = cur
                period = 3 if own else 2
                if ji >= 1:
                    pipeA.tick(pending)
            if prev is not None:
                rope_part2(*prev)
            pipeA.drain(pending)

        km_R = Region()
        OP("dve", "tensor_scalar", out=kmf[:], in0=kms[:], scalar1=1.0 / BL, scalar2=None, op0=ALU.mult, r=[kms_R], w=[km_R])
        OP("dve", "tensor_copy", out=kmhi[:], in_=kmf[:], r=[km_R], w=[km_R])
        OP("dve", "tensor_tensor", out=kmlo[:], in0=kmf[:], in1=kmhi[:], op=ALU.subtract, r=[km_R], w=[km_R])
        if debug:
            dma(dbg["kt"], R2[:, 0:16384], r=[KT_R[p][c] for p in range(4) for c in range(8)])
            dma(dbg["q"], Q[:].rearrange("p a t -> p (a t)"), r=[Q_R[p][c] for p in range(4) for c in range(4)])
            dma(dbg["v"], VF, r=V_R + [vones_R])
            dma(dbg["km"], kmf[:].rearrange("p a n -> p (a n)"), r=[km_R])


        class _CarveAt(Carver):
            def __init__(self, off):
                self.off = off
        cvA = _CarveAt(off_cs2)
        scA = dict(biasp=cvA.get([128, 4, 8, 80], BF16), gbias=cvA.get([128, 4, 16], F32), ownind=cvA.get([128, 4, 16], F32),
                   validf=cvA.get([128, 4, 16], F32), km128=cvA.get([128, 2, 8, 16], BF16), gm=cvA.get([128, 8, 16], F32),
                   top=cvA.get([128, 8, 8], F32), selm=cvA.get([128, 8, 16], F32))
        assert cvA.off <= off_cs2 + 8192, cvA.off
        scA["gate4"] = _CarveAt(off_qraw).get([128, 512], F32)
        for nm_ in ("gate4_R", "gm_R", "top_R", "selm_R", "biasp_R", "km_R", "tab_R"):
            scA[nm_] = Region()

        def p2_init_A():
            allA = [scA[n_] for n_ in ("gm_R", "top_R", "selm_R", "biasp_R", "km_R", "tab_R")]
            OP("pool", "memset", cs2.rearrange("p a b t -> p (a b t)"), 0.0, w=[cs_R2[0], cs_R2[1]] + allA)
            OP("pool", "memset", scA["gate4"], 0.0, w=[scA["gate4_R"], qraw_R[0], qraw_R[1]])
            dma(scA["gbias"].rearrange("p a t -> p (a t)"), gb_d[:, 0:64], w=[scA["tab_R"]])
            dma(scA["ownind"].rearrange("p a t -> p (a t)"), own_d[:, 0:64], w=[scA["tab_R"]])
            dma(scA["validf"].rearrange("p a t -> p (a t)"), valid_d[:, 0:64], w=[scA["tab_R"]])
            km4a = scA["km128"][0:64].rearrange("p a (b two) n -> p a b two n", two=2)
            for a_, kmx in enumerate((kmhi, kmlo)):
                OP("dve", "tensor_copy", out=km4a[:, a_, :, 0, :], in_=kmx[0:64, :, :], r=[km_R], w=[scA["km_R"]])
                OP("dve", "tensor_copy", out=km4a[:, a_, :, 1, :], in_=kmx[64:128, :, :], r=[km_R], w=[scA["km_R"]])

        def prep_pieces(c, sc):
            qb = (c + 1) % 2
            col0 = 512 * c
            gate4, gm, top, selm, biasp, km128 = sc["gate4"], sc["gm"], sc["top"], sc["selm"], sc["biasp"], sc["km128"]
            gbias, validf, ownind = sc["gbias"], sc["validf"], sc["ownind"]
            gate4_R, gm_R, top_R, selm_R, biasp_R, km64_R, tab_R = (sc["gate4_R"], sc["gm_R"], sc["top_R"], sc["selm_R"],
                                                                     sc["biasp_R"], sc["km_R"], sc["tab_R"])
            qp4 = QPc[:, qb].rearrange("p (a two) t -> p a two t", two=2)

            def p_q1():
                OP("dve", "tensor_copy", out=qp4[0:64, :, 0, :], in_=Q[0:64, :, col0:col0 + 512], r=[Q_R[p][c] for p in range(4)], w=[QP_R[qb]])
                OP("dve", "tensor_copy", out=qp4[0:64, :, 1, :], in_=Q[64:128, :, col0:col0 + 512], r=[Q_R[p][c] for p in range(4)], w=[QP_R[qb]])

            def p_q():
                for i in range(4):
                    for h in range(NH):
                        for a_ in range(2):
                            OP("pe", "matmul", PTF[:, i * 128 + h * 16: i * 128 + (h + 1) * 16], lhsT=QPc[0:80, qb, h, i * 128:(i + 1) * 128],
                               rhs=km128[0:80, a_, h, :], start=(a_ == 0), stop=(a_ == 1), r=[QP_R[qb], km64_R], w=[PT_R])
                OP("dve", "tensor_copy", out=gate4, in_=PTF[:], r=[PT_R], w=[gate4_R])

            def p_sel(i):
                def f():
                    qt = 4 * c + i
                    OP("dve", "tensor_tensor", out=gm, in0=gate4[:, i * 128:(i + 1) * 128].rearrange("p (h n) -> p h n", h=8),
                       in1=gbias[:, qt, :].unsqueeze(1).to_broadcast([128, 8, 16]), op=ALU.add, r=[gate4_R, tab_R], w=[gm_R])
                    for h in range(NH):
                        OP("dve", "max", out=top[:, h, :], in_=gm[:, h, :], r=[gm_R], w=[top_R])
                    OP("dve", "tensor_tensor", out=selm, in0=gm, in1=top[:, :, 2:3].to_broadcast([128, 8, 16]), op=ALU.is_ge,
                       r=[gm_R, top_R], w=[selm_R])
                    OP("dve", "tensor_tensor", out=selm, in0=selm, in1=validf[:, qt, :].unsqueeze(1).to_broadcast([128, 8, 16]), op=ALU.mult,
                       r=[selm_R, tab_R], w=[selm_R])
                    OP("dve", "tensor_tensor", out=selm, in0=selm, in1=ownind[:, qt, :].unsqueeze(1).to_broadcast([128, 8, 16]), op=ALU.add,
                       r=[selm_R, tab_R], w=[selm_R])
                    OP("dve", "tensor_scalar", out=biasp[:, i, :, 64:80], in0=selm, scalar1=-NEG, scalar2=NEG,
                       op0=ALU.mult, op1=ALU.add, r=[selm_R], w=[biasp_R])
                return f

            def p_bias(h):
                def f():
                    for i in range(4):
                        OP("pe", "matmul", PTF[0:80, i * 128:(i + 1) * 128], lhsT=biasp[:, i, h, :], rhs=ident[:], start=True, stop=True,
                           r=[biasp_R, c_ident], w=[PT_R])
                    OP("dve", "tensor_copy", out=QPc[64:80, qb, h, :], in_=PTF[64:80, :], r=[PT_R], w=[QP_R[qb]])
                return f
            return [p_q1, p_q] + [p_sel(i) for i in range(4)] + [p_bias(h) for h in range(NH)]

        cvK = Carver()
        KA = cvK.get([128, 3, 4096], BF16)
        causal = cvK.get([128, 4, 512], BF16)
        assert cvK.off <= 33792
        KA_R = [Region() for _ in range(3)]
        KAoh_R = [Region() for _ in range(3)]
        causal_R = Region()

        def build_ka(c, h, deps=()):
            buf = (NH * c + h) % 3
            p, eo = h // 2, h % 2
            nk = 512 * (c + 1)
            for k0 in (0, 2048):
                OP("dve", "tensor_copy", out=KA[0:64, buf, k0:k0 + nk], in_=KT[eo * 64:(eo + 1) * 64, p, k0:k0 + nk],
                   r=[KT_R[p][cc] for cc in range(8)], w=[KA_R[buf]], deps=deps)

        fence = OP("dve", "tensor_copy", out=w_dw_bf[:, :, 0:31], in_=w_dw, r=[c_vecs], w=[wdb_R] + WQ_R + WK_R + WV_R)
        state["junk"] = (qraw.rearrange("p a t -> p (a t)"), [qraw_R[0], qraw_R[1]])
        state["scale_on_pool"] = False
        finish_wb(deps=[fence])
        mh512 = mhalf1[:].to_broadcast([128, 512])
        def ln_stats_groups(c, banks=None):
            if banks is None:
                s_ap, s_R = PR[0][:], PR_R[0]
                q_ap, q_R = PR[1][:], PR_R[1]
            else:
                (s_ap, s_R), (q_ap, q_R) = banks

            def mm_s(ct):
                return lambda: OP("pe", "matmul", s_ap, lhsT=onesf[:], rhs=yb[:, ct, :], start=(ct == 0), stop=(ct == 3),
                                  r=[y_R[ct], c_onesf], w=[s_R])

            def sq(ct):
                return lambda: OP("act", "activation", out=t2, in_=yb[:, ct, :], func=AF.Square, r=[y_R[ct]], w=[t2_R])

            def mm_q(ct):
                return lambda: OP("pe", "matmul", q_ap, lhsT=onesf[:], rhs=t2, start=(ct == 0), stop=(ct == 3), r=[t2_R, c_onesf], w=[q_R])
            if banks is not None:
                per_ct = [[mm_s(ct), sq(ct), mm_q(ct)] for ct in range(4)]
                fin = [lambda: OP("dve", "tensor_scalar", out=lnm, in0=s_ap, scalar1=1.0 / 512, scalar2=None, op0=ALU.mult, r=[s_R], w=[lnm_R]),
                       lambda: OP("dve", "tensor_tensor", out=lnv, in0=lnm, in1=lnm, op=ALU.mult, r=[lnm_R], w=[lnv_R]),
                       lambda: OP("dve", "scalar_tensor_tensor", out=lnv, in0=q_ap, scalar=1.0 / 512, in1=lnv, op0=ALU.mult,
                                  op1=ALU.subtract, r=[q_R, lnv_R], w=[lnv_R]),
                       lambda: OP("dve", "tensor_scalar", out=lnv, in0=lnv, scalar1=1.0, scalar2=EPS, op0=ALU.mult, op1=ALU.add,
                                  r=[lnv_R], w=[lnv_R]),
                       lambda: OP("act", "activation", out=lnv, in_=lnv, func=AF.Sqrt, r=[lnv_R], w=[lnv_R]),
                       lambda: OP("dve", "reciprocal", out=lnv, in_=lnv, r=[lnv_R], w=[lnv_R])]
                return per_ct, fin
            g = []
            g.append([mm_s(0), mm_s(1), mm_s(2), mm_s(3), sq(0)])
            g.append([mm_q(0), sq(1)])
            g.append([mm_q(1), sq(2)])
            g.append([mm_q(2), sq(3)])
            g.append([mm_q(3),
                      lambda: OP("dve", "tensor_scalar", out=lnm, in0=s_ap, scalar1=1.0 / 512, scalar2=None, op0=ALU.mult, r=[s_R], w=[lnm_R]),
                      lambda: OP("dve", "tensor_tensor", out=lnv, in0=lnm, in1=lnm, op=ALU.mult, r=[lnm_R], w=[lnv_R])])
            g.append([lambda: OP("dve", "scalar_tensor_tensor", out=lnv, in0=q_ap, scalar=1.0 / 512, in1=lnv, op0=ALU.mult, op1=ALU.subtract,
                                 r=[q_R, lnv_R], w=[lnv_R]),
                      lambda: OP("dve", "tensor_scalar", out=lnv, in0=lnv, scalar1=1.0, scalar2=EPS, op0=ALU.mult, op1=ALU.add,
                                 r=[lnv_R], w=[lnv_R])])
            g.append([lambda: OP("act", "activation", out=lnv, in_=lnv, func=AF.Sqrt, r=[lnv_R], w=[lnv_R])])
            for _ in range(2):
                g.append([])
            g.append([lambda: OP("dve", "reciprocal", out=lnv, in_=lnv, r=[lnv_R], w=[lnv_R])])
            return g

        def ln_norm_ops(c, col0):
            ops = []
            A = ops.append
            for ct in range(4):
                zb, zb_R = ((t2, t2_R), (xh[:, 0:512], xh_R))[ct % 2]
                A(lambda ct=ct, zb=zb, zb_R=zb_R: OP("dve", "tensor_tensor", out=zb, in0=yb[:, ct, :], in1=lnm, op=ALU.subtract,
                                                     r=[y_R[ct], lnm_R], w=[zb_R]))
                A(lambda zb=zb, zb_R=zb_R: OP("dve", "tensor_tensor", out=zb, in0=zb, in1=lnv, op=ALU.mult, r=[zb_R, lnv_R], w=[zb_R]))
                A(lambda ct=ct, zb=zb, zb_R=zb_R: OP("act", "activation", out=CVT[:, ct, col0:col0 + 512], in_=zb, func=AF.Silu,
                                                     bias=b_ln[:, ct:ct + 1], scale=g_ln[:, ct:ct + 1], r=[zb_R, c_vecs],
                                                     w=CV_R[4 * c:4 * c + 4]))
            return ops

        pending_stats = []

        def pop_stats():
            if pending_stats:
                for f in pending_stats.pop(0):
                    f()

        pending_ln = []
        for c in range(4):
            cb = c % 2
            col0 = 512 * c
            pendingB = norm_pieces(c + 1, (c + 1) % 2, True) if c + 1 < 4 else []
            pipeB = Pipe3()
            halves = [(ct_, hf_, k0_, nk_) for ct_ in range(4) for hf_, (k0_, nk_) in enumerate(((0, 16), (16, 15)))]

            def build_diag(ix):
                ct_, hf_, k0_, nk_ = halves[ix]
                OP("dve", "tensor_tensor", out=dg[:, hf_, 0:nk_, :], in0=ident[:].unsqueeze(1).to_broadcast([128, nk_, 128]),
                   in1=w_dw_bf[:, ct_, k0_:k0_ + nk_].unsqueeze(2).to_broadcast([128, nk_, 128]), op=ALU.mult,
                   r=[c_ident, wdb_R], w=[dg_R[hf_]])
            build_diag(0)
            build_diag(1)
            xr_main = xnT_R[cb][0:4]
            for ct in range(4):
                pipeB.tick(pendingB)
                a_ap, a_R = next_pa()
                g_ap, g_R = next_pa()
                for (dst_ap, dst_R, cbase) in ((a_ap, a_R, 0), (g_ap, g_R, 512)):
                    for k in range(8):
                        OP("pe", "matmul", dst_ap, lhsT=WB[:, k, cbase + 128 * ct: cbase + 128 * (ct + 1)], rhs=xnT[:, cb, k, 32:544],
                           start=(k == 0), stop=(k == 7), r=[W_R[k]] + xr_main, w=[dst_R])
                    pop_stats()
                for (off, cbase) in ((0, 0), (32, 512)):
                    for k in range(8):
                        OP("pe", "matmul", PH[:, off:off + 32], lhsT=WB[:, k, cbase + 128 * ct: cbase + 128 * (ct + 1)],
                           rhs=xnT[:, cb, k, 0:32], start=(k == 0), stop=(k == 7), r=[W_R[k], xnT_R[cb][4]], w=[PH_R])
                pop_stats()
                pipeB.tick(pendingB)
                OP("act", "activation", out=t1, in_=g_ap, func=AF.Sigmoid, bias=b_g[:, ct:ct + 1], r=[g_R, c_vecs], w=[t1_R])
                OP("dve", "scalar_tensor_tensor", out=hT[:, ct, 32:544], in0=a_ap, scalar=b_a[:, ct:ct + 1], in1=t1,
                   op0=ALU.add, op1=ALU.mult, r=[a_R, t1_R, c_vecs], w=[hT_R[ct]])
                OP("act", "activation", out=sigh, in_=PH[:, 32:64], func=AF.Sigmoid, bias=b_g[:, ct:ct + 1], r=[PH_R, c_vecs], w=[sigh_R])
                OP("dve", "scalar_tensor_tensor", out=hT[:, ct, 0:32], in0=PH[:, 0:32], scalar=b_a[:, ct:ct + 1], in1=sigh,
                   op0=ALU.add, op1=ALU.mult, r=[PH_R, sigh_R, c_vecs], w=[hT_R[ct]])
                OP("dve", "tensor_scalar", out=hT[:, ct, 0:32], in0=hT[:, ct, 0:32], scalar1=halo_mask[:, c:c + 1], scalar2=None,
                   op0=ALU.mult, r=[hT_R[ct], c_vecs], w=[hT_R[ct]])
            pipeB.drain(pendingB)
            while pending_stats:
                pop_stats()
            if c == 2:
                early = [p2_init_A] + prep_pieces(0, scA)
            elif c < 2:
                early = []
            own_q = []
            if c == 3:
                last_per_ct, last_fin = ln_stats_groups(c, banks=(next_pa(), next_pa()))
            tapn = 0
            for ix, (ct, hf, k0, nk) in enumerate(halves):
                acc_ap, acc_R = PR[ct % 2][:], PR_R[ct % 2]
                for kk in range(nk):
                    k = k0 + kk
                    OP("pe", "matmul", acc_ap, lhsT=dg[:, hf, kk, :], rhs=hT[:, ct, 2 + k:514 + k], start=(k == 0), stop=(k == 30),
                       r=[dg_R[hf], hT_R[ct]], w=[acc_R])
                    if pending_ln and k % 2 == 1:
                        pending_ln.pop(0)()
                    tapn += 1
                    if early and tapn % 16 == 0:
                        early.pop(0)()
                    if own_q and tapn % 4 == 2:
                        own_q.pop(0)()
                if ix + 2 < len(halves):
                    build_diag(ix + 2)
                if hf == 1:
                    if ct == 0:
                        for f in pending_ln:
                            f()
                        pending_ln = []
                    OP("act", "activation", out=yb[:, ct, :], in_=acc_ap, func=AF.Identity, bias=b_dw[:, ct:ct + 1],
                       r=[acc_R, c_vecs], w=[y_R[ct]])
                    if c == 3:
                        own_q.extend(last_per_ct[ct])
            if c == 3:
                for f in own_q:
                    f()
                pending_stats.append(last_fin)
            else:
                pending_stats.extend(ln_stats_groups(c))
            pending_ln = ln_norm_ops(c, col0)
        soft = []
        for e_ in ENGS:
            real = [o for o in S.ops[e_] if o.fn is not None]
            if real:
                soft.append(real[-1])
        dma(causal.rearrange("p a t -> p (a t)"), causal_d, w=[causal_R], deps=soft)
        for b3 in range(3):
            dma(KA[64:80, b3, :], onehot_d, w=[KAoh_R[b3]], deps=soft)
        build_ka(0, 0, deps=soft)
        tail = [f for grp in pending_stats for f in grp] + pending_ln
        del pending_stats[:]
        pending_ln = tail
        while pending_ln or early:
            if pending_ln:
                pending_ln.pop(0)()
            if early:
                early.pop(0)()

        S.barrier()
        cv = cvK
        biasp = cv.get([128, 4, 8, 80], BF16)
        km128 = cv.get([128, 2, 8, 16], BF16)
        km64 = km128[0:64]
        gate4 = cv.get([128, 512], F32)
        gate4_R = Region()
        PTb = cv.get([128, 3, 2, 512], BF16)
        un = cv.get([128, 2, 512], F32)
        gm = cv.get([128, 8, 16], F32)
        top = cv.get([128, 8, 8], F32)
        selm = cv.get([128, 8, 16], F32)
        gbias = cv.get([128, 16, 16], F32)
        ownind = cv.get([128, 16, 16], F32)
        validf = cv.get([128, 16, 16], F32)
        print("phase 2 arena bytes", cv.off)
        PTb_R = [Region() for _ in range(3)]
        un_R = [Region(), Region()]
        gm_R = Region()
        top_R = Region()
        selm_R = Region()
        biasp_R = Region()
        km64_R = Region()
        tab_R = Region()
        dma(gbias.rearrange("p a t -> p (a t)"), gb_d, w=[tab_R])
        dma(ownind.rearrange("p a t -> p (a t)"), own_d, w=[tab_R])
        dma(validf.rearrange("p a t -> p (a t)"), valid_d, w=[tab_R])
        OP("pool", "memset", QPc[:, 0].rearrange("p h t -> p (h t)"), 0.0, w=[QP_R[0]])
        OP("pool", "memset", biasp.rearrange("p a h t -> p (a h t)"), 0.0, w=[biasp_R])
        OP("dve", "memset", km128.rearrange("p a h n -> p (a h n)"), 0.0, w=[km64_R])
        km4 = km64.rearrange("p a (b two) n -> p a b two n", two=2)
        for a_, kmx in enumerate((kmhi, kmlo)):
            OP("dve", "tensor_copy", out=km4[:, a_, :, 0, :], in_=kmx[0:64, :, :], r=[km_R], w=[km64_R])
            OP("dve", "tensor_copy", out=km4[:, a_, :, 1, :], in_=kmx[64:128, :, :], r=[km_R], w=[km64_R])

        scB = dict(gate4=gate4, gm=gm, top=top, selm=selm, biasp=biasp, km128=km128, gbias=gbias, validf=validf, ownind=ownind,
                   gate4_R=gate4_R, gm_R=gm_R, top_R=top_R, selm_R=selm_R, biasp_R=biasp_R, km_R=km64_R, tab_R=tab_R)
        denr = cv.get([128, 2, 512], F32)
        rc4 = cv.get([128, 2, 4], F32)
        rchl = cv.get([128, 2, 2, 4], BF16)
        sel64b = cv.get([128, 128], BF16)
        wo_stg = cv.get([128, 1024], F32)
        wo_stg_R = Region()
        WO = Q[:].rearrange("p a t -> p (a t)").rearrange("p (k c) -> p k c", k=8)
        WO_R = [Region() for _ in range(8)]
        all_Q_R = [Q_R[p_][c_] for p_ in range(4) for c_ in range(4)]
        w_out_v = w_out.rearrange("(k p) c -> p k c", p=128)

        def wout_prefetch_pieces():
            res = []
            for k in range(8):
                def f(k=k):
                    dma(wo_stg, w_out_v[:, k, :], w=[wo_stg_R])
                    OP("dve", "tensor_copy", out=WO[:, k, :], in_=wo_stg, r=[wo_stg_R], w=[WO_R[k]] + all_Q_R)
                res.append(f)
            return res

        denr_R = [Region(), Region()]
        rc_R = [Region(), Region()]
        sel64b_R = Region()
        OP("dve", "tensor_copy", out=sel64b, in_=sel64[:], r=[c_sel64], w=[sel64b_R])

        work = []
        for c in range(4):
            tiles = list(range(4 * (c + 1))) + [16 + t for t in range(2 * N_OTHER_BLOCKS[c])]
            groups = [tiles[g:g + 2] for g in range(0, len(tiles), 2)]
            for h in range(NH):
                for gi, grp in enumerate(groups):
                    work.append((c, h, gi, grp, len(groups)))

        def emit_qk(wi):
            c, h, gi, grp, ngroups = work[wi]
            qb = (c + 1) % 2
            buf = (NH * c + h) % 3
            sb_i = wi % 2
            for s_, kt in enumerate(grp):
                diag = 4 * c <= kt < 4 * c + 4
                sc_ap, sc_R = PA[sb_i][:, s_, :], PA_R[sb_i][s_]
                OP("pe", "matmul", sc_ap, lhsT=KA[0:80, buf, kt * 128:(kt + 1) * 128], rhs=QPc[0:80, qb, h, :], start=True, stop=not diag,
                   r=[KA_R[buf], KAoh_R[buf], QP_R[qb]], w=[sc_R])
                if diag:
                    OP("pe", "matmul", sc_ap, lhsT=ident[:], rhs=causal[:, kt - 4 * c, :], start=False, stop=True,
                       r=[causal_R, c_ident], w=[sc_R])

        def emit_exp(wi):
            c, h, gi, grp, ngroups = work[wi]
            sb_i = wi % 2
            pb = wi % 3
            ng = len(grp)
            OP("act", "activation", out=PTb[:, pb, 0:ng, :], in_=PA[sb_i][:, 0:ng, :], func=AF.Exp, scale=0.125,
               r=PA_R[sb_i][0:ng], w=[PTb_R[pb]])

        def emit_pv(wi):
            c, h, gi, grp, ngroups = work[wi]
            pb = wi % 3
            ng = len(grp)
            ob = (c * NH + h) % 2
            o_ap, o_R = PR[ob][:], PR_R[ob]
            for s_, kt in enumerate(grp):
                first = (gi == 0 and s_ == 0)
                last = (gi == ngroups - 1 and s_ == ng - 1)
                OP("pe", "matmul", o_ap, lhsT=VF[:, kt * VT + h * 65: kt * VT + h * 65 + 128], rhs=PTb[:, pb, s_, :],
                   start=first, stop=last, r=[V_R[kt], vones_R, PTb_R[pb]], w=[o_R])

        def emit_norm1(c, h):
            ob = (c * NH + h) % 2
            o_ap, o_R = PR[ob][:], PR_R[ob]
            ub = h % 2
            OP("dve", "tensor_copy", out=denr[64:65, ub, :], in_=o_ap[64:65, :], r=[o_R], w=[denr_R[ub]])
            OP("dve", "tensor_copy", out=un[0:64, ub, :], in_=o_ap[0:64, :], r=[o_R], w=[un_R[ub]])

        def norm_tail_pieces(c, h):
            p, eo = h // 2, h % 2
            ub = h % 2
            col0 = 512 * c

            def den_mm(j):
                def f():
                    OP("pe", "matmul", PTF[:, j:j + 1], lhsT=denr[64:65, ub, 128 * j:128 * (j + 1)], rhs=onesf[64:65, 0:1],
                       start=True, stop=True, r=[denr_R[ub], c_onesf], w=[PT_R])
                return f

            def recip():
                OP("dve", "reciprocal", out=rc4[:, ub, :], in_=PTF[:, 0:4], r=[PT_R], w=[rc_R[ub]])
                OP("dve", "tensor_copy", out=rchl[:, ub, 0, :], in_=rc4[:, ub, :], r=[rc_R[ub]], w=[rc_R[ub]])
                OP("dve", "tensor_tensor", out=rchl[:, ub, 1, :], in0=rc4[:, ub, :], in1=rchl[:, ub, 0, :], op=ALU.subtract,
                   r=[rc_R[ub]], w=[rc_R[ub]])

            def bc_mm(j):
                def f():
                    for a_ in range(2):
                        OP("pe", "matmul", PH[:, 128 * j:128 * (j + 1)], lhsT=rchl[:, ub, a_, j:j + 1].to_broadcast([128, 128]), rhs=ident[:],
                           start=(a_ == 0), stop=(a_ == 1), r=[rc_R[ub], c_ident], w=[PH_R])
                return f

            def final():
                OP("dve", "tensor_tensor", out=MX[eo * 64:(eo + 1) * 64, p, col0:col0 + 512], in0=un[0:64, ub, :], in1=PH[0:64, :],
                   op=ALU.mult, r=[un_R[ub], PH_R], w=MX_R[4 * c:4 * c + 4])
            def den_all():
                for j in range(4):
                    den_mm(j)()
                recip()
            return [den_all, bc_mm(0), bc_mm(1), bc_mm(2), bc_mm(3), final]

        emit_qk(0)
        emit_qk(1)
        deferred = []
        for wi in range(len(work)):
            c, h, gi, grp, ngroups = work[wi]
            if gi == 0:
                nxt = NH * c + h + 1
                if nxt < 4 * NH:
                    build_ka(nxt // NH, nxt % NH)
            emit_exp(wi)
            if wi + 2 < len(work):
                emit_qk(wi + 2)
            emit_pv(wi)
            nd = []
            for (cnt, fn) in deferred:
                if cnt <= 1:
                    fn()
                else:
                    nd.append((cnt - 1, fn))
            deferred = nd
            if gi == ngroups - 1:
                emit_norm1(c, h)
                tp = norm_tail_pieces(c, h)
                if ngroups >= 8:
                    sched_ = [4, 8, 9, 10, 11, 11]
                else:
                    sched_ = [3, 5, 5, 6, 6, 6]
                for cnt_, fn_ in zip(sched_, tp):
                    deferred.append((cnt_, fn_))
                if h == 0 and c + 1 < 4:
                    if c >= 1:
                        offs = [1, 3, 5, 9, 13, 17] + [21 + 3 * hh for hh in range(NH)]
                    else:
                        offs = [1, 3, 5, 7, 9, 11] + [15 + hh for hh in range(NH)]
                    for off_, piece in zip(offs, prep_pieces(c + 1, scB)):
                        deferred.append((off_, piece))
                if c == 3 and h == 0:
                    for pi, piece in enumerate(wout_prefetch_pieces()):
                        deferred.append((2 + 12 * pi, piece))
        for (cnt, fn) in deferred:
            fn()

        if debug:
            dma(dbg["mx"][:, 0:8192], R1[:, 0:8192], r=MX_R)
            dma(dbg["mx"][:, 8192:16384], CVT[:].rearrange("p a t -> p (a t)"), r=CV_R)
        S.barrier()
        cv = Carver()
        W1B = cv.get([128, 2, 8, 512], BF16)
        W2B = cv.get([128, 2, 4, 1024], BF16)
        wst3 = cv.get([128, 2, 2048], F32)
        FF = cv.get([128, 2, 4, 512], BF16)
        rl = cv.get([128, 2, 512], F32)
        hn = cv.get([128, 3, 1024], BF16)
        junk3 = cv.get([128, 1024], BF16)
        ost = cv.get([128, 1024], F32)
        gfin = cv.get([128, 1024], F32)
        print("phase 3 arena bytes", cv.off)
        W1B_R = [[Region() for _ in range(8)] for _ in range(2)]
        W2B_R = [[Region() for _ in range(4)] for _ in range(2)]
        wst3_R = [Region(), Region()]
        FF_R = [[Region() for _ in range(4)] for _ in range(2)]
        rl_R = [Region(), Region()]
        hn_R = [Region(), Region(), Region()]
        ost_R = Region()
        gfin_R = Region()
        H1_R = [Region() for _ in range(16)]
        HN = MX
        HN_R = [Region() for _ in range(16)]
        dma(gfin, gfin_d, w=[gfin_R])

        cast_i = {"n": 0}

        def cast(out, in_, r, w, scale=None):
            eng = ("dve", "act")[cast_i["n"] % 2]
            cast_i["n"] += 1
            if eng == "dve":
                if scale is None:
                    OP("dve", "tensor_copy", out=out, in_=in_, r=r, w=w)
                else:
                    OP("dve", "tensor_scalar", out=out, in0=in_, scalar1=scale, scalar2=None, op0=ALU.mult, r=r, w=w)
            else:
                if scale is None:
                    OP("act", "activation", out=out, in_=in_, func=AF.Copy, r=r, w=w)
                else:
                    OP("act", "activation", out=out, in_=in_, func=AF.Copy, scale=scale, r=r, w=w)

        stage_i = {"n": 0}

        def stage_dma(src_ap):
            b = stage_i["n"] % 2
            stage_i["n"] += 1
            dma(wst3[:, b, :], src_ap, w=[wst3_R[b]])
            return b

        w_out_v = w_out.rearrange("(k p) c -> p k c", p=128)
        w1_v = w_m1.rearrange("(k p) c -> p k c", p=128)
        w2_v = w_m2.rearrange("(f p) c -> p f c", p=128)

        def wout_pieces():
            res = []
            for k2 in range(4):
                def d(k2=k2):
                    return stage_dma(w_out_v[:, 2 * k2:2 * k2 + 2, :])

                def cfn(b, k2=k2):
                    for kk in range(2):
                        cast(WO[:, 2 * k2 + kk, :], wst3[:, b, kk * 1024:(kk + 1) * 1024], [wst3_R[b]], [WO_R[2 * k2 + kk]])
                res.append((d, cfn))
            return res

        def ffblock_pieces(fb):
            wb = fb % 2
            res = []
            for half in range(2):
                def d(half=half):
                    return stage_dma(w1_v[:, 4 * half:4 * half + 4, fb * 512:(fb + 1) * 512])

                def cfn(b, half=half):
                    for kk in range(4):
                        k = 4 * half + kk
                        cast(W1B[:, wb, k, :], wst3[:, b, kk * 512:(kk + 1) * 512], [wst3_R[b], c_vecs], [W1B_R[wb][k]], scale=g_mlp[:, k:k + 1])
                res.append((d, cfn))
            for half in range(2):
                def d(half=half):
                    return stage_dma(w2_v[:, 4 * fb + 2 * half: 4 * fb + 2 * half + 2, :])

                def cfn(b, half=half):
                    for kk in range(2):
                        f = 2 * half + kk
                        cast(W2B[:, wb, f, :], wst3[:, b, kk * 1024:(kk + 1) * 1024], [wst3_R[b]], [W2B_R[wb][f]])
                res.append((d, cfn))
            return res

        class Loader:
            def __init__(self):
                self.queue = []
                self.inflight = []

            def add(self, pieces):
                self.queue.extend(pieces)
                self.pump()

            def pump(self):
                while self.queue and len(self.inflight) < 2:
                    d, cfn = self.queue.pop(0)
                    self.inflight.append((d(), cfn))

            def tick(self, n=1):
                for _ in range(n):
                    if not self.inflight:
                        return
                    b, cfn = self.inflight.pop(0)
                    cfn(b)
                    self.pump()

            def drain(self):
                while self.inflight:
                    self.tick()

        def h1_view(i):
            return H1[:, i, :].rearrange("p (a d) -> p a d", a=2)

        def rms_stats(i, sl):
            ssq = st4[:, 0, sl:sl + 1]
            vv = st4[:, 1, sl:sl + 1]
            rs = st4[:, 2, sl:sl + 1]
            OP("act", "activation", out=junk3, in_=H1[:, i, :], func=AF.Square, accum_out=ssq, r=[H1_R[i]], w=[st_R[sl], junk_R])
            OP("pool", "tensor_scalar", out=vv, in0=ssq, scalar1=1.0 / D, scalar2=EPS, op0=ALU.mult, op1=ALU.add, r=[st_R[sl]], w=[st_R[sl]])
            OP("pool", "tensor_tensor", out=rs, in0=vv, in1=mhalf1[:], op=ALU.pow, r=[st_R[sl], c_mhalf1], w=[st_R[sl]])
            return rs

        LD = Loader()
        for i in range(16):
            dma(H1[:, i, :], x_tiles[i], w=[H1_R[i]])
        LD.add(ffblock_pieces(0))

        pa_i = 0

        def outproj_mm(i):
            pa = i % 2
            for half in range(2):
                for k in range(8):
                    src = MX[:, k, 128 * i:128 * (i + 1)] if k < 4 else CVT[:, k - 4, 128 * i:128 * (i + 1)]
                    OP("pe", "matmul", PA[pa][:, half, :], lhsT=src, rhs=WO[:, k, half * 512:(half + 1) * 512], start=(k == 0), stop=(k == 7),
                       r=[MX_R[i], CV_R[i], WO_R[k]], w=[PA_R[pa][half]])
            OP("dve", "tensor_tensor", out=h1_view(i), in0=PA[pa][:], in1=h1_view(i), op=ALU.add, r=PA_R[pa] + [H1_R[i]], w=[H1_R[i]])
            sl = i % 4
            rms_stats(i, sl)

        def outproj_scale(i):
            sl = i % 4
            hb = i % 3
            rs = st4[:, 2, sl:sl + 1]
            OP("dve", "tensor_scalar", out=hn[:, hb, :], in0=H1[:, i, :], scalar1=rs, scalar2=None, op0=ALU.mult,
               r=[H1_R[i], st_R[sl]], w=[hn_R[hb]])

        def outproj_tr(i):
            hb = i % 3
            for k in range(8):
                OP("pe", "transpose", out=PT[:, k * 128:(k + 1) * 128], in_=hn[:, hb, k * 128:(k + 1) * 128], identity=ident[:],
                   r=[hn_R[hb], c_ident], w=[PT_R])
            OP("act", "activation", out=HN[:, :, 128 * i:128 * (i + 1)], in_=PT3, func=AF.Copy, r=[PT_R], w=[HN_R[i], MX_R[i]])

        outproj_mm(0)
        outproj_mm(1)
        outproj_scale(0)
        for i in range(16):
            if i + 2 < 16:
                outproj_mm(i + 2)
            if i + 1 < 16:
                outproj_scale(i + 1)
            outproj_tr(i)
            if i % 4 == 3:
                LD.tick()
        LD.drain()
        if debug:
            dma(dbg["h1"], H1.rearrange("p i d -> p (i d)"), r=H1_R)

        items = [(fb, tc) for fb in range(8) for tc in range(4)]
        out_ops = []

        def mlp_in(ii):
            fb, tc = items[ii]
            wb = fb % 2
            fbuf = ii % 2
            for ft in range(4):
                fpr = ft % 2
                for k in range(8):
                    OP("pe", "matmul", PR[fpr][:], lhsT=W1B[:, wb, k, ft * 128:(ft + 1) * 128], rhs=HN[:, k, 512 * tc:512 * (tc + 1)],
                       start=(k == 0), stop=(k == 7), r=[W1B_R[wb][k]] + HN_R[4 * tc:4 * tc + 4], w=[PR_R[fpr]])
                rb = ft % 2
                OP("act", "activation", out=rl[:, rb, :], in_=PR[fpr][:], func=AF.Relu, r=[PR_R[fpr]], w=[rl_R[rb]])
                OP("dve", "tensor_tensor", out=FF[:, fbuf, ft, :], in0=rl[:, rb, :], in1=rl[:, rb, :], op=ALU.mult,
                   r=[rl_R[rb]], w=[FF_R[fbuf][ft]])

        def mlp_out(ii):
            fb, tc = items[ii]
            wb = fb % 2
            fbuf = ii % 2
            for ti in range(4):
                i = 4 * tc + ti
                pa = ti % 2
                for half in range(2):
                    for ft in range(4):
                        OP("pe", "matmul", PA[pa][:, half, :], lhsT=FF[:, fbuf, ft, 128 * ti:128 * (ti + 1)],
                           rhs=W2B[:, wb, ft, half * 512:(half + 1) * 512], start=(ft == 0), stop=(ft == 3),
                           r=[FF_R[fbuf][ft], W2B_R[wb][ft]], w=[PA_R[pa][half]])
                OP("dve", "tensor_tensor", out=h1_view(i), in0=PA[pa][:], in1=h1_view(i), op=ALU.add, r=PA_R[pa] + [H1_R[i]], w=[H1_R[i]])
                if fb == 7:
                    sl = i % 4
                    rs = rms_stats(i, sl)

                    def fin(i=i, sl=sl, rs=rs):
                        OP("dve", "scalar_tensor_tensor", out=H1[:, i, :], in0=H1[:, i, :], scalar=rs, in1=gfin, op0=ALU.mult, op1=ALU.mult,
                           r=[H1_R[i], st_R[sl], gfin_R], w=[H1_R[i]])
                        out_ops.append(dma(y_out[128 * i:128 * (i + 1), :], H1[:, i, :], r=[H1_R[i]]))
                    finals.append(fin)
                    if len(finals) > 2:
                        finals.pop(0)()

        finals = []
        mlp_in(0)
        for ii in range(len(items)):
            fb, tc = items[ii]
            if tc == 0 and fb + 1 < 8:
                LD.add(ffblock_pieces(fb + 1))
            if ii + 1 < len(items):
                if items[ii + 1][1] == 0:
                    LD.drain()
                mlp_in(ii + 1)
            mlp_out(ii)
            LD.tick()
        for f in finals:
            f()
        S.add("sp", None, deps=out_ops + S.dma_since_barrier)
        S.emit(nc)
    return nc


def _core_tables(role):
    own = OWN_BLOCKS[role]
    oth = OWN_BLOCKS[1 - role]
    nat = own + oth
    pos = np.concatenate([np.arange(b * BL, (b + 1) * BL) for b in nat]).astype(np.float32)
    inv_freq = (np.float32(500000.0) ** (-np.arange(8, dtype=np.float32) * np.float32(2.0) / np.float32(16))).astype(np.float32)
    ang = (pos[:, None] * inv_freq[None, :]).astype(np.float32)
    cos = np.cos(ang).astype(np.float32)
    sin = np.sin(ang).astype(np.float32)
    cosT = np.ones((128, SEQ), np.float32)
    sinT = np.zeros((128, SEQ), np.float32)
    for p in range(128):
        d = p % 64
        if d < 8:
            cosT[p] = cos[:, d]
            sinT[p] = -sin[:, d]
        elif d < 16:
            cosT[p] = cos[:, d - 8]
            sinT[p] = sin[:, d - 8]
    gb = np.zeros((16, 16), np.float32)
    ownind = np.zeros((16, 16), np.float32)
    valid = np.zeros((16, 16), np.float32)
    for qt in range(16):
        j = qt // 2
        for n in range(16):
            if nat[n] < own[j]:
                valid[qt, n] = 1.0
            else:
                gb[qt, n] = -1e9
            if n == j:
                ownind[qt, n] = 1.0
    rep = lambda a: np.ascontiguousarray(np.broadcast_to(a.reshape(1, -1), (128, a.size))).astype(np.float32)
    return dict(nat=nat, own=own, cosT=cosT, sinT=sinT, gbias=rep(gb), ownind=rep(ownind), validf=rep(valid))


def _const_tables():
    ident = np.eye(128, dtype=np.float32).astype(ml_dtypes.bfloat16)
    perm = np.zeros((128, 128), np.float32)
    for m in range(128):
        d = m % 64
        if d < 8:
            perm[m + 8, m] = 1.0
        elif d < 16:
            perm[m - 8, m] = 1.0
    causal = np.zeros((128, 4, 512), np.float32)
    for kk in range(4):
        kp = 128 * kk + np.arange(128)[:, None]
        qi = np.arange(512)[None, :]
        causal[:, kk, :] = np.where(kp <= qi, 0.0, NEG)
    sel64 = np.zeros((128, 128), np.float32)
    sel64[64, :] = 1.0
    onehot = np.zeros((16, SEQ), np.float32)
    for n in range(16):
        onehot[n, n * BL:(n + 1) * BL] = 1.0
    return dict(ident=ident, perm=perm.astype(ml_dtypes.bfloat16),
                causal=causal.reshape(128, 2048).astype(ml_dtypes.bfloat16), sel64=sel64,
                onehot=onehot.astype(ml_dtypes.bfloat16))


_NC_CACHE = {}


def kernel(x, g_mix_norm, w_in, b_glu, w_dw, b_dw, g_conv_ln, b_conv_ln, w_out, g_mlp_norm, w_mlp_in, w_mlp_out, g_final,
           _debug=False):
    f = lambda a: np.ascontiguousarray(np.asarray(a, dtype=np.float32))
    x = f(x)
    w_in0, w_out0, w_m1, w_m2 = f(w_in)[0], f(w_out)[0], f(w_mlp_in)[0], f(w_mlp_out)[0]
    vecs = np.zeros((128, NVEC), np.float32)
    vecs[:, 0:8] = f(g_mix_norm)[0].reshape(8, 128).T
    vecs[:, 8:16] = f(g_mlp_norm)[0].reshape(8, 128).T
    bg = f(b_glu)[0]
    vecs[:, 16:20] = bg[0:512].reshape(4, 128).T
    vecs[:, 20:24] = bg[512:1024].reshape(4, 128).T
    vecs[:, 24:28] = f(b_dw)[0].reshape(4, 128).T
    vecs[:, 28:32] = f(g_conv_ln)[0].reshape(4, 128).T
    vecs[:, 32:36] = f(b_conv_ln)[0].reshape(4, 128).T
    wd = f(w_dw)[0, :, 0, :]
    vecs[:, 40:164] = wd.reshape(31, 4, 128).transpose(2, 1, 0).reshape(128, 124)
    gfin = np.ascontiguousarray(np.broadcast_to(f(g_final).reshape(1, D), (128, D)))
    consts = _const_tables()
    tabs = [_core_tables(0), _core_tables(1)]

    in_maps = []
    for core in range(8):
        b, role = core // 2, core % 2
        T = tabs[role]
        xb = x[b]
        x_loc = np.concatenate([xb[n * BL:(n + 1) * BL] for n in T["nat"]], axis=0)
        x_halo = np.zeros((4, 32, D), np.float32)
        v = vecs.copy()
        for c in range(4):
            start = T["own"][2 * c] * BL
            if start > 0:
                x_halo[c] = xb[start - 32:start]
                v[:, 36 + c] = 1.0
        in_maps.append(dict(x_loc=np.ascontiguousarray(x_loc), x_halo=x_halo, w_in=w_in0, w_out=w_out0, w_m1=w_m1, w_m2=w_m2,
                            cosT=T["cosT"], sinT=T["sinT"], vecs=v, gfin=gfin, gbias=T["gbias"], ownind=T["ownind"],
                            validf=T["validf"], **consts))

    key = bool(_debug)
    if key not in _NC_CACHE:
        _NC_CACHE[key] = build_program(debug=_debug)
    nc = _NC_CACHE[key]
    res = run_bass_kernel_spmd(nc, in_maps, core_ids=list(range(8)))
    out = np.zeros((4, SEQ, D), np.float32)
    for core in range(8):
        b, role = core // 2, core % 2
        y = res.results[core]["y"]
        for j, n in enumerate(tabs[role]["own"]):
            out[b, n * BL:(n + 1) * BL] = y[j * BL:(j + 1) * BL]
    if _debug:
        return out, res.results, tabs
    return out
```

```python
from contextlib import ExitStack

import numpy as np
import ml_dtypes

import concourse.bass as bass
import concourse.mybir as mybir
from concourse.bass_utils import run_bass_kernel_spmd

F32 = mybir.dt.float32
BF16 = mybir.dt.bfloat16
AF = mybir.ActivationFunctionType
ALU = mybir.AluOpType
AX = mybir.AxisListType

D = 1024
SEQ = 4096
NBLK = 16
BL = 256
NH = 8
EPS = 1e-6
OWN_BLOCKS = {0: [0, 1, 6, 7, 8, 9, 14, 15], 1: [2, 3, 4, 5, 10, 11, 12, 13]}
N_OTHER_BLOCKS = [2, 4, 6, 8]
NEG = -30000.0
VT = 520
NVEC = 164

ENGS = ("pe", "act", "dve", "pool", "sp")
N_DMA_SEMS = 40


class Region:
    __slots__ = ("writer", "readers")

    def __init__(self):
        self.writer = None
        self.readers = []


class Op:
    __slots__ = ("eng", "fn", "deps", "sig", "needs_sig", "is_dma", "sem", "val", "idx")

    def __init__(self, eng, fn, is_dma):
        self.eng = eng
        self.fn = fn
        self.deps = set()
        self.sig = None
        self.needs_sig = False
        self.is_dma = is_dma
        self.sem = None
        self.val = None


class Sched:
    def __init__(self):
        self.ops = {e: [] for e in ENGS}
        self.all = []
        self.dma_ops = []
        self.dma_since_barrier = []

    def add(self, eng, fn, r=(), w=(), deps=(), dma=False):
        op = Op(eng, fn, dma)
        op.idx = len(self.ops[eng])
        for d in deps:
            if d is not None:
                op.deps.add(d)
        for reg in r:
            if reg.writer is not None:
                op.deps.add(reg.writer)
            reg.readers.append(op)
        for reg in w:
            best = {}
            for rd in reg.readers:
                if rd.is_dma:
                    op.deps.add(rd)
                elif rd.eng not in best or rd.idx > best[rd.eng].idx:
                    best[rd.eng] = rd
            op.deps.update(best.values())
            reg.readers = []
            if reg.writer is not None:
                op.deps.add(reg.writer)
            reg.writer = op
        op.deps.discard(op)
        if dma:
            i = len(self.dma_ops)
            if i >= N_DMA_SEMS:
                op.deps.add(self.dma_ops[i - N_DMA_SEMS])
            op.sem = i % N_DMA_SEMS
            op.val = 16 * (i // N_DMA_SEMS + 1)
            self.dma_ops.append(op)
            self.dma_since_barrier.append(op)
        op.idx = len(self.ops[eng])
        self.ops[eng].append(op)
        self.all.append(op)
        return op

    def barrier(self):
        last = []
        for e in ENGS:
            real = [o for o in self.ops[e] if o.fn is not None]
            if real:
                last.append(real[-1])
        deps = last + self.dma_since_barrier[-4:]
        self.dma_since_barrier = []
        for e in ENGS:
            self.add(e, None, deps=deps)

    def finalize(self):
        for op in self.all:
            for d in op.deps:
                if d.is_dma:
                    continue
                if d.eng == "pe" and op.eng == "pe" and not op.is_dma:
                    continue
                d.needs_sig = True
        for e in ENGS:
            c = 0
            for op in self.ops[e]:
                if op.needs_sig and not op.is_dma:
                    c += 1
                    op.sig = c

    def emit(self, nc):
        self.finalize()
        with ExitStack() as st:
            esem = {e: st.enter_context(nc.semaphore("s_" + e)) for e in ENGS}
            dsem = [st.enter_context(nc.semaphore("d%d" % i)) for i in range(N_DMA_SEMS)]
            block = st.enter_context(nc.Block())

            def run(ename, eng):
                waited = {}
                for op in self.ops[ename]:
                    need = {}
                    for d in op.deps:
                        if d.is_dma:
                            key, sem, val = ("d", d.sem), dsem[d.sem], d.val
                        else:
                            if d.eng == "pe" and ename == "pe" and not op.is_dma:
                                continue
                            key, sem, val = ("e", d.eng), esem[d.eng], d.sig
                        if key not in need or need[key][1] < val:
                            need[key] = (sem, val)
                    for key in sorted(need, key=lambda k: (k[0], str(k[1]))):
                        sem, val = need[key]
                        if waited.get(key, 0) >= val:
                            continue
                        eng.wait_ge(sem, val)
                        waited[key] = val
                    if op.fn is None:
                        continue
                    name, a, kw = op.fn
                    ins = getattr(eng, name)(*a, **kw)
                    if op.is_dma:
                        ins.then_inc(dsem[op.sem], 16)
                    elif op.needs_sig:
                        ins.then_inc(esem[ename], 1)

            @block.tensor
            def _(eng):
                run("pe", eng)

            @block.scalar
            def _(eng):
                run("act", eng)

            @block.vector
            def _(eng):
                run("dve", eng)

            @block.gpsimd
            def _(eng):
                run("pool", eng)

            @block.sync
            def _(eng):
                run("sp", eng)


def build_program(debug=False):
    nc = bass.Bass("TRN2", target_bir_lowering=False)
    S = Sched()

    def OP(eng, name, *a, r=(), w=(), deps=(), **kw):
        return S.add(eng, (name, a, kw), r=r, w=w, deps=deps)

    def dma(out, in_, r=(), w=(), eng="sp", deps=()):
        return S.add(eng, ("dma_start", (), dict(out=out, in_=in_)), r=r, w=w, dma=True, deps=deps)

    def din(name, shape, dt=F32):
        return nc.dram_tensor(name, shape, dt, kind="ExternalInput").ap()

    x_loc = din("x_loc", [SEQ, D])
    x_halo = din("x_halo", [4, 32, D])
    w_in = din("w_in", [D, 2560])
    w_out = din("w_out", [D, D])
    w_m1 = din("w_m1", [D, 4096])
    w_m2 = din("w_m2", [4096, D])
    cosT_d = din("cosT", [128, SEQ])
    sinT_d = din("sinT", [128, SEQ])
    vecs_d = din("vecs", [128, NVEC])
    gfin_d = din("gfin", [128, D])
    gb_d = din("gbias", [128, 256])
    own_d = din("ownind", [128, 256])
    valid_d = din("validf", [128, 256])
    ident_d = din("ident", [128, 128], BF16)
    perm_d = din("perm", [128, 128], BF16)
    causal_d = din("causal", [128, 4 * 512], BF16)
    sel64_d = din("sel64", [128, 128])
    onehot_d = din("onehot", [16, SEQ], BF16)
    y_out = nc.dram_tensor("y", [2048, D], F32, kind="ExternalOutput").ap()
    dbg = {}
    if debug:
        dbg["kt"] = nc.dram_tensor("dbg_kt", [128, 4 * SEQ], BF16, kind="ExternalOutput").ap()
        dbg["q"] = nc.dram_tensor("dbg_q", [128, 4 * 2048], BF16, kind="ExternalOutput").ap()
        dbg["v"] = nc.dram_tensor("dbg_v", [128, 32 * VT + 64], BF16, kind="ExternalOutput").ap()
        dbg["mx"] = nc.dram_tensor("dbg_mx", [128, 8 * 2048], BF16, kind="ExternalOutput").ap()
        dbg["km"] = nc.dram_tensor("dbg_km", [128, 64], F32, kind="ExternalOutput").ap()
        dbg["h1"] = nc.dram_tensor("dbg_h1", [128, 16 * 1024], F32, kind="ExternalOutput").ap()

    with ExitStack() as st:
        def sb(name, shape, dt=F32):
            return st.enter_context(nc.sbuf_tensor("sb_" + name, shape, dt))

        def ps(name, shape, dt=F32):
            return st.enter_context(nc.psum_tensor("ps_" + name, shape, dt))

        R1 = sb("R1", [128, 16384], BF16)
        MX = R1[:].rearrange("p (k t) -> p k t", k=8)
        WQKV = R1[:, 0:12288].rearrange("p (k c) -> p k c", k=8)
        WB = R1[:, 0:8192].rearrange("p (k c) -> p k c", k=8)
        R2 = sb("R2", [128, 16384 + 32 * VT + 64], BF16)
        KT = R2[:, 0:16384].rearrange("p (a t) -> p a t", a=4)
        VF = R2[:, 16384:16384 + 32 * VT + 64]
        V4 = R2[:, 16384:16384 + 32 * VT].rearrange("p (t h e) -> p t h e", h=8, e=65)
        H1 = R2[:, 0:32768].bitcast(F32).rearrange("p (i d) -> p i d", i=16)
        Q = sb("Q", [128, 4, 2048], BF16)
        QPc = R1[:, 8192:16384].rearrange("p (a h t) -> p a h t", a=2, h=8)
        QP_R = [Region(), Region()]
        CVT = sb("CVT", [128, 4, 2048], BF16)
        ident = sb("ident", [128, 128], BF16)
        perm = sb("perm", [128, 128], BF16)
        sel64 = sb("sel64", [128, 128])
        onesf = sb("onesf", [128, 128])
        vecs = sb("vecs", [128, NVEC])
        kms = sb("kms", [128, 4, 16])
        kmhi = sb("kmhi", [128, 4, 16], BF16)
        kmlo = sb("kmlo", [128, 4, 16], BF16)
        kmf = sb("kmf", [128, 4, 16])
        st4 = sb("st4", [128, 3, 4])
        mhalf1 = sb("mhalf1", [128, 1])
        ARENA_BYTES = 77824
        ARENA = sb("ARENA", [128, ARENA_BYTES // 2], BF16)

        g_mix = vecs[:, 0:8]
        g_mlp = vecs[:, 8:16]
        b_a = vecs[:, 16:20]
        b_g = vecs[:, 20:24]
        b_dw = vecs[:, 24:28]
        g_ln = vecs[:, 28:32]
        b_ln = vecs[:, 32:36]
        halo_mask = vecs[:, 36:40]
        w_dw = vecs[:, 40:164].rearrange("p (c k) -> p c k", c=4)

        class Carver:
            def __init__(self):
                self.off = 0

            def get(self, shape, dt):
                n = 1
                for s_ in shape[1:]:
                    n *= s_
                nbytes = n * (2 if dt == BF16 else 4)
                off = self.off
                self.off += (nbytes + 63) // 64 * 64
                assert self.off <= ARENA_BYTES, (self.off, ARENA_BYTES)
                if dt == BF16:
                    ap = ARENA[:, off // 2: off // 2 + n]
                else:
                    ap = ARENA[:, off // 2: off // 2 + 2 * n].bitcast(F32)
                if len(shape) == 2:
                    return ap
                names = " ".join("d%d" % i for i in range(len(shape) - 1))
                kw = {"d%d" % i: shape[i + 1] for i in range(len(shape) - 2)}
                return ap.rearrange("p (%s) -> p %s" % (names, names), **kw)

        PA = [ps("PA%d" % i, [128, 2, 512]) for i in range(2)]
        PR = [ps("PR%d" % i, [128, 512]) for i in range(2)]
        PH = ps("PH", [128, 512])
        PTF = ps("PT", [128, 512])
        PT = PTF[:].bitcast(BF16)
        PA_R = [[Region(), Region()] for _ in range(2)]
        PR_R = [Region(), Region()]
        PH_R = Region()
        PT_R = Region()
        pa_banks = [(PA[i][:, j, :], PA_R[i][j]) for i in range(2) for j in range(2)]
        PT3 = PT.rearrange("p (k t) -> p k t", k=8)

        c_ident, c_perm, c_sel64, c_onesf, c_vecs, c_mhalf1 = [Region() for _ in range(6)]
        dma(ident[:], ident_d, w=[c_ident])
        dma(vecs[:], vecs_d, w=[c_vecs])
        dma(perm[:], perm_d, w=[c_perm])
        dma(sel64[:], sel64_d, w=[c_sel64])
        OP("pool", "memset", onesf[:], 1.0, w=[c_onesf])
        OP("pool", "memset", QPc[:, 1].rearrange("p h t -> p (h t)"), 0.0, w=[QP_R[1]])
        OP("pool", "memset", mhalf1[:], -0.5, w=[c_mhalf1])
        kms_R = Region()
        OP("pool", "memset", kms[:], 0.0, w=[kms_R])

        cv = Carver()
        xnT = cv.get([128, 2, 8, 544], BF16)
        xst = cv.get([128, 3, 1024], F32)
        xh = cv.get([128, 1024], F32)
        xn = cv.get([128, 2, 1024], BF16)
        off_cs2 = cv.off
        cs2 = cv.get([128, 2, 2, 512], F32)
        t12 = cv.get([128, 1024], F32)
        t1 = t12[:, 0:512]
        t2 = t12[:, 512:1024]
        off_qraw = cv.off
        qraw = cv.get([128, 2, 512], BF16)
        hT = cv.get([128, 4, 544], BF16)
        dg = cv.get([128, 2, 16, 128], BF16)
        junk = dg[:, 0, 0:8, :].rearrange("p a t -> p (a t)")
        w_dw_bf = cv.get([128, 4, 32], BF16)
        yb = cv.get([128, 4, 512], F32)
        lnmv = cv.get([128, 1024], F32)
        lnm = lnmv[:, 0:512]
        lnv = lnmv[:, 512:1024]
        sigh = cv.get([128, 32], F32)
        print("pass A/B arena bytes", cv.off)

        xnT_R = [[Region() for _ in range(5)] for _ in range(2)]
        xst_R = [Region() for _ in range(3)]
        xh_R = Region()
        junk_R = Region()
        xn_R = [Region(), Region()]
        st_R = [Region() for _ in range(4)]
        cs_R2 = [Region(), Region()]
        t1_R = Region()
        t2_R = Region()
        qraw_R = [Region(), Region()]
        hT_R = [Region() for _ in range(4)]
        dg_R = [Region(), Region()]
        wdb_R = Region()
        y_R = [Region() for _ in range(4)]
        lnm_R = Region()
        lnv_R = Region()
        sigh_R = Region()
        W_R = [Region() for _ in range(8)]
        WQ_R = [Region() for _ in range(8)]
        WK_R = [Region() for _ in range(8)]
        WV_R = [Region() for _ in range(8)]
        KT_R = [[Region() for _ in range(8)] for _ in range(4)]
        Q_R = [[Region() for _ in range(4)] for _ in range(4)]
        V_R = [Region() for _ in range(32)]
        MX_R = [Region() for _ in range(16)]
        CV_R = [Region() for _ in range(16)]
        vones_R = Region()
        OP("pool", "memset", VF[:, 0:32 * VT].rearrange("p (n e) -> p n e", e=65)[:, :, 64:65], 1.0, w=[vones_R])
        OP("pool", "memset", VF[:, 32 * VT:32 * VT + 64], 0.0, w=[vones_R])

        state = {"tile": 0, "pa": 0, "pr": 0, "qraw": 0, "evac": 0, "ld": 0, "cast": 0, "junk": (junk, [dg_R[0]])}

        def next_pa():
            i = state["pa"] % 4
            state["pa"] += 1
            return pa_banks[i]

        def next_pr():
            i = state["pr"] % 2
            state["pr"] += 1
            return PR[i][:], PR_R[i]

        w_in_v = w_in.rearrange("(k p) c -> p k c", p=128)

        def load_w_in(dst, c0, ncols, wregs_of, stg, n_now=8):
            ns = len(stg)

            def issue(kt):
                st_ap, st_regs = stg[kt % ns]
                dma(st_ap, w_in_v[:, kt, c0:c0 + ncols], w=st_regs)
            for kt in range(min(ns, 8, n_now)):
                issue(kt)

            def finish(deps=()):
                for kt in range(8):
                    st_ap, st_regs = stg[kt % ns]
                    eng = ("dve", "act")[kt % 2]
                    if eng == "act":
                        OP("act", "activation", out=dst[:, kt, :], in_=st_ap, func=AF.Copy, scale=g_mix[:, kt:kt + 1],
                           r=st_regs + [c_vecs], w=wregs_of(kt), deps=deps)
                    else:
                        OP(eng, "tensor_scalar", out=dst[:, kt, :], in0=st_ap, scalar1=g_mix[:, kt:kt + 1], scalar2=None,
                           op0=ALU.mult, r=st_regs + [c_vecs], w=wregs_of(kt), deps=deps)
                    if kt + ns < 8:
                        issue(kt + ns)
            finish.more = lambda: [issue(kt) for kt in range(min(ns, 8, n_now), min(ns, 8))]
            return finish

        def norm_tile(src, src_R, npart, dst_cols, dst_R):
            j = state["tile"]
            state["tile"] += 1
            sl = j % 4
            xb = j % 2
            ssq = st4[0:npart, 0, sl:sl + 1]
            vv = st4[0:npart, 1, sl:sl + 1]
            rs = st4[0:npart, 2, sl:sl + 1]
            jk, jk_regs = state["junk"]
            OP("act", "activation", out=jk[0:npart, :], in_=src, func=AF.Square, accum_out=ssq, r=[src_R], w=[st_R[sl]] + jk_regs)
            OP("pool", "tensor_scalar", out=vv, in0=ssq, scalar1=1.0 / D, scalar2=EPS, op0=ALU.mult, op1=ALU.add,
               r=[st_R[sl]], w=[st_R[sl]])
            OP("pool", "tensor_tensor", out=rs, in0=vv, in1=mhalf1[0:npart, :], op=ALU.pow, r=[st_R[sl], c_mhalf1], w=[st_R[sl]])

            def part_a2():
                if state.get("scale_on_pool"):
                    OP("pool", "tensor_scalar", out=xn[0:npart, xb, :], in0=src, scalar1=rs, scalar2=1.0, op0=ALU.mult, op1=ALU.mult,
                       r=[src_R, st_R[sl]], w=[xn_R[xb]])
                else:
                    OP("dve", "tensor_scalar", out=xn[0:npart, xb, :], in0=src, scalar1=rs, scalar2=None, op0=ALU.mult,
                       r=[src_R, st_R[sl]], w=[xn_R[xb]])
                return part_b

            def part_b():
                for k in range(8):
                    OP("pe", "transpose", out=PT[:, k * 128:k * 128 + npart], in_=xn[0:npart, xb, k * 128:(k + 1) * 128],
                       identity=ident[0:npart, 0:npart], r=[xn_R[xb], c_ident], w=[PT_R])
                src_ps = PT3[:, :, 0:npart]
                if state["evac"] % 2 == 0:
                    OP("act", "activation", out=dst_cols, in_=src_ps, func=AF.Copy, r=[PT_R], w=[dst_R])
                else:
                    OP("dve", "tensor_copy", out=dst_cols, in_=src_ps, r=[PT_R], w=[dst_R])
                state["evac"] += 1
            return part_a2

        def rope_part1(ps_ap, ps_R):
            qb = state["qraw"] % 2
            state["qraw"] += 1
            OP("act", "activation", out=qraw[:, qb, :], in_=ps_ap, func=AF.Copy, r=[ps_R], w=[qraw_R[qb]])
            return qb

        def rope_part2(ps_ap, ps_R, qb, is_k, p, c, col0, csb):
            cs = cs2[:, csb]
            cs_R = cs_R2[csb]
            pr_ap, pr_R = next_pr()
            OP("pe", "matmul", pr_ap, lhsT=perm[:], rhs=qraw[:, qb, :], start=True, stop=True, r=[qraw_R[qb], c_perm], w=[pr_R])
            OP("dve", "tensor_tensor", out=t1, in0=pr_ap, in1=cs[:, 1, :], op=ALU.mult, r=[pr_R, cs_R], w=[t1_R])
            OP("dve", "tensor_tensor", out=t2, in0=ps_ap, in1=cs[:, 0, :], op=ALU.mult, r=[ps_R, cs_R], w=[t2_R])
            if is_k:
                for bb in range(2):
                    n = 2 * c + bb
                    OP("dve", "scalar_tensor_tensor", out=KT[:, p, col0 + bb * 256: col0 + (bb + 1) * 256],
                       in0=t1[:, bb * 256:(bb + 1) * 256], scalar=1.0, in1=t2[:, bb * 256:(bb + 1) * 256],
                       op0=ALU.mult, op1=ALU.add, accum_out=kms[:, p, n:n + 1], r=[t1_R, t2_R], w=[KT_R[p][c], kms_R])
            else:
                OP("dve", "tensor_tensor", out=Q[:, p, col0:col0 + 512], in0=t1, in1=t2, op=ALU.add, r=[t1_R, t2_R], w=[Q_R[p][c]])

        x_tiles = x_loc.rearrange("(n p) d -> n p d", p=128)

        def norm_pieces(c, cb, with_halo):
            pieces = []

            def mk(i):
                def f():
                    if state.get("first_chunk") and i == 3:
                        dma(xh[:], x_tiles[4 * c + i], w=[xh_R])
                        return norm_tile(xh[:], xh_R, 128, xnT[:, cb, :, 32 + 128 * i: 32 + 128 * (i + 1)], xnT_R[cb][i])
                    b = state["ld"] % 3
                    state["ld"] += 1
                    dma(xst[:, b, :], x_tiles[4 * c + i], w=[xst_R[b]])
                    return norm_tile(xst[:, b, :], xst_R[b], 128, xnT[:, cb, :, 32 + 128 * i: 32 + 128 * (i + 1)], xnT_R[cb][i])
                return f

            def halo():
                dma(xh[0:32, :], x_halo[c], w=[xh_R])
                return norm_tile(xh[0:32, :], xh_R, 32, xnT[:, cb, :, 0:32], xnT_R[cb][4])
            if with_halo:
                pieces.append(halo)
            for i in range(4):
                pieces.append(mk(i))
            return pieces

        class Pipe3:
            def __init__(self):
                self.s2 = []
                self.s3 = []

            def tick(self, pending):
                if self.s3:
                    self.s3.pop(0)()
                if self.s2:
                    self.s3.append(self.s2.pop(0)())
                if pending:
                    self.s2.append(pending.pop(0)())

            def drain(self, pending):
                while pending or self.s2 or self.s3:
                    self.tick(pending)

        state["scale_on_pool"] = False
        state["first_chunk"] = True
        stgA = [(R2[:, i * 1024:(i + 1) * 1024].bitcast(F32), [Region()]) for i in range(16)]
        pieces0 = norm_pieces(0, 0, False)
        pipe0 = Pipe3()
        pipe0.tick(pieces0)
        pipe0.tick(pieces0)
        fin_k = load_w_in(WQKV[:, :, 512:1024], 512, 512, lambda kt: [WK_R[kt]], stgA[0:8])
        pipe0.tick(pieces0)
        pipe0.tick(pieces0)
        fin_q = load_w_in(WQKV[:, :, 0:512], 0, 512, lambda kt: [WQ_R[kt]], stgA[8:16])
        fin_k()
        fin_v = load_w_in(WQKV[:, :, 1024:1536], 1024, 512, lambda kt: [WV_R[kt]], stgA[0:8])
        pipe0.drain(pieces0)
        state["scale_on_pool"] = True
        state["first_chunk"] = False
        fin_q()
        fin_v()
        for c in range(8):
            own = c < 4
            cb = c % 2
            col0 = 512 * c
            csb = c % 2
            dma(cs2[:, csb, 0, :], cosT_d[:, col0:col0 + 512], w=[cs_R2[csb]])
            dma(cs2[:, csb, 1, :], sinT_d[:, col0:col0 + 512], w=[cs_R2[csb]])
            if c + 1 < 8:
                pending = norm_pieces(c + 1, (c + 1) % 2, False)
            else:
                pending = norm_pieces(0, 0, True)
            pipeA = Pipe3()
            if c == 6:
                stgB = [(yb[:, 0:2, :].rearrange("p a t -> p (a t)"), [y_R[0], y_R[1]]),
                        (yb[:, 2:4, :].rearrange("p a t -> p (a t)"), [y_R[2], y_R[3]]),
                        (dg[:, 1].rearrange("p a t -> p (a t)").bitcast(F32), [dg_R[1]]),
                        (lnmv, [lnm_R, lnv_R])]
                stgB += [(CVT[:, ct_, :].bitcast(F32), [Region()]) for ct_ in range(4)]
                finish_wb = load_w_in(WB, 1536, 1024, lambda kt: [W_R[kt]], stgB, n_now=4)
            if c == 7:
                finish_wb.more()
            xr_main = xnT_R[cb][0:4]
            jobs = []
            if c == 0:
                jobs = [("k", p) for p in range(4)] + [("q", p) for p in range(4)] + [("v", p) for p in range(4)]
            else:
                for p in range(4):
                    jobs.append(("k", p))
                    if own:
                        jobs.append(("q", p))
                    jobs.append(("v", p))
            prev = None
            for ji, (kind, idx) in enumerate(jobs):
                pa_ap, pa_R = next_pa()
                if kind == "v":
                    i = idx
                    ktile = 4 * c + i
                    for k in range(8):
                        OP("pe", "matmul", pa_ap, lhsT=xnT[:, cb, k, 32 + 128 * i: 32 + 128 * (i + 1)], rhs=WQKV[:, k, 1024:1536],
                           start=(k == 0), stop=(k == 7), r=[WV_R[k], xnT_R[cb][i]], w=[pa_R])
                    src = pa_ap.rearrange("p (h d) -> p h d", h=8)
                    dst = V4[:, ktile, :, 0:64]
                    OP("act", "activation", out=dst, in_=src, func=AF.Copy, r=[pa_R], w=[V_R[ktile]])
                    cur = None
                else:
                    p = idx
                    cb0 = 512 if kind == "k" else 0
                    wr_ = WK_R if kind == "k" else WQ_R
                    for k in range(8):
                        OP("pe", "matmul", pa_ap, lhsT=WQKV[:, k, cb0 + 128 * p: cb0 + 128 * (p + 1)], rhs=xnT[:, cb, k, 32:544],
                           start=(k == 0), stop=(k == 7), r=[wr_[k]] + xr_main, w=[pa_R])
                    qb = rope_part1(pa_ap, pa_R)
                    cur = (pa_ap, pa_R, qb, kind == "k", p, c, col0, csb)
                if prev is not None:
                    rope_part2(*prev)
                prev = cur
                period = 3 if own else 2
                if ji >= 1:
                    pipeA.tick(pending)
            if prev is not None:
                rope_part2(*prev)
            pipeA.drain(pending)

        km_R = Region()
        OP("dve", "tensor_scalar", out=kmf[:], in0=kms[:], scalar1=1.0 / BL, scalar2=None, op0=ALU.mult, r=[kms_R], w=[km_R])
        OP("dve", "tensor_copy", out=kmhi[:], in_=kmf[:], r=[km_R], w=[km_R])
        OP("dve", "tensor_tensor", out=kmlo[:], in0=kmf[:], in1=kmhi[:], op=ALU.subtract, r=[km_R], w=[km_R])
        if debug:
            dma(dbg["kt"], R2[:, 0:16384], r=[KT_R[p][c] for p in range(4) for c in range(8)])
            dma(dbg["q"], Q[:].rearrange("p a t -> p (a t)"), r=[Q_R[p][c] for p in range(4) for c in range(4)])
            dma(dbg["v"], VF, r=V_R + [vones_R])
            dma(dbg["km"], kmf[:].rearrange("p a n -> p (a n)"), r=[km_R])


        class _CarveAt(Carver):
            def __init__(self, off):
                self.off = off
        cvA = _CarveAt(off_cs2)
        scA = dict(biasp=cvA.get([128, 4, 8, 80], BF16), gbias=cvA.get([128, 4, 16], F32), ownind=cvA.get([128, 4, 16], F32),
                   validf=cvA.get([128, 4, 16], F32), km128=cvA.get([128, 2, 8, 16], BF16), gm=cvA.get([128, 8, 16], F32),
                   top=cvA.get([128, 8, 8], F32), selm=cvA.get([128, 8, 16], F32))
        assert cvA.off <= off_cs2 + 8192, cvA.off
        scA["gate4"] = _CarveAt(off_qraw).get([128, 512], F32)
        for nm_ in ("gate4_R", "gm_R", "top_R", "selm_R", "biasp_R", "km_R", "tab_R"):
            scA[nm_] = Region()

        def p2_init_A():
            allA = [scA[n_] for n_ in ("gm_R", "top_R", "selm_R", "biasp_R", "km_R", "tab_R")]
            OP("pool", "memset", cs2.rearrange("p a b t -> p (a b t)"), 0.0, w=[cs_R2[0], cs_R2[1]] + allA)
            OP("pool", "memset", scA["gate4"], 0.0, w=[scA["gate4_R"], qraw_R[0], qraw_R[1]])
            dma(scA["gbias"].rearrange("p a t -> p (a t)"), gb_d[:, 0:64], w=[scA["tab_R"]])
            dma(scA["ownind"].rearrange("p a t -> p (a t)"), own_d[:, 0:64], w=[scA["tab_R"]])
            dma(scA["validf"].rearrange("p a t -> p (a t)"), valid_d[:, 0:64], w=[scA["tab_R"]])
            km4a = scA["km128"][0:64].rearrange("p a (b two) n -> p a b two n", two=2)
            for a_, kmx in enumerate((kmhi, kmlo)):
                OP("dve", "tensor_copy", out=km4a[:, a_, :, 0, :], in_=kmx[0:64, :, :], r=[km_R], w=[scA["km_R"]])
                OP("dve", "tensor_copy", out=km4a[:, a_, :, 1, :], in_=kmx[64:128, :, :], r=[km_R], w=[scA["km_R"]])

        def prep_pieces(c, sc):
            qb = (c + 1) % 2
            col0 = 512 * c
            gate4, gm, top, selm, biasp, km128 = sc["gate4"], sc["gm"], sc["top"], sc["selm"], sc["biasp"], sc["km128"]
            gbias, validf, ownind = sc["gbias"], sc["validf"], sc["ownind"]
            gate4_R, gm_R, top_R, selm_R, biasp_R, km64_R, tab_R = (sc["gate4_R"], sc["gm_R"], sc["top_R"], sc["selm_R"],
                                                                     sc["biasp_R"], sc["km_R"], sc["tab_R"])
            qp4 = QPc[:, qb].rearrange("p (a two) t -> p a two t", two=2)

            def p_q1():
                OP("dve", "tensor_copy", out=qp4[0:64, :, 0, :], in_=Q[0:64, :, col0:col0 + 512], r=[Q_R[p][c] for p in range(4)], w=[QP_R[qb]])
                OP("dve", "tensor_copy", out=qp4[0:64, :, 1, :], in_=Q[64:128, :, col0:col0 + 512], r=[Q_R[p][c] for p in range(4)], w=[QP_R[qb]])

            def p_q():
                for i in range(4):
                    for h in range(NH):
                        for a_ in range(2):
                            OP("pe", "matmul", PTF[:, i * 128 + h * 16: i * 128 + (h + 1) * 16], lhsT=QPc[0:80, qb, h, i * 128:(i + 1) * 128],
                               rhs=km128[0:80, a_, h, :], start=(a_ == 0), stop=(a_ == 1), r=[QP_R[qb], km64_R], w=[PT_R])
                OP("dve", "tensor_copy", out=gate4, in_=PTF[:], r=[PT_R], w=[gate4_R])

            def p_sel(i):
                def f():
                    qt = 4 * c + i
                    OP("dve", "tensor_tensor", out=gm, in0=gate4[:, i * 128:(i + 1) * 128].rearrange("p (h n) -> p h n", h=8),
                       in1=gbias[:, qt, :].unsqueeze(1).to_broadcast([128, 8, 16]), op=ALU.add, r=[gate4_R, tab_R], w=[gm_R])
                    for h in range(NH):
                        OP("dve", "max", out=top[:, h, :], in_=gm[:, h, :], r=[gm_R], w=[top_R])
                    OP("dve", "tensor_tensor", out=selm, in0=gm, in1=top[:, :, 2:3].to_broadcast([128, 8, 16]), op=ALU.is_ge,
                       r=[gm_R, top_R], w=[selm_R])
                    OP("dve", "tensor_tensor", out=selm, in0=selm, in1=validf[:, qt, :].unsqueeze(1).to_broadcast([128, 8, 16]), op=ALU.mult,
                       r=[selm_R, tab_R], w=[selm_R])
                    OP("dve", "tensor_tensor", out=selm, in0=selm, in1=ownind[:, qt, :].unsqueeze(1).to_broadcast([128, 8, 16]), op=ALU.add,
                       r=[selm_R, tab_R], w=[selm_R])
                    OP("dve", "tensor_scalar", out=biasp[:, i, :, 64:80], in0=selm, scalar1=-NEG, scalar2=NEG,
                       op0=ALU.mult, op1=ALU.add, r=[selm_R], w=[biasp_R])
                return f

            def p_bias(h):
                def f():
                    for i in range(4):
                        OP("pe", "matmul", PTF[0:80, i * 128:(i + 1) * 128], lhsT=biasp[:, i, h, :], rhs=ident[:], start=True, stop=True,
                           r=[biasp_R, c_ident], w=[PT_R])
                    OP("dve", "tensor_copy", out=QPc[64:80, qb, h, :], in_=PTF[64:80, :], r=[PT_R], w=[QP_R[qb]])
                return f
            return [p_q1, p_q] + [p_sel(i) for i in range(4)] + [p_bias(h) for h in range(NH)]

        cvK = Carver()
        KA = cvK.get([128, 3, 4096], BF16)
        causal = cvK.get([128, 4, 512], BF16)
        assert cvK.off <= 33792
        KA_R = [Region() for _ in range(3)]
        KAoh_R = [Region() for _ in range(3)]
        causal_R = Region()

        def build_ka(c, h, deps=()):
            buf = (NH * c + h) % 3
            p, eo = h // 2, h % 2
            nk = 512 * (c + 1)
            for k0 in (0, 2048):
                OP("dve", "tensor_copy", out=KA[0:64, buf, k0:k0 + nk], in_=KT[eo * 64:(eo + 1) * 64, p, k0:k0 + nk],
                   r=[KT_R[p][cc] for cc in range(8)], w=[KA_R[buf]], deps=deps)

        fence = OP("dve", "tensor_copy", out=w_dw_bf[:, :, 0:31], in_=w_dw, r=[c_vecs], w=[wdb_R] + WQ_R + WK_R + WV_R)
        state["junk"] = (qraw.rearrange("p a t -> p (a t)"), [qraw_R[0], qraw_R[1]])
        state["scale_on_pool"] = False
        finish_wb(deps=[fence])
        mh512 = mhalf1[:].to_broadcast([128, 512])
        def ln_stats_groups(c, banks=None):
            if banks is None:
                s_ap, s_R = PR[0][:], PR_R[0]
                q_ap, q_R = PR[1][:], PR_R[1]
            else:
                (s_ap, s_R), (q_ap, q_R) = banks

            def mm_s(ct):
                return lambda: OP("pe", "matmul", s_ap, lhsT=onesf[:], rhs=yb[:, ct, :], start=(ct == 0), stop=(ct == 3),
                                  r=[y_R[ct], c_onesf], w=[s_R])

            def sq(ct):
                return lambda: OP("act", "activation", out=t2, in_=yb[:, ct, :], func=AF.Square, r=[y_R[ct]], w=[t2_R])

            def mm_q(ct):
                return lambda: OP("pe", "matmul", q_ap, lhsT=onesf[:], rhs=t2, start=(ct == 0), stop=(ct == 3), r=[t2_R, c_onesf], w=[q_R])
            if banks is not None:
                per_ct = [[mm_s(ct), sq(ct), mm_q(ct)] for ct in range(4)]
                fin = [lambda: OP("dve", "tensor_scalar", out=lnm, in0=s_ap, scalar1=1.0 / 512, scalar2=None, op0=ALU.mult, r=[s_R], w=[lnm_R]),
                       lambda: OP("dve", "tensor_tensor", out=lnv, in0=lnm, in1=lnm, op=ALU.mult, r=[lnm_R], w=[lnv_R]),
                       lambda: OP("dve", "scalar_tensor_tensor", out=lnv, in0=q_ap, scalar=1.0 / 512, in1=lnv, op0=ALU.mult,
                                  op1=ALU.subtract, r=[q_R, lnv_R], w=[lnv_R]),
                       lambda: OP("dve", "tensor_scalar", out=lnv, in0=lnv, scalar1=1.0, scalar2=EPS, op0=ALU.mult, op1=ALU.add,
                                  r=[lnv_R], w=[lnv_R]),
                       lambda: OP("act", "activation", out=lnv, in_=lnv, func=AF.Sqrt, r=[lnv_R], w=[lnv_R]),
                       lambda: OP("dve", "reciprocal", out=lnv, in_=lnv, r=[lnv_R], w=[lnv_R])]
                return per_ct, fin
            g = []
            g.append([mm_s(0), mm_s(1), mm_s(2), mm_s(3), sq(0)])
            g.append([mm_q(0), sq(1)])
            g.append([mm_q(1), sq(2)])
            g.append([mm_q(2), sq(3)])
            g.append([mm_q(3),
                      lambda: OP("dve", "tensor_scalar", out=lnm, in0=s_ap, scalar1=1.0 / 512, scalar2=None, op0=ALU.mult, r=[s_R], w=[lnm_R]),
                      lambda: OP("dve", "tensor_tensor", out=lnv, in0=lnm, in1=lnm, op=ALU.mult, r=[lnm_R], w=[lnv_R])])
            g.append([lambda: OP("dve", "scalar_tensor_tensor", out=lnv, in0=q_ap, scalar=1.0 / 512, in1=lnv, op0=ALU.mult, op1=ALU.subtract,
                                 r=[q_R, lnv_R], w=[lnv_R]),
                      lambda: OP("dve", "tensor_scalar", out=lnv, in0=lnv, scalar1=1.0, scalar2=EPS, op0=ALU.mult, op1=ALU.add,
                                 r=[lnv_R], w=[lnv_R])])
            g.append([lambda: OP("act", "activation", out=lnv, in_=lnv, func=AF.Sqrt, r=[lnv_R], w=[lnv_R])])
            for _ in range(2):
                g.append([])
            g.append([lambda: OP("dve", "reciprocal", out=lnv, in_=lnv, r=[lnv_R], w=[lnv_R])])
            return g

        def ln_norm_ops(c, col0):
            ops = []
            A = ops.append
            for ct in range(4):
                zb, zb_R = ((t2, t2_R), (xh[:, 0:512], xh_R))[ct % 2]
                A(lambda ct=ct, zb=zb, zb_R=zb_R: OP("dve", "tensor_tensor", out=zb, in0=yb[:, ct, :], in1=lnm, op=ALU.subtract,
                                                     r=[y_R[ct], lnm_R], w=[zb_R]))
                A(lambda zb=zb, zb_R=zb_R: OP("dve", "tensor_tensor", out=zb, in0=zb, in1=lnv, op=ALU.mult, r=[zb_R, lnv_R], w=[zb_R]))
                A(lambda ct=ct, zb=zb, zb_R=zb_R: OP("act", "activation", out=CVT[:, ct, col0:col0 + 512], in_=zb, func=AF.Silu,
                                                     bias=b_ln[:, ct:ct + 1], scale=g_ln[:, ct:ct + 1], r=[zb_R, c_vecs],
                                                     w=CV_R[4 * c:4 * c + 4]))
            return ops

        pending_stats = []

        def pop_stats():
            if pending_stats:
                for f in pending_stats.pop(0):
                    f()

        pending_ln = []
        for c in range(4):
            cb = c % 2
            col0 = 512 * c
            pendingB = norm_pieces(c + 1, (c + 1) % 2, True) if c + 1 < 4 else []
            pipeB = Pipe3()
            halves = [(ct_, hf_, k0_, nk_) for ct_ in range(4) for hf_, (k0_, nk_) in enumerate(((0, 16), (16, 15)))]

            def build_diag(ix):
                ct_, hf_, k0_, nk_ = halves[ix]
                OP("dve", "tensor_tensor", out=dg[:, hf_, 0:nk_, :], in0=ident[:].unsqueeze(1).to_broadcast([128, nk_, 128]),
                   in1=w_dw_bf[:, ct_, k0_:k0_ + nk_].unsqueeze(2).to_broadcast([128, nk_, 128]), op=ALU.mult,
                   r=[c_ident, wdb_R], w=[dg_R[hf_]])
            build_diag(0)
            build_diag(1)
            xr_main = xnT_R[cb][0:4]
            for ct in range(4):
                pipeB.tick(pendingB)
                a_ap, a_R = next_pa()
                g_ap, g_R = next_pa()
                for (dst_ap, dst_R, cbase) in ((a_ap, a_R, 0), (g_ap, g_R, 512)):
                    for k in range(8):
                        OP("pe", "matmul", dst_ap, lhsT=WB[:, k, cbase + 128 * ct: cbase + 128 * (ct + 1)], rhs=xnT[:, cb, k, 32:544],
                           start=(k == 0), stop=(k == 7), r=[W_R[k]] + xr_main, w=[dst_R])
                    pop_stats()
                for (off, cbase) in ((0, 0), (32, 512)):
                    for k in range(8):
                        OP("pe", "matmul", PH[:, off:off + 32], lhsT=WB[:, k, cbase + 128 * ct: cbase + 128 * (ct + 1)],
                           rhs=xnT[:, cb, k, 0:32], start=(k == 0), stop=(k == 7), r=[W_R[k], xnT_R[cb][4]], w=[PH_R])
                pop_stats()
                pipeB.tick(pendingB)
                OP("act", "activation", out=t1, in_=g_ap, func=AF.Sigmoid, bias=b_g[:, ct:ct + 1], r=[g_R, c_vecs], w=[t1_R])
                OP("dve", "scalar_tensor_tensor", out=hT[:, ct, 32:544], in0=a_ap, scalar=b_a[:, ct:ct + 1], in1=t1,
                   op0=ALU.add, op1=ALU.mult, r=[a_R, t1_R, c_vecs], w=[hT_R[ct]])
                OP("act", "activation", out=sigh, in_=PH[:, 32:64], func=AF.Sigmoid, bias=b_g[:, ct:ct + 1], r=[PH_R, c_vecs], w=[sigh_R])
                OP("dve", "scalar_tensor_tensor", out=hT[:, ct, 0:32], in0=PH[:, 0:32], scalar=b_a[:, ct:ct + 1], in1=sigh,
                   op0=ALU.add, op1=ALU.mult, r=[PH_R, sigh_R, c_vecs], w=[hT_R[ct]])
                OP("dve", "tensor_scalar", out=hT[:, ct, 0:32], in0=hT[:, ct, 0:32], scalar1=halo_mask[:, c:c + 1], scalar2=None,
                   op0=ALU.mult, r=[hT_R[ct], c_vecs], w=[hT_R[ct]])
            pipeB.drain(pendingB)
            while pending_stats:
                pop_stats()
            if c == 2:
                early = [p2_init_A] + prep_pieces(0, scA)
            elif c < 2:
                early = []
            own_q = []
            if c == 3:
                last_per_ct, last_fin = ln_stats_groups(c, banks=(next_pa(), next_pa()))
            tapn = 0
            for ix, (ct, hf, k0, nk) in enumerate(halves):
                acc_ap, acc_R = PR[ct % 2][:], PR_R[ct % 2]
                for kk in range(nk):
                    k = k0 + kk
                    OP("pe", "matmul", acc_ap, lhsT=dg[:, hf, kk, :], rhs=hT[:, ct, 2 + k:514 + k], start=(k == 0), stop=(k == 30),
                       r=[dg_R[hf], hT_R[ct]], w=[acc_R])
                    if pending_ln and k % 2 == 1:
                        pending_ln.pop(0)()
                    tapn += 1
                    if early and tapn % 16 == 0:
                        early.pop(0)()
                    if own_q and tapn % 4 == 2:
                        own_q.pop(0)()
                if ix + 2 < len(halves):
                    build_diag(ix + 2)
                if hf == 1:
                    if ct == 0:
                        for f in pending_ln:
                            f()
                        pending_ln = []
                    OP("act", "activation", out=yb[:, ct, :], in_=acc_ap, func=AF.Identity, bias=b_dw[:, ct:ct + 1],
                       r=[acc_R, c_vecs], w=[y_R[ct]])
                    if c == 3:
                        own_q.extend(last_per_ct[ct])
            if c == 3:
                for f in own_q:
                    f()
                pending_stats.append(last_fin)
            else:
                pending_stats.extend(ln_stats_groups(c))
            pending_ln = ln_norm_ops(c, col0)
        soft = []
        for e_ in ENGS:
            real = [o for o in S.ops[e_] if o.fn is not None]
            if real:
                soft.append(real[-1])
        dma(causal.rearrange("p a t -> p (a t)"), causal_d, w=[causal_R], deps=soft)
        for b3 in range(3):
            dma(KA[64:80, b3, :], onehot_d, w=[KAoh_R[b3]], deps=soft)
        build_ka(0, 0, deps=soft)
        tail = [f for grp in pending_stats for f in grp] + pending_ln
        del pending_stats[:]
        pending_ln = tail
        while pending_ln or early:
            if pending_ln:
                pending_ln.pop(0)()
            if early:
                early.pop(0)()

        S.barrier()
        cv = cvK
        biasp = cv.get([128, 4, 8, 80], BF16)
        km128 = cv.get([128, 2, 8, 16], BF16)
        km64 = km128[0:64]
        gate4 = cv.get([128, 512], F32)
        gate4_R = Region()
        PTb = cv.get([128, 3, 2, 512], BF16)
        un = cv.get([128, 2, 512], F32)
        gm = cv.get([128, 8, 16], F32)
        top = cv.get([128, 8, 8], F32)
        selm = cv.get([128, 8, 16], F32)
        gbias = cv.get([128, 16, 16], F32)
        ownind = cv.get([128, 16, 16], F32)
        validf = cv.get([128, 16, 16], F32)
        print("phase 2 arena bytes", cv.off)
        PTb_R = [Region() for _ in range(3)]
        un_R = [Region(), Region()]
        gm_R = Region()
        top_R = Region()
        selm_R = Region()
        biasp_R = Region()
        km64_R = Region()
        tab_R = Region()
        dma(gbias.rearrange("p a t -> p (a t)"), gb_d, w=[tab_R])
        dma(ownind.rearrange("p a t -> p (a t)"), own_d, w=[tab_R])
        dma(validf.rearrange("p a t -> p (a t)"), valid_d, w=[tab_R])
        OP("pool", "memset", QPc[:, 0].rearrange("p h t -> p (h t)"), 0.0, w=[QP_R[0]])
        OP("pool", "memset", biasp.rearrange("p a h t -> p (a h t)"), 0.0, w=[biasp_R])
        OP("dve", "memset", km128.rearrange("p a h n -> p (a h n)"), 0.0, w=[km64_R])
        km4 = km64.rearrange("p a (b two) n -> p a b two n", two=2)
        for a_, kmx in enumerate((kmhi, kmlo)):
            OP("dve", "tensor_copy", out=km4[:, a_, :, 0, :], in_=kmx[0:64, :, :], r=[km_R], w=[km64_R])
            OP("dve", "tensor_copy", out=km4[:, a_, :, 1, :], in_=kmx[64:128, :, :], r=[km_R], w=[km64_R])

        scB = dict(gate4=gate4, gm=gm, top=top, selm=selm, biasp=biasp, km128=km128, gbias=gbias, validf=validf, ownind=ownind,
                   gate4_R=gate4_R, gm_R=gm_R, top_R=top_R, selm_R=selm_R, biasp_R=biasp_R, km_R=km64_R, tab_R=tab_R)
        denr = cv.get([128, 2, 512], F32)
        rc4 = cv.get([128, 2, 4], F32)
        rchl = cv.get([128, 2, 2, 4], BF16)
        sel64b = cv.get([128, 128], BF16)
        wo_stg = cv.get([128, 1024], F32)
        wo_stg_R = Region()
        WO = Q[:].rearrange("p a t -> p (a t)").rearrange("p (k c) -> p k c", k=8)
        WO_R = [Region() for _ in range(8)]
        all_Q_R = [Q_R[p_][c_] for p_ in range(4) for c_ in range(4)]
        w_out_v = w_out.rearrange("(k p) c -> p k c", p=128)

        def wout_prefetch_pieces():
            res = []
            for k in range(8):
                def f(k=k):
                    dma(wo_stg, w_out_v[:, k, :], w=[wo_stg_R])
                    OP("dve", "tensor_copy", out=WO[:, k, :], in_=wo_stg, r=[wo_stg_R], w=[WO_R[k]] + all_Q_R)
                res.append(f)
            return res

        denr_R = [Region(), Region()]
        rc_R = [Region(), Region()]
        sel64b_R = Region()
        OP("dve", "tensor_copy", out=sel64b, in_=sel64[:], r=[c_sel64], w=[sel64b_R])

        work = []
        for c in range(4):
            tiles = list(range(4 * (c + 1))) + [16 + t for t in range(2 * N_OTHER_BLOCKS[c])]
            groups = [tiles[g:g + 2] for g in range(0, len(tiles), 2)]
            for h in range(NH):
                for gi, grp in enumerate(groups):
                    work.append((c, h, gi, grp, len(groups)))

        def emit_qk(wi):
            c, h, gi, grp, ngroups = work[wi]
            qb = (c + 1) % 2
            buf = (NH * c + h) % 3
            sb_i = wi % 2
            for s_, kt in enumerate(grp):
                diag = 4 * c <= kt < 4 * c + 4
                sc_ap, sc_R = PA[sb_i][:, s_, :], PA_R[sb_i][s_]
                OP("pe", "matmul", sc_ap, lhsT=KA[0:80, buf, kt * 128:(kt + 1) * 128], rhs=QPc[0:80, qb, h, :], start=True, stop=not diag,
                   r=[KA_R[buf], KAoh_R[buf], QP_R[qb]], w=[sc_R])
                if diag:
                    OP("pe", "matmul", sc_ap, lhsT=ident[:], rhs=causal[:, kt - 4 * c, :], start=False, stop=True,
                       r=[causal_R, c_ident], w=[sc_R])

        def emit_exp(wi):
            c, h, gi, grp, ngroups = work[wi]
            sb_i = wi % 2
            pb = wi % 3
            ng = len(grp)
            OP("act", "activation", out=PTb[:, pb, 0:ng, :], in_=PA[sb_i][:, 0:ng, :], func=AF.Exp, scale=0.125,
               r=PA_R[sb_i][0:ng], w=[PTb_R[pb]])

        def emit_pv(wi):
            c, h, gi, grp, ngroups = work[wi]
            pb = wi % 3
            ng = len(grp)
            ob = (c * NH + h) % 2
            o_ap, o_R = PR[ob][:], PR_R[ob]
            for s_, kt in enumerate(grp):
                first = (gi == 0 and s_ == 0)
                last = (gi == ngroups - 1 and s_ == ng - 1)
                OP("pe", "matmul", o_ap, lhsT=VF[:, kt * VT + h * 65: kt * VT + h * 65 + 128], rhs=PTb[:, pb, s_, :],
                   start=first, stop=last, r=[V_R[kt], vones_R, PTb_R[pb]], w=[o_R])

        def emit_norm1(c, h):
            ob = (c * NH + h) % 2
            o_ap, o_R = PR[ob][:], PR_R[ob]
            ub = h % 2
            OP("dve", "tensor_copy", out=denr[64:65, ub, :], in_=o_ap[64:65, :], r=[o_R], w=[denr_R[ub]])
            OP("dve", "tensor_copy", out=un[0:64, ub, :], in_=o_ap[0:64, :], r=[o_R], w=[un_R[ub]])

        def norm_tail_pieces(c, h):
            p, eo = h // 2, h % 2
            ub = h % 2
            col0 = 512 * c

            def den_mm(j):
                def f():
                    OP("pe", "matmul", PTF[:, j:j + 1], lhsT=denr[64:65, ub, 128 * j:128 * (j + 1)], rhs=onesf[64:65, 0:1],
                       start=True, stop=True, r=[denr_R[ub], c_onesf], w=[PT_R])
                return f

            def recip():
                OP("dve", "reciprocal", out=rc4[:, ub, :], in_=PTF[:, 0:4], r=[PT_R], w=[rc_R[ub]])
                OP("dve", "tensor_copy", out=rchl[:, ub, 0, :], in_=rc4[:, ub, :], r=[rc_R[ub]], w=[rc_R[ub]])
                OP("dve", "tensor_tensor", out=rchl[:, ub, 1, :], in0=rc4[:, ub, :], in1=rchl[:, ub, 0, :], op=ALU.subtract,
                   r=[rc_R[ub]], w=[rc_R[ub]])

            def bc_mm(j):
                def f():
                    for a_ in range(2):
                        OP("pe", "matmul", PH[:, 128 * j:128 * (j + 1)], lhsT=rchl[:, ub, a_, j:j + 1].to_broadcast([128, 128]), rhs=ident[:],
                           start=(a_ == 0), stop=(a_ == 1), r=[rc_R[ub], c_ident], w=[PH_R])
                return f

            def final():
                OP("dve", "tensor_tensor", out=MX[eo * 64:(eo + 1) * 64, p, col0:col0 + 512], in0=un[0:64, ub, :], in1=PH[0:64, :],
                   op=ALU.mult, r=[un_R[ub], PH_R], w=MX_R[4 * c:4 * c + 4])
            def den_all():
                for j in range(4):
                    den_mm(j)()
                recip()
            return [den_all, bc_mm(0), bc_mm(1), bc_mm(2), bc_mm(3), final]

        emit_qk(0)
        emit_qk(1)
        deferred = []
        for wi in range(len(work)):
            c, h, gi, grp, ngroups = work[wi]
            if gi == 0:
                nxt = NH * c + h + 1
                if nxt < 4 * NH:
                    build_ka(nxt // NH, nxt % NH)
            emit_exp(wi)
            if wi + 2 < len(work):
                emit_qk(wi + 2)
            emit_pv(wi)
            nd = []
            for (cnt, fn) in deferred:
                if cnt <= 1:
                    fn()
                else:
                    nd.append((cnt - 1, fn))
            deferred = nd
            if gi == ngroups - 1:
                emit_norm1(c, h)
                tp = norm_tail_pieces(c, h)
                if ngroups >= 8:
                    sched_ = [4, 8, 9, 10, 11, 11]
                else:
                    sched_ = [3, 5, 5, 6, 6, 6]
                for cnt_, fn_ in zip(sched_, tp):
                    deferred.append((cnt_, fn_))
                if h == 0 and c + 1 < 4:
                    if c >= 1:
                        offs = [1, 3, 5, 9, 13, 17] + [21 + 3 * hh for hh in range(NH)]
                    else:
                        offs = [1, 3, 5, 8, 11, 14] + [17 + hh for hh in range(NH)]
                    for off_, piece in zip(offs, prep_pieces(c + 1, scB)):
                        deferred.append((off_, piece))
                if c == 3 and h == 0:
                    for pi, piece in enumerate(wout_prefetch_pieces()):
                        deferred.append((2 + 12 * pi, piece))
        for (cnt, fn) in deferred:
            fn()

        if debug:
            dma(dbg["mx"][:, 0:8192], R1[:, 0:8192], r=MX_R)
            dma(dbg["mx"][:, 8192:16384], CVT[:].rearrange("p a t -> p (a t)"), r=CV_R)
        S.barrier()
        cv = Carver()
        W1B = cv.get([128, 2, 8, 512], BF16)
        W2B = cv.get([128, 2, 4, 1024], BF16)
        wst3 = cv.get([128, 2, 2048], F32)
        FF = cv.get([128, 2, 4, 512], BF16)
        rl = cv.get([128, 2, 512], F32)
        hn = cv.get([128, 3, 1024], BF16)
        junk3 = cv.get([128, 1024], BF16)
        ost = cv.get([128, 1024], F32)
        gfin = cv.get([128, 1024], F32)
        print("phase 3 arena bytes", cv.off)
        W1B_R = [[Region() for _ in range(8)] for _ in range(2)]
        W2B_R = [[Region() for _ in range(4)] for _ in range(2)]
        wst3_R = [Region(), Region()]
        FF_R = [[Region() for _ in range(4)] for _ in range(2)]
        rl_R = [Region(), Region()]
        hn_R = [Region(), Region(), Region()]
        ost_R = Region()
        gfin_R = Region()
        H1_R = [Region() for _ in range(16)]
        HN = MX
        HN_R = [Region() for _ in range(16)]
        dma(gfin, gfin_d, w=[gfin_R])

        cast_i = {"n": 0}

        def cast(out, in_, r, w, scale=None):
            eng = ("dve", "act")[cast_i["n"] % 2]
            cast_i["n"] += 1
            if eng == "dve":
                if scale is None:
                    OP("dve", "tensor_copy", out=out, in_=in_, r=r, w=w)
                else:
                    OP("dve", "tensor_scalar", out=out, in0=in_, scalar1=scale, scalar2=None, op0=ALU.mult, r=r, w=w)
            else:
                if scale is None:
                    OP("act", "activation", out=out, in_=in_, func=AF.Copy, r=r, w=w)
                else:
                    OP("act", "activation", out=out, in_=in_, func=AF.Copy, scale=scale, r=r, w=w)

        stage_i = {"n": 0}

        def stage_dma(src_ap):
            b = stage_i["n"] % 2
            stage_i["n"] += 1
            dma(wst3[:, b, :], src_ap, w=[wst3_R[b]])
            return b

        w_out_v = w_out.rearrange("(k p) c -> p k c", p=128)
        w1_v = w_m1.rearrange("(k p) c -> p k c", p=128)
        w2_v = w_m2.rearrange("(f p) c -> p f c", p=128)

        def wout_pieces():
            res = []
            for k2 in range(4):
                def d(k2=k2):
                    return stage_dma(w_out_v[:, 2 * k2:2 * k2 + 2, :])

                def cfn(b, k2=k2):
                    for kk in range(2):
                        cast(WO[:, 2 * k2 + kk, :], wst3[:, b, kk * 1024:(kk + 1) * 1024], [wst3_R[b]], [WO_R[2 * k2 + kk]])
                res.append((d, cfn))
            return res

        def ffblock_pieces(fb):
            wb = fb % 2
            res = []
            for half in range(2):
                def d(half=half):
                    return stage_dma(w1_v[:, 4 * half:4 * half + 4, fb * 512:(fb + 1) * 512])

                def cfn(b, half=half):
                    for kk in range(4):
                        k = 4 * half + kk
                        cast(W1B[:, wb, k, :], wst3[:, b, kk * 512:(kk + 1) * 512], [wst3_R[b], c_vecs], [W1B_R[wb][k]], scale=g_mlp[:, k:k + 1])
                res.append((d, cfn))
            for half in range(2):
                def d(half=half):
                    return stage_dma(w2_v[:, 4 * fb + 2 * half: 4 * fb + 2 * half + 2, :])

                def cfn(b, half=half):
                    for kk in range(2):
                        f = 2 * half + kk
                        cast(W2B[:, wb, f, :], wst3[:, b, kk * 1024:(kk + 1) * 1024], [wst3_R[b]], [W2B_R[wb][f]])
                res.append((d, cfn))
            return res

        class Loader:
            def __init__(self):
                self.queue = []
                self.inflight = []

            def add(self, pieces):
                self.queue.extend(pieces)
                self.pump()

            def pump(self):
                while self.queue and len(self.inflight) < 2:
                    d, cfn = self.queue.pop(0)
                    self.inflight.append((d(), cfn))

            def tick(self, n=1):
                for _ in range(n):
                    if not self.inflight:
                        return
                    b, cfn = self.inflight.pop(0)
                    cfn(b)
                    self.pump()

            def drain(self):
                while self.inflight:
                    self.tick()

        def h1_view(i):
            return H1[:, i, :].rearrange("p (a d) -> p a d", a=2)

        def rms_stats(i, sl):
            ssq = st4[:, 0, sl:sl + 1]
            vv = st4[:, 1, sl:sl + 1]
            rs = st4[:, 2, sl:sl + 1]
            OP("act", "activation", out=junk3, in_=H1[:, i, :], func=AF.Square, accum_out=ssq, r=[H1_R[i]], w=[st_R[sl], junk_R])
            OP("pool", "tensor_scalar", out=vv, in0=ssq, scalar1=1.0 / D, scalar2=EPS, op0=ALU.mult, op1=ALU.add, r=[st_R[sl]], w=[st_R[sl]])
            OP("pool", "tensor_tensor", out=rs, in0=vv, in1=mhalf1[:], op=ALU.pow, r=[st_R[sl], c_mhalf1], w=[st_R[sl]])
            return rs

        LD = Loader()
        for i in range(16):
            dma(H1[:, i, :], x_tiles[i], w=[H1_R[i]])
        LD.add(ffblock_pieces(0))

        pa_i = 0

        def outproj_mm(i):
            pa = i % 2
            for half in range(2):
                for k in range(8):
                    src = MX[:, k, 128 * i:128 * (i + 1)] if k < 4 else CVT[:, k - 4, 128 * i:128 * (i + 1)]
                    OP("pe", "matmul", PA[pa][:, half, :], lhsT=src, rhs=WO[:, k, half * 512:(half + 1) * 512], start=(k == 0), stop=(k == 7),
                       r=[MX_R[i], CV_R[i], WO_R[k]], w=[PA_R[pa][half]])
            OP("dve", "tensor_tensor", out=h1_view(i), in0=PA[pa][:], in1=h1_view(i), op=ALU.add, r=PA_R[pa] + [H1_R[i]], w=[H1_R[i]])
            sl = i % 4
            rms_stats(i, sl)

        def outproj_scale(i):
            sl = i % 4
            hb = i % 3
            rs = st4[:, 2, sl:sl + 1]
            OP("dve", "tensor_scalar", out=hn[:, hb, :], in0=H1[:, i, :], scalar1=rs, scalar2=None, op0=ALU.mult,
               r=[H1_R[i], st_R[sl]], w=[hn_R[hb]])

        def outproj_tr(i):
            hb = i % 3
            for k in range(8):
                OP("pe", "transpose", out=PT[:, k * 128:(k + 1) * 128], in_=hn[:, hb, k * 128:(k + 1) * 128], identity=ident[:],
                   r=[hn_R[hb], c_ident], w=[PT_R])
            OP("act", "activation", out=HN[:, :, 128 * i:128 * (i + 1)], in_=PT3, func=AF.Copy, r=[PT_R], w=[HN_R[i], MX_R[i]])

        outproj_mm(0)
        outproj_mm(1)
        outproj_scale(0)
        for i in range(16):
            if i + 2 < 16:
                outproj_mm(i + 2)
            if i + 1 < 16:
                outproj_scale(i + 1)
            outproj_tr(i)
            if i % 4 == 3:
                LD.tick()
        LD.drain()
        if debug:
            dma(dbg["h1"], H1.rearrange("p i d -> p (i d)"), r=H1_R)

        items = [(fb, tc) for fb in range(8) for tc in range(4)]
        out_ops = []

        def mlp_in(ii):
            fb, tc = items[ii]
            wb = fb % 2
            fbuf = ii % 2
            for ft in range(4):
                fpr = ft % 2
                for k in range(8):
                    OP("pe", "matmul", PR[fpr][:], lhsT=W1B[:, wb, k, ft * 128:(ft + 1) * 128], rhs=HN[:, k, 512 * tc:512 * (tc + 1)],
                       start=(k == 0), stop=(k == 7), r=[W1B_R[wb][k]] + HN_R[4 * tc:4 * tc + 4], w=[PR_R[fpr]])
                rb = ft % 2
                OP("act", "activation", out=rl[:, rb, :], in_=PR[fpr][:], func=AF.Relu, r=[PR_R[fpr]], w=[rl_R[rb]])
                OP("dve", "tensor_tensor", out=FF[:, fbuf, ft, :], in0=rl[:, rb, :], in1=rl[:, rb, :], op=ALU.mult,
                   r=[rl_R[rb]], w=[FF_R[fbuf][ft]])

        def mlp_out(ii):
            fb, tc = items[ii]
            wb = fb % 2
            fbuf = ii % 2
            for ti in range(4):
                i = 4 * tc + ti
                pa = ti % 2
                for half in range(2):
                    for ft in range(4):
                        OP("pe", "matmul", PA[pa][:, half, :], lhsT=FF[:, fbuf, ft, 128 * ti:128 * (ti + 1)],
                           rhs=W2B[:, wb, ft, half * 512:(half + 1) * 512], start=(ft == 0), stop=(ft == 3),
                           r=[FF_R[fbuf][ft], W2B_R[wb][ft]], w=[PA_R[pa][half]])
                OP("dve", "tensor_tensor", out=h1_view(i), in0=PA[pa][:], in1=h1_view(i), op=ALU.add, r=PA_R[pa] + [H1_R[i]], w=[H1_R[i]])
                if fb == 7:
                    sl = i % 4
                    rs = rms_stats(i, sl)

                    def fin(i=i, sl=sl, rs=rs):
                        OP("dve", "scalar_tensor_tensor", out=H1[:, i, :], in0=H1[:, i, :], scalar=rs, in1=gfin, op0=ALU.mult, op1=ALU.mult,
                           r=[H1_R[i], st_R[sl], gfin_R], w=[H1_R[i]])
                        out_ops.append(dma(y_out[128 * i:128 * (i + 1), :], H1[:, i, :], r=[H1_R[i]]))
                    finals.append(fin)
                    if len(finals) > 2:
                        finals.pop(0)()

        finals = []
        mlp_in(0)
        for ii in range(len(items)):
            fb, tc = items[ii]
            if tc == 0 and fb + 1 < 8:
                LD.add(ffblock_pieces(fb + 1))
            if ii + 1 < len(items):
                if items[ii + 1][1] == 0:
                    LD.drain()
                mlp_in(ii + 1)
            mlp_out(ii)
            LD.tick()
        for f in finals:
            f()
        S.add("sp", None, deps=out_ops + S.dma_since_barrier)
        S.emit(nc)
    return nc


def _core_tables(role):
    own = OWN_BLOCKS[role]
    oth = OWN_BLOCKS[1 - role]
    nat = own + oth
    pos = np.concatenate([np.arange(b * BL, (b + 1) * BL) for b in nat]).astype(np.float32)
    inv_freq = (np.float32(500000.0) ** (-np.arange(8, dtype=np.float32) * np.float32(2.0) / np.float32(16))).astype(np.float32)
    ang = (pos[:, None] * inv_freq[None, :]).astype(np.float32)
    cos = np.cos(ang).astype(np.float32)
    sin = np.sin(ang).astype(np.float32)
    cosT = np.ones((128, SEQ), np.float32)
    sinT = np.zeros((128, SEQ), np.float32)
    for p in range(128):
        d = p % 64
        if d < 8:
            cosT[p] = cos[:, d]
            sinT[p] = -sin[:, d]
        elif d < 16:
            cosT[p] = cos[:, d - 8]
            sinT[p] = sin[:, d - 8]
    gb = np.zeros((16, 16), np.float32)
    ownind = np.zeros((16, 16), np.float32)
    valid = np.zeros((16, 16), np.float32)
    for qt in range(16):
        j = qt // 2
        for n in range(16):
            if nat[n] < own[j]:
                valid[qt, n] = 1.0
            else:
                gb[qt, n] = -1e9
            if n == j:
                ownind[qt, n] = 1.0
    rep = lambda a: np.ascontiguousarray(np.broadcast_to(a.reshape(1, -1), (128, a.size))).astype(np.float32)
    return dict(nat=nat, own=own, cosT=cosT, sinT=sinT, gbias=rep(gb), ownind=rep(ownind), validf=rep(valid))


def _const_tables():
    ident = np.eye(128, dtype=np.float32).astype(ml_dtypes.bfloat16)
    perm = np.zeros((128, 128), np.float32)
    for m in range(128):
        d = m % 64
        if d < 8:
            perm[m + 8, m] = 1.0
        elif d < 16:
            perm[m - 8, m] = 1.0
    causal = np.zeros((128, 4, 512), np.float32)
    for kk in range(4):
        kp = 128 * kk + np.arange(128)[:, None]
        qi = np.arange(512)[None, :]
        causal[:, kk, :] = np.where(kp <= qi, 0.0, NEG)
    sel64 = np.zeros((128, 128), np.float32)
    sel64[64, :] = 1.0
    onehot = np.zeros((16, SEQ), np.float32)
    for n in range(16):
        onehot[n, n * BL:(n + 1) * BL] = 1.0
    return dict(ident=ident, perm=perm.astype(ml_dtypes.bfloat16),
                causal=causal.reshape(128, 2048).astype(ml_dtypes.bfloat16), sel64=sel64,
                onehot=onehot.astype(ml_dtypes.bfloat16))


_NC_CACHE = {}


def kernel(x, g_mix_norm, w_in, b_glu, w_dw, b_dw, g_conv_ln, b_conv_ln, w_out, g_mlp_norm, w_mlp_in, w_mlp_out, g_final,
           _debug=False):
    f = lambda a: np.ascontiguousarray(np.asarray(a, dtype=np.float32))
    x = f(x)
    w_in0, w_out0, w_m1, w_m2 = f(w_in)[0], f(w_out)[0], f(w_mlp_in)[0], f(w_mlp_out)[0]
    vecs = np.zeros((128, NVEC), np.float32)
    vecs[:, 0:8] = f(g_mix_norm)[0].reshape(8, 128).T
    vecs[:, 8:16] = f(g_mlp_norm)[0].reshape(8, 128).T
    bg = f(b_glu)[0]
    vecs[:, 16:20] = bg[0:512].reshape(4, 128).T
    vecs[:, 20:24] = bg[512:1024].reshape(4, 128).T
    vecs[:, 24:28] = f(b_dw)[0].reshape(4, 128).T
    vecs[:, 28:32] = f(g_conv_ln)[0].reshape(4, 128).T
    vecs[:, 32:36] = f(b_conv_ln)[0].reshape(4, 128).T
    wd = f(w_dw)[0, :, 0, :]
    vecs[:, 40:164] = wd.reshape(31, 4, 128).transpose(2, 1, 0).reshape(128, 124)
    gfin = np.ascontiguousarray(np.broadcast_to(f(g_final).reshape(1, D), (128, D)))
    consts = _const_tables()
    tabs = [_core_tables(0), _core_tables(1)]

    in_maps = []
    for core in range(8):
        b, role = core // 2, core % 2
        T = tabs[role]
        xb = x[b]
        x_loc = np.concatenate([xb[n * BL:(n + 1) * BL] for n in T["nat"]], axis=0)
        x_halo = np.zeros((4, 32, D), np.float32)
        v = vecs.copy()
        for c in range(4):
            start = T["own"][2 * c] * BL
            if start > 0:
                x_halo[c] = xb[start - 32:start]
                v[:, 36 + c] = 1.0
        in_maps.append(dict(x_loc=np.ascontiguousarray(x_loc), x_halo=x_halo, w_in=w_in0, w_out=w_out0, w_m1=w_m1, w_m2=w_m2,
                            cosT=T["cosT"], sinT=T["sinT"], vecs=v, gfin=gfin, gbias=T["gbias"], ownind=T["ownind"],
                            validf=T["validf"], **consts))

    key = bool(_debug)
    if key not in _NC_CACHE:
        _NC_CACHE[key] = build_program(debug=_debug)
    nc = _NC_CACHE[key]
    res = run_bass_kernel_spmd(nc, in_maps, core_ids=list(range(8)))
    out = np.zeros((4, SEQ, D), np.float32)
    for core in range(8):
        b, role = core // 2, core % 2
        y = res.results[core]["y"]
        for j, n in enumerate(tabs[role]["own"]):
            out[b, n * BL:(n + 1) * BL] = y[j * BL:(j + 1) * BL]
    if _debug:
        return out, res.results, tabs
    return out
```
